# Optimizing a Trainium2 kernel written in Bass

```python
import jax, jax.numpy as jnp
from jax import lax
import numpy as np

D_MODEL = 1024
BATCH = 8
SEQ = 8192
DEPTH = 2

GRID_W = 64
ROPE_THETA = 10000.0
Q_BLOCK = 128
EPS = 1e-6
N_MIXERS = 2

N_MEM = 256
MEM_HEADS = 4
MEM_HEAD_DIM = 128
MEM_W = MEM_HEADS * MEM_HEAD_DIM

MLA_HEADS = 8
MLA_Q_RANK = 384
MLA_KV_RANK = 256
MLA_NOPE = 128
MLA_ROPE = 64
MLA_V = 128
MLA_IN_W = MLA_Q_RANK + MLA_KV_RANK + MLA_ROPE

GQA_HEADS = 8
GQA_KV_HEADS = 2
GQA_HEAD_DIM = 128
GQA_Q_W = GQA_HEADS * GQA_HEAD_DIM
GQA_KV_W = GQA_KV_HEADS * GQA_HEAD_DIM
GQA_IN_W = GQA_Q_W + 2 * GQA_KV_W

MIX_W = MLA_HEADS * MLA_V + MEM_W

D_FF = -(-8 * D_MODEL // (3 * 256)) * 256

kernel_name = "hybrid_mla_gqa_memory_encoder"


def _rmsnorm(x, g):
    x32 = x.astype(jnp.float32)
    y = x32 * lax.rsqrt(jnp.mean(x32 * x32, axis=-1, keepdims=True) + EPS)
    return (y * g.astype(jnp.float32)).astype(x.dtype)


def _axial_rope_tables(seq, dim, dtype):
    rows = seq // GRID_W
    row = jnp.repeat(jnp.arange(rows, dtype=jnp.float32), GRID_W)
    col = jnp.tile(jnp.arange(GRID_W, dtype=jnp.float32), rows)
    axis_dim = dim // 2
    inv = ROPE_THETA ** (-jnp.arange(0, axis_dim, 2, dtype=jnp.float32) / axis_dim)
    ang_r = row[:, None] * inv
    ang_c = col[:, None] * inv
    tabs = (jnp.cos(ang_r), jnp.sin(ang_r), jnp.cos(ang_c), jnp.sin(ang_c))
    return tuple(t[:, None, :].astype(dtype) for t in tabs)


def _rope_1d(x, cos, sin):
    x1, x2 = jnp.split(x, 2, axis=-1)
    return jnp.concatenate([x1 * cos - x2 * sin, x2 * cos + x1 * sin], axis=-1)


def _apply_axial_rope(x, tabs):
    cos_r, sin_r, cos_c, sin_c = tabs
    xr, xc = jnp.split(x, 2, axis=-1)
    return jnp.concatenate([_rope_1d(xr, cos_r, sin_r), _rope_1d(xc, cos_c, sin_c)], axis=-1)


def _blocked_attention(q_parts, k_parts, v, scale):
    B, S, Hk, Dv = v.shape
    H = q_parts[0].shape[2]
    G = H // Hk
    nb = S // Q_BLOCK
    qb = tuple(q.reshape(B, nb, Q_BLOCK, Hk, G, q.shape[-1]).transpose(1, 0, 2, 3, 4, 5)
               for q in q_parts)

    def one_block(qs):
        s = None
        for qp, kp in zip(qs, k_parts):
            eq = 'bqkgd,bskd->bkgqs' if kp.ndim == 4 else 'bqkgd,bsd->bkgqs'
            t = jnp.einsum(eq, qp, kp, preferred_element_type=jnp.float32)
            s = t if s is None else s + t
        p = jax.nn.softmax(s * scale, axis=-1).astype(v.dtype)
        return jnp.einsum('bkgqs,bskd->bqkgd', p, v)

    out = lax.map(one_block, qb)
    return out.transpose(1, 0, 2, 3, 4, 5).reshape(B, S, H, Dv)


def _mla_mixer(p, q_a_norm, w_q_b, kv_a_norm, w_kv_b, q_norm, k_norm, rope):
    B, S, _ = p.shape
    c_q = p[..., :MLA_Q_RANK]
    c_kv = p[..., MLA_Q_RANK:MLA_Q_RANK + MLA_KV_RANK]
    k_pe = p[..., MLA_Q_RANK + MLA_KV_RANK:]
    q = (_rmsnorm(c_q, q_a_norm) @ w_q_b).reshape(B, S, MLA_HEADS, MLA_NOPE + MLA_ROPE)
    kv = (_rmsnorm(c_kv, kv_a_norm) @ w_kv_b).reshape(B, S, MLA_HEADS, MLA_NOPE + MLA_V)
    q_nope = _rmsnorm(q[..., :MLA_NOPE], q_norm[:MLA_NOPE])
    q_pe = _apply_axial_rope(_rmsnorm(q[..., MLA_NOPE:], q_norm[MLA_NOPE:]), rope)
    k_nope = _rmsnorm(kv[..., :MLA_NOPE], k_norm[:MLA_NOPE])
    v = kv[..., MLA_NOPE:]
    k_pe = _apply_axial_rope(_rmsnorm(k_pe, k_norm[MLA_NOPE:])[:, :, None, :], rope)[:, :, 0, :]
    out = _blocked_attention([q_nope, q_pe], [k_nope, k_pe], v,
                             (MLA_NOPE + MLA_ROPE) ** -0.5)
    return out.reshape(B, S, MLA_HEADS * MLA_V)


def _gqa_mixer(p, q_norm, k_norm, rope):
    B, S, _ = p.shape
    q = p[..., :GQA_Q_W].reshape(B, S, GQA_HEADS, GQA_HEAD_DIM)
    k = p[..., GQA_Q_W:GQA_Q_W + GQA_KV_W].reshape(B, S, GQA_KV_HEADS, GQA_HEAD_DIM)
    v = p[..., GQA_Q_W + GQA_KV_W:].reshape(B, S, GQA_KV_HEADS, GQA_HEAD_DIM)
    q = _apply_axial_rope(_rmsnorm(q, q_norm), rope)
    k = _apply_axial_rope(_rmsnorm(k, k_norm), rope)
    out = _blocked_attention([q], [k], v, GQA_HEAD_DIM ** -0.5)
    return out.reshape(B, S, GQA_Q_W)


def _memory_attention(q, mem_n, w_kv, q_norm, k_norm):
    B, S, _ = q.shape
    M = mem_n.shape[1]
    q = _rmsnorm(q.reshape(B, S, MEM_HEADS, MEM_HEAD_DIM), q_norm)
    kv = (mem_n @ w_kv).reshape(B, M, 2, MEM_HEADS, MEM_HEAD_DIM)
    k = _rmsnorm(kv[:, :, 0], k_norm)
    v = kv[:, :, 1]
    s = jnp.einsum('bqhd,bmhd->bhqm', q, k, preferred_element_type=jnp.float32)
    p = jax.nn.softmax(s * (MEM_HEAD_DIM ** -0.5), axis=-1).astype(v.dtype)
    return jnp.einsum('bhqm,bmhd->bqhd', p, v).reshape(B, S, MEM_W)


def _swiglu(h, w_gate_up, w_down):
    gu = h @ w_gate_up
    g, u = gu[..., :D_FF], gu[..., D_FF:]
    return (jax.nn.silu(g) * u) @ w_down


def setup_inputs(seed: int = 0) -> dict:
    key = jax.random.key(seed)
    ks = iter(jax.random.split(key, 32))
    n_mla = (DEPTH + 1) // 2
    n_gqa = DEPTH // 2

    def w(shape, fan_in):
        return jax.random.normal(next(ks), shape, jnp.float32) * (fan_in ** -0.5)

    def gain(shape):
        return 1.0 + 0.05 * jax.random.normal(next(ks), shape, jnp.float32)

    return {
        "x": jax.random.normal(next(ks), (BATCH, SEQ, D_MODEL), jnp.float32),
        "mem": jax.random.normal(next(ks), (BATCH, N_MEM, D_MODEL), jnp.float32),
        "mem_norm": gain((D_MODEL,)),
        "norm_mix": gain((DEPTH, D_MODEL)),
        "norm_ffn": gain((DEPTH, D_MODEL)),
        "w_out": w((DEPTH, MIX_W, D_MODEL), MIX_W),
        "w_mem_kv": w((DEPTH, D_MODEL, 2 * MEM_W), D_MODEL),
        "memq_norm": gain((DEPTH, MEM_HEAD_DIM)),
        "memk_norm": gain((DEPTH, MEM_HEAD_DIM)),
        "w_gate_up": w((DEPTH, D_MODEL, 2 * D_FF), D_MODEL),
        "w_down": w((DEPTH, D_FF, D_MODEL), D_FF),
        "mla_w_in": w((n_mla, D_MODEL, MLA_IN_W + MEM_W), D_MODEL),
        "mla_q_a_norm": gain((n_mla, MLA_Q_RANK)),
        "mla_w_q_b": w((n_mla, MLA_Q_RANK, MLA_HEADS * (MLA_NOPE + MLA_ROPE)), MLA_Q_RANK),
        "mla_kv_a_norm": gain((n_mla, MLA_KV_RANK)),
        "mla_w_kv_b": w((n_mla, MLA_KV_RANK, MLA_HEADS * (MLA_NOPE + MLA_V)), MLA_KV_RANK),
        "mla_q_norm": gain((n_mla, MLA_NOPE + MLA_ROPE)),
        "mla_k_norm": gain((n_mla, MLA_NOPE + MLA_ROPE)),
        "gqa_w_in": w((n_gqa, D_MODEL, GQA_IN_W + MEM_W), D_MODEL),
        "gqa_q_norm": gain((n_gqa, GQA_HEAD_DIM)),
        "gqa_k_norm": gain((n_gqa, GQA_HEAD_DIM)),
    }


def reference(x, mem, mem_norm, norm_mix, norm_ffn, w_out, w_mem_kv, memq_norm, memk_norm,
              w_gate_up, w_down, mla_w_in, mla_q_a_norm, mla_w_q_b, mla_kv_a_norm, mla_w_kv_b,
              mla_q_norm, mla_k_norm, gqa_w_in, gqa_q_norm, gqa_k_norm):
    S = x.shape[1]
    rope_mla = _axial_rope_tables(S, MLA_ROPE, x.dtype)
    rope_gqa = _axial_rope_tables(S, GQA_HEAD_DIM, x.dtype)
    mem_n = _rmsnorm(mem, mem_norm)
    for i in range(DEPTH):
        j = i // N_MIXERS
        h = _rmsnorm(x, norm_mix[i])
        if i % N_MIXERS == 0:
            proj = h @ mla_w_in[j]
            mix = _mla_mixer(proj[..., :MLA_IN_W], mla_q_a_norm[j], mla_w_q_b[j],
                             mla_kv_a_norm[j], mla_w_kv_b[j], mla_q_norm[j], mla_k_norm[j],
                             rope_mla)
            q_mem = proj[..., MLA_IN_W:]
        else:
            proj = h @ gqa_w_in[j]
            mix = _gqa_mixer(proj[..., :GQA_IN_W], gqa_q_norm[j], gqa_k_norm[j], rope_gqa)
            q_mem = proj[..., GQA_IN_W:]
        mem_out = _memory_attention(q_mem, mem_n, w_mem_kv[i], memq_norm[i], memk_norm[i])
        x = x + jnp.concatenate([mix, mem_out], axis=-1) @ w_out[i]
        x = x + _swiglu(_rmsnorm(x, norm_ffn[i]), w_gate_up[i], w_down[i])
    return x
```

```python
import numpy as np
from contextlib import ExitStack
import concourse.bass as bass
import concourse.mybir as mybir
from concourse.bass_utils import run_bass_kernel_spmd

F32 = mybir.dt.float32
BF16 = mybir.dt.bfloat16
AF = mybir.ActivationFunctionType
ALU = mybir.AluOpType
AX = mybir.AxisListType

D = 1024
DFF = 2816
EPS = 1e-6
ARENA_WORDS = 50000


class Tok:
    __slots__ = ("key", "val", "needed")

    def __init__(self, key, val=None):
        self.key = key
        self.val = val
        self.needed = False


class Buf:
    __slots__ = ("name", "writers", "readers", "ignore")

    def __init__(self, name, ignore=False):
        self.name = name
        self.writers = {}
        self.readers = {}
        self.ignore = ignore


class Sched:
    ENGS = ("pe", "act", "dve", "pool", "sp")

    def __init__(self, nc, stack):
        self.nc = nc
        self.stack = stack
        self.ops = {e: [] for e in self.ENGS}
        self.sems = {}
        self.cnt = {}
        self.last = {}
        for e in self.ENGS:
            self._mksem("c_" + e)

    def _mksem(self, key):
        if key not in self.sems:
            self.sems[key] = self.stack.enter_context(self.nc.semaphore(key))
            self.cnt[key] = 0
        return self.sems[key]

    def _deps(self, eng, r, w, deps):
        ds = {}

        def add(t):
            if t is None:
                return
            if eng == "pe" and t.key == "c_pe":
                return
            ds[id(t)] = t
        for b in r:
            if b.ignore:
                continue
            for t in b.writers.values():
                add(t)
        for b in w:
            if b.ignore:
                continue
            for t in b.writers.values():
                add(t)
            for t in b.readers.values():
                add(t)
        for t in deps:
            add(t)
        out = list(ds.values())
        for t in out:
            t.needed = True
        return out

    def op(self, eng, fn, r=(), w=(), deps=()):
        d = self._deps(eng, r, w, deps)
        key = "c_" + eng
        tok = Tok(key)
        for b in r:
            if not b.ignore:
                b.readers[key] = tok
        for b in w:
            if not b.ignore:
                b.writers[key] = tok
        self.ops[eng].append(("op", fn, d, tok))
        self.last[key] = tok
        return tok

    def dma(self, q, out, in_, r=(), w=(), key=None, deps=()):
        d = self._deps(q, r, w, deps)
        self._mksem(key)
        self.cnt[key] += 16
        tok = Tok(key, self.cnt[key])
        tok.needed = True
        for b in r:
            if not b.ignore:
                b.readers[key] = tok
        for b in w:
            if not b.ignore:
                b.writers[key] = tok
        self.ops[q].append(("dma", (out, in_), d, tok))
        self.last[key] = tok
        return tok

    def barrier(self):
        toks = list(self.last.values())
        for t in toks:
            t.needed = True
        for e in self.ENGS:
            self.ops[e].append(("wait", None, toks, None))

    def emit_all(self):
        for e in self.ENGS:
            key = "c_" + e
            c = 0
            for kind, fn, d, tok in self.ops[e]:
                if kind == "op" and tok.needed:
                    c += 1
                    tok.val = c
        sems = self.sems

        def run(e, eng):
            waited = {}
            for kind, fn, d, tok in self.ops[eng]:
                for t in d:
                    if t.key == "c_" + eng and eng == "pe":
                        continue
                    if waited.get(t.key, 0) >= t.val:
                        continue
                    waited[t.key] = t.val
                    e.wait_ge(sems[t.key], t.val)
                if kind == "op":
                    ins = fn(e)
                    if tok.needed:
                        ins.then_inc(sems[tok.key], 1)
                elif kind == "dma":
                    e.dma_start(out=fn[0], in_=fn[1]).then_inc(sems[tok.key], 16)
        with self.nc.Block() as block:
            @block.tensor
            def _(e):
                run(e, "pe")

            @block.scalar
            def _(e):
                run(e, "act")

            @block.vector
            def _(e):
                run(e, "dve")

            @block.gpsimd
            def _(e):
                run(e, "pool")

            @block.sync
            def _(e):
                run(e, "sp")


class Arena:
    def __init__(self, ap, words):
        self.ap = ap
        self.words = words
        self.off = 0

    def f32(self, n):
        a = self.ap[:, self.off:self.off + n]
        self.off += n
        assert self.off <= self.words, ("arena overflow", self.off)
        return a

    def bf16(self, n):
        assert n % 2 == 0
        return self.f32(n // 2).bitcast(BF16)

    def mark(self):
        return self.off

    def reset(self, m):
        self.off = m


def interleave(gens, width, on_done=None):
    gens = list(gens)
    active = []
    nxt = 0
    while active or nxt < len(gens):
        while len(active) < width and nxt < len(gens):
            active.append((nxt, gens[nxt]))
            nxt += 1
        still = []
        for idx, g in active:
            try:
                next(g)
                still.append((idx, g))
            except StopIteration:
                if on_done is not None:
                    on_done(idx)
        active = still


def bcast_row(ap_row, n):
    return bass.AP(ap_row.tensor, ap_row.offset, [[0, 128], [1, n]])


def build(S, debug=False, stop_after=99):
    NT = S // 128
    NB = S // 512
    nc = bass.Bass("TRN2", target_bir_lowering=False)

    def din(name, shape):
        return nc.dram_tensor(name, list(shape), F32, kind="ExternalInput").ap()

    def dscr(name, shape, dt=BF16):
        kind = "ExternalOutput" if (debug and name != "wgub") else "Internal"
        return nc.dram_tensor(name, list(shape), dt, kind=kind).ap()

    x = din("x", [S, D])
    mem = din("mem", [256, D])
    mem_norm = din("mem_norm", [D])
    norm_mix = din("norm_mix", [2, D])
    norm_ffn = din("norm_ffn", [2, D])
    w_out = din("w_out", [2, 1536, D])
    w_mem_kv = din("w_mem_kv", [2, D, 1024])
    memq_norm = din("memq_norm", [2, 128])
    memk_norm = din("memk_norm", [2, 128])
    w_gate_up = din("w_gate_up", [2, D, 2 * DFF])
    w_down = din("w_down", [2, DFF, D])
    mla_w_in = din("mla_w_in", [1, D, 1216])
    mla_q_a_norm = din("mla_q_a_norm", [1, 384])
    mla_w_q_b = din("mla_w_q_b", [1, 384, 1536])
    mla_kv_a_norm = din("mla_kv_a_norm", [1, 256])
    mla_w_kv_b = din("mla_w_kv_b", [1, 256, 2048])
    mla_q_norm = din("mla_q_norm", [1, 192])
    mla_k_norm = din("mla_k_norm", [1, 192])
    gqa_w_in = din("gqa_w_in", [1, D, 2048])
    gqa_q_norm = din("gqa_q_norm", [1, 128])
    gqa_k_norm = din("gqa_k_norm", [1, 128])
    cosg = din("cosg", [S, 128])
    sing = din("sing", [S, 128])
    cosm = din("cosm", [S, 64])
    sinm = din("sinm", [S, 64])
    out = nc.dram_tensor("out", [S, D], F32, kind="ExternalOutput").ap()

    wgub = dscr("wgub", [2, 22, 128, 2, 8, 128])
    QnT = dscr("QnT", [8, 128, S])
    QpT = dscr("QpT", [4, 128, S])
    KnT = dscr("KnT", [8, 128, S])
    KpT = dscr("KpT", [128, S])
    QmT = dscr("QmT", [4, 128, S])
    Vd = dscr("Vd", [8, 128, NT, 128])
    mixT = dscr("mixT", [12, 128, S])
    x1d = dscr("x1d", [S, D], F32)
    x2d = dscr("x2d", [S, D], F32)
    h2T = dscr("h2T", [8, 128, S])
    B_QnT, B_QpT, B_KnT, B_KpT, B_QmT, B_Vd = (Buf(n, True) for n in ("QnT", "QpT", "KnT", "KpT", "QmT", "Vd"))
    B_mixT, B_x1d, B_x2d, B_h2T = (Buf(n, True) for n in ("mixT", "x1d", "x2d", "h2T"))
    B_wgub = [Buf("wgub0", True), Buf("wgub1", True)]

    with ExitStack() as st:
        s = Sched(nc, st)
        arena_t = st.enter_context(nc.sbuf_tensor("arena", [128, ARENA_WORDS], F32))
        ps = st.enter_context(nc.psum_tensor("ps", [128, 4096], F32))
        A = Arena(arena_t, ARENA_WORDS)
        PB = [ps[:, i * 512:(i + 1) * 512] for i in range(8)]
        PBT = [p.bitcast(BF16) for p in PB]
        B_PB = [Buf("pb%d" % i) for i in range(8)]

        identf = A.f32(128)
        ident = A.bf16(128)
        ones = A.bf16(128)
        eps_t = A.f32(1)
        KmT = [A.bf16(4 * 256) for _ in range(2)]
        Vm = [A.bf16(2 * 512) for _ in range(2)]
        B_const = Buf("const")
        B_KmT = [Buf("KmT0"), Buf("KmT1")]
        B_Vm = [Buf("Vm0"), Buf("Vm1")]
        s.op("pool", lambda e: e.memset(identf, 0.0), w=[B_const])
        s.op("pool", lambda e: e.affine_select(out=identf, in_=identf, pattern=[[-1, 128]],
                                               compare_op=ALU.not_equal, fill=1.0, base=0,
                                               channel_multiplier=1), w=[B_const])
        s.op("pool", lambda e: e.tensor_copy(ident, identf), w=[B_const])
        s.op("pool", lambda e: e.memset(ones, 1.0), w=[B_const])
        s.op("pool", lambda e: e.memset(eps_t, EPS), w=[B_const])
        PERS = A.mark()

        def rstd_ops(ss, n_cols, inv_n, sd, rs, B_ss, B_sd, B_rs):
            s.op("act", lambda e: e.activation(out=sd, in_=ss, func=AF.Sqrt, scale=inv_n, bias=eps_t),
                 r=[B_ss, B_const], w=[B_sd])
            s.op("dve", lambda e: e.reciprocal(out=rs, in_=sd), r=[B_sd], w=[B_rs])

        def transposes(srcs, B_src, tb, dst, B_dst, evac):
            n = len(srcs)
            assert n <= 8
            for c, sp_ in enumerate(srcs):
                s.op("pe", lambda e, c=c, sp_=sp_: e.transpose(PBT[tb][:, c * 128:(c + 1) * 128], sp_, ident),
                     r=[B_src, B_const], w=[B_PB[tb]])
            src_ps = PBT[tb][:, 0:n * 128]
            if len(dst.shape) == 3:
                src_ps = src_ps.rearrange("p (h s) -> p h s", h=n)
            if evac == "act":
                s.op("act", lambda e: e.copy(dst, src_ps), r=[B_PB[tb]], w=[B_dst])
            else:
                s.op("dve", lambda e: e.tensor_copy(dst, src_ps), r=[B_PB[tb]], w=[B_dst])

        def load_w_bf16(dst_flat, kcn, ncols, src2d, Bw, key):
            for kc in range(kcn):
                s.dma("pool", dst_flat[:, kc * ncols:(kc + 1) * ncols], src2d[kc * 128:(kc + 1) * 128, :],
                      w=[Bw], key=key)

        def rope(eng_a, eng_b, xin, nh, hd, cos_t, sin_t, t1, t2, outb, B_in, B_tab, B_t1, B_t2, B_out):
            q = hd // 4
            xv = xin.rearrange("p (h a b c) -> p h a b c", h=nh, a=2, b=2, c=q)
            t2v = t2.rearrange("p (h a b c) -> p h a b c", h=nh, a=2, b=2, c=q)
            sv = sin_t.rearrange("p (a b c) -> p a b c", a=2, b=2, c=q)
            cb = cos_t.unsqueeze(1).to_broadcast([128, nh, hd])
            x3 = xin.rearrange("p (h d) -> p h d", h=nh)
            t13 = t1.rearrange("p (h d) -> p h d", h=nh)
            s.op(eng_a, lambda e: e.tensor_tensor(out=t13, in0=x3, in1=cb, op=ALU.mult),
                 r=[B_in, B_tab], w=[B_t1])
            for b in range(2):
                sb_ = sv[:, :, b, :].unsqueeze(1).to_broadcast([128, nh, 2, q])
                s.op(eng_b, lambda e, b=b, sb_=sb_: e.tensor_tensor(out=t2v[:, :, :, b, :], in0=xv[:, :, :, 1 - b, :],
                                                                  in1=sb_, op=ALU.mult),
                     r=[B_in, B_tab], w=[B_t2])
            s.op(eng_a, lambda e: e.tensor_tensor(out=outb, in0=t1, in1=t2, op=ALU.add),
                 r=[B_t1, B_t2], w=[B_out])

        def phase_M():
            m0 = A.mark()
            gmem = A.f32(1024)
            gk = [A.f32(128) for _ in range(2)]
            wm = A.bf16(8 * 1024)
            memt = [A.f32(1024) for _ in range(2)]
            junk = A.bf16(1024)
            hb = A.bf16(1024)
            hmT = [A.bf16(1024) for _ in range(2)]
            sq = A.f32(512)
            kf = A.f32(512)
            knb = A.bf16(512)
            small = A.f32(16)
            Bg, Bwm, Bjunk, Bhb, Bsq, Bkf, Bknb = (Buf(n) for n in ("Mg", "Mwm", "Mjunk", "Mhb", "Msq", "Mkf", "Mknb"))
            Bmem = [Buf("Mmem0"), Buf("Mmem1")]
            BhmT = [Buf("MhmT0"), Buf("MhmT1")]
            Bss, Bsd, Brs = Buf("Mss"), Buf("Msd"), Buf("Mrs")
            s.dma("sp", gmem, bcast_row(mem_norm, 1024), w=[Bg], key="d_Mg")
            for i in range(2):
                s.dma("sp", gk[i], bcast_row(memk_norm[i], 128), w=[Bg], key="d_Mg")
            for mt in range(2):
                s.dma("sp", memt[mt], mem[mt * 128:(mt + 1) * 128, :], w=[Bmem[mt]], key="d_Mmem%d" % mt)
            for mt in range(2):
                ss, sd, rs = small[:, 0:1], small[:, 1:2], small[:, 2:3]
                s.op("act", lambda e, mt=mt: e.activation(out=junk, in_=memt[mt], func=AF.Square, accum_out=ss),
                     r=[Bmem[mt]], w=[Bjunk, Bss])
                rstd_ops(ss, 1, 1.0 / 1024, sd, rs, Bss, Bsd, Brs)
                s.op("dve", lambda e, mt=mt: e.scalar_tensor_tensor(out=hb, in0=memt[mt], scalar=rs, in1=gmem,
                                                                     op0=ALU.mult, op1=ALU.mult),
                     r=[Bmem[mt], Brs, Bg], w=[Bhb])
                transposes([hb[:, c * 128:(c + 1) * 128] for c in range(8)], Bhb, 0, hmT[mt], BhmT[mt], "act")
            for i in range(2):
                load_w_bf16(wm, 8, 1024, w_mem_kv[i], Bwm, "d_Mwm")
                for mt in range(2):
                    for n in range(2):
                        for kc in range(8):
                            s.op("pe", lambda e, n=n, kc=kc, mt=mt: e.matmul(
                                PB[1 + n], hmT[mt][:, kc * 128:(kc + 1) * 128],
                                wm[:, kc * 1024 + n * 512:kc * 1024 + (n + 1) * 512],
                                start=(kc == 0), stop=(kc == 7)),
                                r=[BhmT[mt], Bwm], w=[B_PB[1 + n]])
                    ss4, sd4, rs4 = small[:, 4:8], small[:, 8:12], small[:, 12:16]
                    s.op("act", lambda e: e.activation(out=sq, in_=PB[1], func=AF.Square), r=[B_PB[1]], w=[Bsq])
                    s.op("dve", lambda e: e.tensor_reduce(out=ss4, in_=sq.rearrange("p (h d) -> p h d", h=4),
                                                          axis=AX.X, op=ALU.add), r=[Bsq], w=[Bss])
                    rstd_ops(ss4, 4, 1.0 / 128, sd4, rs4, Bss, Bsd, Brs)
                    s.op("dve", lambda e: e.tensor_tensor(out=kf.rearrange("p (h d) -> p h d", h=4),
                                                          in0=PB[1].rearrange("p (h d) -> p h d", h=4),
                                                          in1=rs4.unsqueeze(2).to_broadcast([128, 4, 128]), op=ALU.mult),
                         r=[B_PB[1], Brs], w=[Bkf])
                    s.op("pool", lambda e, i=i: e.tensor_tensor(out=knb.rearrange("p (h d) -> p h d", h=4),
                                                                 in0=kf.rearrange("p (h d) -> p h d", h=4),
                                                                 in1=gk[i].unsqueeze(1).to_broadcast([128, 4, 128]),
                                                                 op=ALU.mult),
                         r=[Bkf, Bg], w=[Bknb])
                    dstK = KmT[i].rearrange("p (h m) -> p h m", h=4)[:, :, mt * 128:(mt + 1) * 128]
                    for c in range(4):
                        s.op("pe", lambda e, c=c: e.transpose(PBT[0][:, c * 128:(c + 1) * 128],
                                                              knb[:, c * 128:(c + 1) * 128], ident),
                             r=[Bknb, B_const], w=[B_PB[0]])
                    s.op("act", lambda e, dstK=dstK: e.copy(dstK, PBT[0][:, 0:512].rearrange("p (h m) -> p h m", h=4)),
                         r=[B_PB[0]], w=[B_KmT[i]])
                    s.op("dve", lambda e, i=i, mt=mt: e.tensor_copy(Vm[i][:, mt * 512:(mt + 1) * 512], PB[2]),
                         r=[B_PB[2]], w=[B_Vm[i]])
            s.barrier()
            A.reset(m0)

        def phase_A_gqa(xsrc, B_xsrc):
            m0 = A.mark()
            W = 3
            win = A.bf16(8 * 2048)
            g1 = A.f32(1024)
            G14 = A.f32(14 * 128)
            gtmp = A.f32(3 * 128)
            Bw, Bg = Buf("A1w"), Buf("A1g")
            wsrc = gqa_w_in[0]
            for kc in range(8):
                rows = slice(kc * 128, (kc + 1) * 128)
                base = kc * 2048
                s.dma("pool", win[:, base:base + 1280], wsrc[rows, 0:1280], w=[Bw], key="d_A1w")
                s.dma("pool", win[:, base + 1280:base + 1792], wsrc[rows, 1536:2048], w=[Bw], key="d_A1w")
                s.dma("pool", win[:, base + 1792:base + 2048], wsrc[rows, 1280:1536], w=[Bw], key="d_A1w")
            s.dma("sp", g1, bcast_row(norm_mix[1], 1024), w=[Bg], key="d_A1g")
            s.dma("sp", gtmp[:, 0:128], bcast_row(gqa_q_norm[0], 128), w=[Bg], key="d_A1g")
            s.dma("sp", gtmp[:, 128:256], bcast_row(gqa_k_norm[0], 128), w=[Bg], key="d_A1g")
            s.dma("sp", gtmp[:, 256:384], bcast_row(memq_norm[1], 128), w=[Bg], key="d_A1g")
            for h in range(14):
                src = gtmp[:, 0:128] if h < 8 else (gtmp[:, 128:256] if h < 10 else gtmp[:, 256:384])
                s.op("pool", lambda e, h=h, src=src: e.tensor_copy(G14[:, h * 128:(h + 1) * 128], src),
                     r=[Bg], w=[Bg])
            sl = []
            for k in range(W):
                d = dict(
                    xt=A.f32(1024), cos=A.f32(128), sin=A.f32(128), junk=A.bf16(1024), hb=A.bf16(1024),
                    hT=A.bf16(1024), sq=A.f32(1792), qf=A.f32(1792), t1=A.f32(1280),
                    qb=A.bf16(1792), small=A.f32(48))
                d["t2"] = d["sq"][:, 0:1280]
                d["B"] = {n: Buf("A1%s%d" % (n, k)) for n in
                          ("xt", "tab", "junk", "hb", "hT", "sq", "qf", "t1", "t2", "qb", "ss", "sd", "rs",
                           "ss14", "sd14", "rs14")}
                d["B"]["t2"] = d["B"]["sq"]
                sl.append(d)
            stg = [A.bf16(14 * 512) for _ in range(2)]
            vst = [A.bf16(2 * 4 * 128) for _ in range(2)]
            Bstg = [Buf("A1stg0"), Buf("A1stg1")]
            Bvst = [Buf("A1vst0"), Buf("A1vst1")]
            TBK = (0, 1)
            PBK = (2, 3, 4, 5)

            def tile(t):
                k = t % W
                d = sl[k]
                Bk = d["B"]
                g = t // 4
                j = t % 4
                sg = stg[g % 2]
                rows = slice(t * 128, (t + 1) * 128)
                s.dma("sp", d["xt"], xsrc[rows, :], r=[B_xsrc], w=[Bk["xt"]], key="d_A1x%d" % k)
                s.dma("sp", d["cos"], cosg[rows, :], w=[Bk["tab"]], key="d_A1t%d" % k)
                s.dma("sp", d["sin"], sing[rows, :], w=[Bk["tab"]], key="d_A1t%d" % k)
                sm = d["small"]
                ss, sd, rs = sm[:, 0:1], sm[:, 1:2], sm[:, 2:3]
                ss14, sd14, rs14 = sm[:, 4:18], sm[:, 18:32], sm[:, 32:46]
                s.op("act", lambda e: e.activation(out=d["junk"], in_=d["xt"], func=AF.Square, accum_out=ss),
                     r=[Bk["xt"]], w=[Bk["junk"], Bk["ss"]])
                rstd_ops(ss, 1, 1.0 / 1024, sd, rs, Bk["ss"], Bk["sd"], Bk["rs"])
                s.op("dve", lambda e: e.scalar_tensor_tensor(out=d["hb"], in0=d["xt"], scalar=rs, in1=g1,
                                                             op0=ALU.mult, op1=ALU.mult),
                     r=[Bk["xt"], Bk["rs"], Bg], w=[Bk["hb"]])
                yield
                transposes([d["hb"][:, c * 128:(c + 1) * 128] for c in range(8)], Bk["hb"], TBK[0], d["hT"], Bk["hT"], "act")
                yield
                for n in range(4):
                    for kc in range(8):
                        s.op("pe", lambda e, n=n, kc=kc: e.matmul(
                            PB[PBK[n]], d["hT"][:, kc * 128:(kc + 1) * 128],
                            win[:, kc * 2048 + n * 512:kc * 2048 + (n + 1) * 512],
                            start=(kc == 0), stop=(kc == 7)), r=[Bk["hT"], Bw], w=[B_PB[PBK[n]]])
                for n in range(4):
                    wd = 512 if n < 3 else 256
                    s.op("act", lambda e, n=n, wd=wd: e.activation(out=d["sq"][:, n * 512:n * 512 + wd],
                                                                     in_=PB[PBK[n]][:, 0:wd], func=AF.Square),
                         r=[B_PB[PBK[n]]], w=[Bk["sq"]])
                s.op("dve", lambda e: e.tensor_reduce(out=ss14, in_=d["sq"].rearrange("p (h d) -> p h d", h=14),
                                                      axis=AX.X, op=ALU.add), r=[Bk["sq"]], w=[Bk["ss14"]])
                rstd_ops(ss14, 14, 1.0 / 128, sd14, rs14, Bk["ss14"], Bk["sd14"], Bk["rs14"])
                for n in range(4):
                    nh = 4 if n < 3 else 2
                    s.op("dve", lambda e, n=n, nh=nh: e.tensor_tensor(
                        out=d["qf"][:, n * 512:n * 512 + nh * 128].rearrange("p (h d) -> p h d", h=nh),
                        in0=PB[PBK[n]][:, 0:nh * 128].rearrange("p (h d) -> p h d", h=nh),
                        in1=rs14[:, n * 4:n * 4 + nh].unsqueeze(2).to_broadcast([128, nh, 128]), op=ALU.mult),
                        r=[B_PB[PBK[n]], Bk["rs14"]], w=[Bk["qf"]])
                vdst = vst[g % 2].rearrange("p (h j d) -> p h j d", h=2, j=4)[:, :, j, :]
                s.op("act", lambda e, vdst=vdst: e.copy(vdst, PB[PBK[3]][:, 256:512].rearrange("p (h d) -> p h d", h=2)),
                     r=[B_PB[PBK[3]]], w=[Bvst[g % 2]])
                yield
                s.op("pool", lambda e: e.tensor_tensor(out=d["qf"], in0=d["qf"], in1=G14, op=ALU.mult),
                     r=[Bg], w=[Bk["qf"]])
                rope("pool", "dve", d["qf"][:, 0:1280], 10, 128, d["cos"], d["sin"], d["t1"], d["t2"],
                     d["qb"][:, 0:1280], Bk["qf"], Bk["tab"], Bk["t1"], Bk["t2"], Bk["qb"])
                s.op("act", lambda e: e.copy(d["qb"][:, 1280:1792], d["qf"][:, 1280:1792]), r=[Bk["qf"]], w=[Bk["qb"]])
                yield
                sg3 = sg.rearrange("p (h s) -> p h s", h=14)
                transposes([d["qb"][:, h * 128:(h + 1) * 128] for h in range(8)], Bk["qb"], TBK[1],
                           sg3[:, 0:8, j * 128:(j + 1) * 128], Bstg[g % 2], "dve")
                yield
                transposes([d["qb"][:, h * 128:(h + 1) * 128] for h in range(8, 14)], Bk["qb"], TBK[0],
                           sg3[:, 8:14, j * 128:(j + 1) * 128], Bstg[g % 2], "act")
                yield

            def group_done(g):
                sg3 = stg[g % 2].rearrange("p (h s) -> p h s", h=14)
                cols = slice(g * 512, (g + 1) * 512)
                key = "d_A1st%d" % (g % 2)
                s.dma("sp", QnT[:, :, cols].rearrange("h d s -> d h s"), sg3[:, 0:8, :],
                      r=[Bstg[g % 2]], w=[B_QnT], key=key)
                s.dma("sp", KnT[0:2, :, cols].rearrange("h d s -> d h s"), sg3[:, 8:10, :],
                      r=[Bstg[g % 2]], w=[B_KnT], key=key)
                s.dma("sp", QmT[:, :, cols].rearrange("h d s -> d h s"), sg3[:, 10:14, :],
                      r=[Bstg[g % 2]], w=[B_QmT], key=key)
                s.dma("sp", Vd[0:2, :, g * 4:(g + 1) * 4, :].rearrange("h p t d -> p h t d"),
                      vst[g % 2].rearrange("p (h j d) -> p h j d", h=2, j=4),
                      r=[Bvst[g % 2]], w=[B_Vd], key="d_A1sv%d" % (g % 2))

            done = set()

            def on_done(idx):
                done.add(idx)
                g = idx // 4
                if all((g * 4 + jj) in done for jj in range(4)):
                    group_done(g)
            interleave([tile(t) for t in range(NT)], W, on_done)
            s.barrier()
            A.reset(m0)

        def phase_A_mla(xsrc, B_xsrc):
            m0 = A.mark()
            W = 2
            win = A.bf16(8 * 1216)
            wqb = A.bf16(3 * 1536)
            wkvb = A.bf16(2 * 2048)
            g0 = A.f32(1024)
            gqa_ = A.f32(384)
            gkva = A.f32(256)
            gqn = A.f32(192)
            gkn = A.f32(192)
            gmq = A.f32(128)
            Bw, Bg = Buf("A0w"), Buf("A0g")
            load_w_bf16(win, 8, 1216, mla_w_in[0], Bw, "d_A0w")
            for kc in range(3):
                rows = slice(kc * 128, (kc + 1) * 128)
                srcv = mla_w_q_b[0][rows, :].rearrange("p (h e) -> p h e", e=192)
                s.dma("pool", wqb[:, kc * 1536:kc * 1536 + 1024].rearrange("p (h d) -> p h d", d=128),
                      srcv[:, :, 0:128], w=[Bw], key="d_A0w")
                s.dma("pool", wqb[:, kc * 1536 + 1024:(kc + 1) * 1536].rearrange("p (h d) -> p h d", d=64),
                      srcv[:, :, 128:192], w=[Bw], key="d_A0w")
            for kc in range(2):
                rows = slice(kc * 128, (kc + 1) * 128)
                srcv = mla_w_kv_b[0][rows, :].rearrange("p (h e) -> p h e", e=256)
                s.dma("pool", wkvb[:, kc * 2048:kc * 2048 + 1024].rearrange("p (h d) -> p h d", d=128),
                      srcv[:, :, 0:128], w=[Bw], key="d_A0w")
                s.dma("pool", wkvb[:, kc * 2048 + 1024:(kc + 1) * 2048].rearrange("p (h d) -> p h d", d=128),
                      srcv[:, :, 128:256], w=[Bw], key="d_A0w")
            s.dma("sp", g0, bcast_row(norm_mix[0], 1024), w=[Bg], key="d_A0g")
            s.dma("sp", gqa_, bcast_row(mla_q_a_norm[0], 384), w=[Bg], key="d_A0g")
            s.dma("sp", gkva, bcast_row(mla_kv_a_norm[0], 256), w=[Bg], key="d_A0g")
            s.dma("sp", gqn, bcast_row(mla_q_norm[0], 192), w=[Bg], key="d_A0g")
            s.dma("sp", gkn, bcast_row(mla_k_norm[0], 192), w=[Bg], key="d_A0g")
            s.dma("sp", gmq, bcast_row(memq_norm[0], 128), w=[Bg], key="d_A0g")
            sl = []
            names = ("xt", "tab", "junk", "hb", "hT", "sq", "f1", "f2", "t1", "t2", "cqb", "cT", "kpf", "kpb",
                     "qmb", "qnb", "qpb", "knb", "ss", "sd", "rs", "ssA", "sdA", "rsA", "ssq", "sdq", "rsq")
            for k in range(W):
                d = dict(
                    xt=A.f32(1024), cos=A.f32(64), sin=A.f32(64), junk=A.bf16(1024), hb=A.bf16(1024),
                    hT=A.bf16(1024), sq=A.f32(1536), f1=A.f32(1024), f2=A.f32(512), t1=A.f32(512), t2=A.f32(512),
                    cqb=A.bf16(640), cT=A.bf16(640), kpf=A.f32(64), kpb=A.bf16(128), qmb=A.bf16(512),
                    qnb=A.bf16(1024), qpb=A.bf16(512), knb=A.bf16(1024), small=A.f32(80))
                d["B"] = {n: Buf("A0%s%d" % (n, k)) for n in names}
                sl.append(d)
            stg = [A.bf16(25 * 512) for _ in range(2)]
            vst = [A.bf16(8 * 4 * 128) for _ in range(2)]
            Bstg = [Buf("A0stg0"), Buf("A0stg1")]
            Bvst = [Buf("A0vst0"), Buf("A0vst1")]
            TB0, TB1 = 0, 1
            P0, P1, P2 = 2, 3, 4
            Q0, Q1, Q2 = 5, 6, 7
            KV = (2, 3, 4, 5)

            def tile(t):
                k = t % W
                d = sl[k]
                Bk = d["B"]
                g = t // 4
                j = t % 4
                sg3 = stg[g % 2].rearrange("p (h s) -> p h s", h=25)
                rows = slice(t * 128, (t + 1) * 128)
                s.dma("sp", d["xt"], xsrc[rows, :], r=[B_xsrc], w=[Bk["xt"]], key="d_A0x%d" % k)
                s.dma("sp", d["cos"], cosm[rows, :], w=[Bk["tab"]], key="d_A0t%d" % k)
                s.dma("sp", d["sin"], sinm[rows, :], w=[Bk["tab"]], key="d_A0t%d" % k)
                sm = d["small"]
                ss, sd, rs = sm[:, 0:1], sm[:, 1:2], sm[:, 2:3]
                ssA, sdA, rsA = sm[:, 4:11], sm[:, 12:19], sm[:, 20:27]
                ssq, sdq, rsq = sm[:, 28:52], sm[:, 52:76], None
                s.op("act", lambda e: e.activation(out=d["junk"], in_=d["xt"], func=AF.Square, accum_out=ss),
                     r=[Bk["xt"]], w=[Bk["junk"], Bk["ss"]])
                rstd_ops(ss, 1, 1.0 / 1024, sd, rs, Bk["ss"], Bk["sd"], Bk["rs"])
                s.op("dve", lambda e: e.scalar_tensor_tensor(out=d["hb"], in0=d["xt"], scalar=rs, in1=g0,
                                                             op0=ALU.mult, op1=ALU.mult),
                     r=[Bk["xt"], Bk["rs"], Bg], w=[Bk["hb"]])
                yield
                transposes([d["hb"][:, c * 128:(c + 1) * 128] for c in range(8)], Bk["hb"], TB0, d["hT"], Bk["hT"], "act")
                yield
                for (pb, c0, wd) in ((P0, 0, 384), (P1, 384, 320), (P2, 704, 512)):
                    for kc in range(8):
                        s.op("pe", lambda e, pb=pb, c0=c0, wd=wd, kc=kc: e.matmul(
                            PB[pb][:, 0:wd], d["hT"][:, kc * 128:(kc + 1) * 128],
                            win[:, kc * 1216 + c0:kc * 1216 + c0 + wd],
                            start=(kc == 0), stop=(kc == 7)), r=[Bk["hT"], Bw], w=[B_PB[pb]])
                sq = d["sq"]
                s.op("act", lambda e: e.activation(out=sq[:, 0:384], in_=PB[P0][:, 0:384], func=AF.Square,
                                                   accum_out=ssA[:, 0:1]), r=[B_PB[P0]], w=[Bk["sq"], Bk["ssA"]])
                s.op("act", lambda e: e.activation(out=sq[:, 384:640], in_=PB[P1][:, 0:256], func=AF.Square,
                                                   accum_out=ssA[:, 1:2]), r=[B_PB[P1]], w=[Bk["sq"], Bk["ssA"]])
                s.op("act", lambda e: e.activation(out=sq[:, 640:704], in_=PB[P1][:, 256:320], func=AF.Square,
                                                   accum_out=ssA[:, 2:3]), r=[B_PB[P1]], w=[Bk["sq"], Bk["ssA"]])
                s.op("act", lambda e: e.activation(out=sq[:, 704:1216], in_=PB[P2], func=AF.Square),
                     r=[B_PB[P2]], w=[Bk["sq"]])
                s.op("dve", lambda e: e.tensor_reduce(out=ssA[:, 3:7], in_=sq[:, 704:1216].rearrange("p (h d) -> p h d", h=4),
                                                      axis=AX.X, op=ALU.add), r=[Bk["sq"]], w=[Bk["ssA"]])
                for (c0, c1, inv) in ((0, 1, 1.0 / 384), (1, 2, 1.0 / 256), (2, 3, 1.0 / 64), (3, 7, 1.0 / 128)):
                    s.op("act", lambda e, c0=c0, c1=c1, inv=inv: e.activation(
                        out=sdA[:, c0:c1], in_=ssA[:, c0:c1], func=AF.Sqrt, scale=inv, bias=eps_t),
                        r=[Bk["ssA"], B_const], w=[Bk["sdA"]])
                s.op("dve", lambda e: e.reciprocal(out=rsA, in_=sdA), r=[Bk["sdA"]], w=[Bk["rsA"]])
                cqb = d["cqb"]
                s.op("dve", lambda e: e.scalar_tensor_tensor(out=cqb[:, 0:384], in0=PB[P0][:, 0:384], scalar=rsA[:, 0:1],
                                                             in1=gqa_, op0=ALU.mult, op1=ALU.mult),
                     r=[B_PB[P0], Bk["rsA"], Bg], w=[Bk["cqb"]])
                s.op("dve", lambda e: e.scalar_tensor_tensor(out=cqb[:, 384:640], in0=PB[P1][:, 0:256], scalar=rsA[:, 1:2],
                                                             in1=gkva, op0=ALU.mult, op1=ALU.mult),
                     r=[B_PB[P1], Bk["rsA"], Bg], w=[Bk["cqb"]])
                s.op("dve", lambda e: e.scalar_tensor_tensor(out=d["kpf"], in0=PB[P1][:, 256:320], scalar=rsA[:, 2:3],
                                                             in1=gkn[:, 128:192], op0=ALU.mult, op1=ALU.mult),
                     r=[B_PB[P1], Bk["rsA"], Bg], w=[Bk["kpf"]])
                s.op("dve", lambda e: e.tensor_tensor(out=d["f2"].rearrange("p (h d) -> p h d", h=4),
                                                      in0=PB[P2].rearrange("p (h d) -> p h d", h=4),
                                                      in1=rsA[:, 3:7].unsqueeze(2).to_broadcast([128, 4, 128]), op=ALU.mult),
                     r=[B_PB[P2], Bk["rsA"]], w=[Bk["f2"]])
                yield
                transposes([cqb[:, c * 128:(c + 1) * 128] for c in range(5)], Bk["cqb"], TB1, d["cT"], Bk["cT"], "act")
                s.op("pool", lambda e: e.tensor_tensor(out=d["qmb"].rearrange("p (h d) -> p h d", h=4),
                                                       in0=d["f2"].rearrange("p (h d) -> p h d", h=4),
                                                       in1=gmq.unsqueeze(1).to_broadcast([128, 4, 128]), op=ALU.mult),
                     r=[Bk["f2"], Bg], w=[Bk["qmb"]])
                rope("pool", "pool", d["kpf"], 1, 64, d["cos"], d["sin"], d["t1"][:, 0:64], d["t2"][:, 0:64],
                     d["kpb"][:, 0:64], Bk["kpf"], Bk["tab"], Bk["t1"], Bk["t2"], Bk["kpb"])
                s.op("pool", lambda e: e.tensor_copy(d["kpb"][:, 64:128], d["kpb"][:, 0:64]), w=[Bk["kpb"]])
                yield
                for n, qb_ in enumerate((Q0, Q1, Q2)):
                    for kc in range(3):
                        s.op("pe", lambda e, n=n, qb_=qb_, kc=kc: e.matmul(
                            PB[qb_], d["cT"][:, kc * 128:(kc + 1) * 128],
                            wqb[:, kc * 1536 + n * 512:kc * 1536 + (n + 1) * 512],
                            start=(kc == 0), stop=(kc == 2)), r=[Bk["cT"], Bw], w=[B_PB[qb_]])
                for n, qb_ in enumerate((Q0, Q1, Q2)):
                    s.op("act", lambda e, n=n, qb_=qb_: e.activation(out=sq[:, n * 512:(n + 1) * 512], in_=PB[qb_],
                                                                       func=AF.Square), r=[B_PB[qb_]], w=[Bk["sq"]])
                s.op("dve", lambda e: e.tensor_reduce(out=ssq[:, 0:8], in_=sq[:, 0:1024].rearrange("p (h d) -> p h d", h=8),
                                                      axis=AX.X, op=ALU.add), r=[Bk["sq"]], w=[Bk["ssq"]])
                s.op("dve", lambda e: e.tensor_reduce(out=ssq[:, 8:16], in_=sq[:, 1024:1536].rearrange("p (h d) -> p h d", h=8),
                                                      axis=AX.X, op=ALU.add), r=[Bk["sq"]], w=[Bk["ssq"]])
                s.op("act", lambda e: e.activation(out=sdq[:, 0:8], in_=ssq[:, 0:8], func=AF.Sqrt, scale=1.0 / 128,
                                                   bias=eps_t), r=[Bk["ssq"], B_const], w=[Bk["sdq"]])
                s.op("act", lambda e: e.activation(out=sdq[:, 8:16], in_=ssq[:, 8:16], func=AF.Sqrt, scale=1.0 / 64,
                                                   bias=eps_t), r=[Bk["ssq"], B_const], w=[Bk["sdq"]])
                rq = sm[:, 52:68]
                s.op("dve", lambda e: e.reciprocal(out=rq, in_=sdq[:, 0:16]), w=[Bk["sdq"]])
                for n, qb_ in enumerate((Q0, Q1)):
                    s.op("dve", lambda e, n=n, qb_=qb_: e.tensor_tensor(
                        out=d["f1"][:, n * 512:(n + 1) * 512].rearrange("p (h d) -> p h d", h=4),
                        in0=PB[qb_].rearrange("p (h d) -> p h d", h=4),
                        in1=rq[:, n * 4:(n + 1) * 4].unsqueeze(2).to_broadcast([128, 4, 128]), op=ALU.mult),
                        r=[B_PB[qb_], Bk["sdq"]], w=[Bk["f1"]])
                s.op("pool", lambda e: e.tensor_tensor(out=d["qnb"].rearrange("p (h d) -> p h d", h=8),
                                                       in0=d["f1"].rearrange("p (h d) -> p h d", h=8),
                                                       in1=gqn[:, 0:128].unsqueeze(1).to_broadcast([128, 8, 128]), op=ALU.mult),
                     r=[Bk["f1"], Bg], w=[Bk["qnb"]])
                s.op("dve", lambda e: e.tensor_tensor(out=d["f2"].rearrange("p (h d) -> p h d", h=8),
                                                      in0=PB[Q2].rearrange("p (h d) -> p h d", h=8),
                                                      in1=rq[:, 8:16].unsqueeze(2).to_broadcast([128, 8, 64]), op=ALU.mult),
                     r=[B_PB[Q2], Bk["sdq"]], w=[Bk["f2"]])
                s.op("pool", lambda e: e.tensor_tensor(out=d["f2"].rearrange("p (h d) -> p h d", h=8),
                                                       in0=d["f2"].rearrange("p (h d) -> p h d", h=8),
                                                       in1=gqn[:, 128:192].unsqueeze(1).to_broadcast([128, 8, 64]), op=ALU.mult),
                     r=[Bg], w=[Bk["f2"]])
                rope("pool", "dve", d["f2"], 8, 64, d["cos"], d["sin"], d["t1"], d["t2"], d["qpb"],
                     Bk["f2"], Bk["tab"], Bk["t1"], Bk["t2"], Bk["qpb"])
                yield
                transposes([d["qnb"][:, h * 128:(h + 1) * 128] for h in range(8)], Bk["qnb"], TB0,
                           sg3[:, 0:8, j * 128:(j + 1) * 128], Bstg[g % 2], "act")
                yield
                srcs = [d["qpb"][:, c * 128:(c + 1) * 128] for c in range(4)] + [d["kpb"]] + \
                       [d["qmb"][:, c * 128:(c + 1) * 128] for c in range(3)]
                for c, sp_ in enumerate(srcs):
                    s.op("pe", lambda e, c=c, sp_=sp_: e.transpose(PBT[TB1][:, c * 128:(c + 1) * 128], sp_, ident),
                         r=[Bk["qpb"], Bk["kpb"], Bk["qmb"], B_const], w=[B_PB[TB1]])
                s.op("dve", lambda e: e.tensor_copy(sg3[:, 16:24, j * 128:(j + 1) * 128],
                                                    PBT[TB1].rearrange("p (h s) -> p h s", h=8)),
                     r=[B_PB[TB1]], w=[Bstg[g % 2]])
                yield
                for n in range(4):
                    for kc in range(2):
                        s.op("pe", lambda e, n=n, kc=kc: e.matmul(
                            PB[KV[n]], d["cT"][:, (3 + kc) * 128:(4 + kc) * 128],
                            wkvb[:, kc * 2048 + n * 512:kc * 2048 + (n + 1) * 512],
                            start=(kc == 0), stop=(kc == 1)), r=[Bk["cT"], Bw], w=[B_PB[KV[n]]])
                for n in range(2):
                    s.op("act", lambda e, n=n: e.activation(out=sq[:, n * 512:(n + 1) * 512], in_=PB[KV[n]],
                                                              func=AF.Square), r=[B_PB[KV[n]]], w=[Bk["sq"]])
                s.op("dve", lambda e: e.tensor_reduce(out=ssq[:, 16:24], in_=sq[:, 0:1024].rearrange("p (h d) -> p h d", h=8),
                                                      axis=AX.X, op=ALU.add), r=[Bk["sq"]], w=[Bk["ssq"]])
                s.op("act", lambda e: e.activation(out=sdq[:, 16:24], in_=ssq[:, 16:24], func=AF.Sqrt, scale=1.0 / 128,
                                                   bias=eps_t), r=[Bk["ssq"], B_const], w=[Bk["sdq"]])
                rk = sm[:, 68:76]
                s.op("dve", lambda e: e.reciprocal(out=rk, in_=sdq[:, 16:24]), w=[Bk["sdq"]])
                for n in range(2):
                    s.op("dve", lambda e, n=n: e.tensor_tensor(
                        out=d["f1"][:, n * 512:(n + 1) * 512].rearrange("p (h d) -> p h d", h=4),
                        in0=PB[KV[n]].rearrange("p (h d) -> p h d", h=4),
                        in1=rk[:, n * 4:(n + 1) * 4].unsqueeze(2).to_broadcast([128, 4, 128]), op=ALU.mult),
                        r=[B_PB[KV[n]], Bk["sdq"]], w=[Bk["f1"]])
                s.op("pool", lambda e: e.tensor_tensor(out=d["knb"].rearrange("p (h d) -> p h d", h=8),
                                                       in0=d["f1"].rearrange("p (h d) -> p h d", h=8),
                                                       in1=gkn[:, 0:128].unsqueeze(1).to_broadcast([128, 8, 128]), op=ALU.mult),
                     r=[Bk["f1"], Bg], w=[Bk["knb"]])
                vv = vst[g % 2].rearrange("p (h j d) -> p h j d", h=8, j=4)
                for n in range(2):
                    s.op("act", lambda e, n=n: e.copy(vv[:, n * 4:(n + 1) * 4, j, :],
                                                      PB[KV[2 + n]].rearrange("p (h d) -> p h d", h=4)),
                         r=[B_PB[KV[2 + n]]], w=[Bvst[g % 2]])
                yield
                transposes([d["knb"][:, h * 128:(h + 1) * 128] for h in range(8)], Bk["knb"], TB0,
                           sg3[:, 8:16, j * 128:(j + 1) * 128], Bstg[g % 2], "act")
                s.op("pe", lambda e: e.transpose(PBT[TB1][:, 0:128], d["qmb"][:, 384:512], ident),
                     r=[Bk["qmb"], B_const], w=[B_PB[TB1]])
                s.op("dve", lambda e: e.tensor_copy(sg3[:, 24, j * 128:(j + 1) * 128], PBT[TB1][:, 0:128]),
                     r=[B_PB[TB1]], w=[Bstg[g % 2]])
                yield

            def group_done(g):
                sg3 = stg[g % 2].rearrange("p (h s) -> p h s", h=25)
                cols = slice(g * 512, (g + 1) * 512)
                key = "d_A0st%d" % (g % 2)
                Bs = Bstg[g % 2]
                s.dma("sp", QnT[:, :, cols].rearrange("h d s -> d h s"), sg3[:, 0:8, :], r=[Bs], w=[B_QnT], key=key)
                s.dma("sp", KnT[:, :, cols].rearrange("h d s -> d h s"), sg3[:, 8:16, :], r=[Bs], w=[B_KnT], key=key)
                s.dma("sp", QpT[:, :, cols].rearrange("h d s -> d h s"), sg3[:, 16:20, :], r=[Bs], w=[B_QpT], key=key)
                s.dma("sp", KpT[:, cols], sg3[:, 20, :], r=[Bs], w=[B_KpT], key=key)
                s.dma("sp", QmT[:, :, cols].rearrange("h d s -> d h s"), sg3[:, 21:25, :], r=[Bs], w=[B_QmT], key=key)
                s.dma("sp", Vd[:, :, g * 4:(g + 1) * 4, :].rearrange("h p t d -> p h t d"),
                      vst[g % 2].rearrange("p (h j d) -> p h j d", h=8, j=4), r=[Bvst[g % 2]], w=[B_Vd],
                      key="d_A0sv%d" % (g % 2))

            done = set()

            def on_done(idx):
                done.add(idx)
                g = idx // 4
                if all((g * 4 + jj) in done for jj in range(4)):
                    group_done(g)
            interleave([tile(t) for t in range(NT)], W, on_done)
            s.barrier()
            A.reset(m0)

        def phase_B(li, mla):
            m0 = A.mark()
            NS = 3
            NPT = 4
            OBK, DBK = 6, 7
            SPAIR = [ps[:, k * 1024:(k + 1) * 1024] for k in range(NS)]
            PT = [A.bf16(1024) for _ in range(NPT)]
            PSUMS = [A.bf16(512) for _ in range(2)]
            qn_t = [A.bf16(512) for _ in range(3)]
            qp_t = [A.bf16(512) for _ in range(3)] if mla else None
            rc = A.f32(512)
            ost = [A.bf16(512) for _ in range(2)]
            Bost = [Buf("Bost0"), Buf("Bost1")]
            Brc = Buf("Brc")
            for kc in range(8):
                for gu in range(2):
                    for fh in range(2):
                        f0 = fh * 11
                        src = w_gate_up[li, kc * 128:(kc + 1) * 128,
                                        gu * DFF + f0 * 128:gu * DFF + (f0 + 11) * 128]
                        src = src.rearrange("p (f c) -> p f c", c=128)
                        dst = wgub[li, f0:f0 + 11, :, gu, kc, :].rearrange("f p c -> p f c")
                        s.dma("pool", dst, src, w=[B_wgub[li]], key="d_wgu%d" % li)
            kt = [A.bf16(S) for _ in range(2)]
            vt = [A.bf16(S) for _ in range(2)]
            Bkv = [Buf("Bkv0"), Buf("Bkv1")]
            if mla:
                kp = A.bf16(S)
                Bkp = Buf("Bkp")
                s.dma("sp", kp, KpT, r=[B_KpT], w=[Bkp], key="d_Bkp")
            else:
                for h in range(2):
                    s.dma("sp", kt[h], KnT[h], r=[B_KnT], w=[Bkv[h]], key="d_Bkv%d" % h)
                    s.dma("sp", vt[h], Vd[h].rearrange("p t d -> p (t d)"), r=[B_Vd], w=[Bkv[h]], key="d_Bkv%d" % h)
            blocks = []
            for h in range(8):
                for qb in range(NB):
                    blocks.append(("main", h, qb))
            for h in range(4):
                for qb in range(NB):
                    blocks.append(("mem", h, qb))
            scale_main = (192.0 ** -0.5) if mla else (128.0 ** -0.5)
            scale_mem = 128.0 ** -0.5
            steps = []
            for n, (kind, h, qb) in enumerate(blocks):
                npair = (NT if kind == "main" else 2) // 2
                for i in range(npair):
                    steps.append((n, i, npair))
            G = len(steps)
            qtok, kvtok = {}, {}
            qk_tok = [None] * G
            exp_tok = [None] * G
            sum_tok = [None] * G
            pv_last_tok, norm_tok, last_qk_of_block = {}, {}, {}

            def issue_kv_load(h):
                b = h % 2
                deps = []
                if h >= 2:
                    deps.append(pv_last_tok[(h - 2) * NB + NB - 1])
                s.dma("sp", kt[b], KnT[h], r=[B_KnT], w=[], key="d_Bkv%d" % b, deps=deps)
                kvtok[h] = s.dma("sp", vt[b], Vd[h].rearrange("p t d -> p (t d)"), r=[B_Vd], w=[],
                                 key="d_Bkv%d" % b, deps=deps)

            def issue_q_load(n):
                kind, h, qb = blocks[n]
                b = n % 3
                deps = []
                if n >= 3:
                    deps.append(last_qk_of_block[n - 3])
                cols = slice(qb * 512, (qb + 1) * 512)
                if kind == "main":
                    t_ = s.dma("sp", qn_t[b], QnT[h][:, cols], r=[B_QnT], w=[], key="d_Bq%d" % b, deps=deps)
                    if mla:
                        t_ = s.dma("sp", qp_t[b], QpT[h // 2][:, cols], r=[B_QpT], w=[], key="d_Bq%d" % b, deps=deps)
                else:
                    t_ = s.dma("sp", qn_t[b], QmT[h][:, cols], r=[B_QmT], w=[], key="d_Bq%d" % b, deps=deps)
                qtok[n] = t_

            def do_qk(g):
                n, i, npair = steps[g]
                kind, h, qb = blocks[n]
                qt = qn_t[n % 3]
                deps = [qtok[n]]
                tok = None
                for half in range(2):
                    kti = 2 * i + half
                    outp = SPAIR[g % NS][:, half * 512:(half + 1) * 512]
                    if kind == "main":
                        if mla:
                            kb = kt[h % 2]
                            deps.append(kvtok[h])
                        else:
                            kb = kt[h // 4]
                        lhs = kb[:, kti * 128:(kti + 1) * 128]
                    else:
                        lhs = KmT[li][:, h * 256 + kti * 128:h * 256 + (kti + 1) * 128]
                    if kind == "main" and mla:
                        s.op("pe", lambda e, outp=outp, lhs=lhs: e.matmul(outp, lhs, qt, start=True, stop=False), deps=deps)
                        r0 = (h % 2) * 64
                        lhs2 = kp[r0:r0 + 64, kti * 128:(kti + 1) * 128]
                        rhs2 = qp_t[n % 3][r0:r0 + 64, :]
                        tok = s.op("pe", lambda e, outp=outp, lhs2=lhs2, rhs2=rhs2: e.matmul(outp, lhs2, rhs2, start=False, stop=True),
                                   r=[Bkp])
                    else:
                        rr = [Bkv[0], Bkv[1]] if kind == "main" else [B_KmT[li]]
                        tok = s.op("pe", lambda e, outp=outp, lhs=lhs: e.matmul(outp, lhs, qt, start=True, stop=True),
                                   deps=deps, r=rr)
                qk_tok[g] = tok
                if i == npair - 1:
                    last_qk_of_block[n] = tok

            def do_exp(g):
                n, i, npair = steps[g]
                sc = scale_main if blocks[n][0] == "main" else scale_mem
                sp_ = SPAIR[g % NS]
                pt = PT[g % NPT]
                exp_tok[g] = s.op("act", lambda e: e.activation(out=pt, in_=sp_, func=AF.Exp, scale=sc),
                                  deps=[qk_tok[g]])
                pss = PSUMS[g % 2]
                sum_tok[g] = s.op("dve", lambda e: e.tensor_tensor(out=pss, in0=pt[:, 0:512], in1=pt[:, 512:1024], op=ALU.add),
                                  deps=[exp_tok[g]])

            def do_pv(g):
                n, i, npair = steps[g]
                kind, h, qb = blocks[n]
                pt = PT[g % NPT]
                deps = [exp_tok[g]]
                if i == 0 and n >= 1:
                    deps.append(norm_tok[n - 1])
                for half in range(2):
                    kti = 2 * i + half
                    if kind == "main":
                        vb = vt[h % 2] if mla else vt[h // 4]
                        lhs = vb[:, kti * 128:(kti + 1) * 128]
                    else:
                        lhs = Vm[li][:, kti * 512 + h * 128:kti * 512 + (h + 1) * 128]
                    s.op("pe", lambda e, lhs=lhs, half=half: e.matmul(
                        PB[OBK], lhs, pt[:, half * 512:(half + 1) * 512],
                        start=(i == 0 and half == 0), stop=(i == npair - 1 and half == 1)), deps=deps,
                        r=([B_Vm[li]] if kind == "mem" else []))
                pss = PSUMS[g % 2]
                tok = s.op("pe", lambda e: e.matmul(PB[DBK], ones, pss, start=(i == 0), stop=(i == npair - 1)),
                           r=[B_const], deps=[sum_tok[g]])
                if i == npair - 1:
                    pv_last_tok[n] = tok
                    do_norm(n)

            def do_norm(n):
                kind, h, qb = blocks[n]
                ob = n % 2
                chunk = h if kind == "main" else 8 + h
                s.op("dve", lambda e: e.reciprocal(out=rc, in_=PB[DBK]), deps=[pv_last_tok[n]], w=[Brc])
                tok = s.op("dve", lambda e: e.tensor_tensor(out=ost[ob], in0=PB[OBK], in1=rc, op=ALU.mult),
                           r=[Brc], w=[Bost[ob]])
                norm_tok[n] = tok
                s.dma("sp", mixT[chunk][:, qb * 512:(qb + 1) * 512], ost[ob], r=[Bost[ob]], w=[B_mixT],
                      key="d_Bo%d" % ob)

            if mla:
                issue_kv_load(0)
            for n in range(min(2, len(blocks))):
                issue_q_load(n)
            kv_issued = 1
            LA = 2
            for g in range(min(LA, G)):
                do_qk(g)
            for g in range(G):
                do_exp(g)
                n, i, npair = steps[g]
                kind, h, qb = blocks[n]
                if i == 0:
                    if n + 2 < len(blocks):
                        issue_q_load(n + 2)
                    if mla and kind == "main" and qb == 1 and h + 1 < 8 and kv_issued == h + 1:
                        issue_kv_load(h + 1)
                        kv_issued += 1
                do_pv(g)
                if g + LA < G:
                    do_qk(g + LA)
            s.barrier()
            A.reset(m0)

        def phase_C1(li, xsrc, B_xsrc):
            m0 = A.mark()
            W = 3
            wo = A.bf16(12 * 1024)
            g2 = A.f32(1024)
            Bw, Bg = Buf("C1w"), Buf("C1g")
            load_w_bf16(wo, 12, 1024, w_out[li], Bw, "d_C1w")
            s.dma("sp", g2, bcast_row(norm_ffn[li], 1024), w=[Bg], key="d_C1g")
            sl = []
            for k in range(W):
                d = dict(mx=A.bf16(12 * 512), xb=A.f32(4 * 1024), hst=A.bf16(8 * 512), junk=A.bf16(1024),
                         hb=A.bf16(1024), small=A.f32(8))
                d["B"] = {n: Buf("C1%s%d" % (n, k)) for n in ("mx", "xb", "hst", "junk", "hb", "ss", "sd", "rs")}
                sl.append(d)
            OBK = (1, 2, 3, 4)
            TBK = (0, 5)

            def block(b):
                k = b % W
                d = sl[k]
                Bk = d["B"]
                cols = slice(b * 512, (b + 1) * 512)
                s.dma("sp", d["mx"].rearrange("p (c s) -> p c s", c=12), mixT[:, :, cols].rearrange("c p s -> p c s"),
                      r=[B_mixT], w=[Bk["mx"]], key="d_C1m%d" % k)
                s.dma("sp", d["xb"].rearrange("p (j n) -> p j n", j=4),
                      xsrc[b * 512:(b + 1) * 512, :].rearrange("(j p) n -> p j n", p=128),
                      r=[B_xsrc], w=[Bk["xb"]], key="d_C1x%d" % k)
                yield
                sm = d["small"]
                for j in range(4):
                    ob = (OBK[0], OBK[1]) if j % 2 == 0 else (OBK[2], OBK[3])
                    for n in range(2):
                        for c in range(12):
                            s.op("pe", lambda e, n=n, c=c, j=j, ob=ob: e.matmul(
                                PB[ob[n]], d["mx"][:, c * 512 + j * 128:c * 512 + (j + 1) * 128],
                                wo[:, c * 1024 + n * 512:c * 1024 + (n + 1) * 512],
                                start=(c == 0), stop=(c == 11)), r=[Bk["mx"], Bw], w=[B_PB[ob[n]]])
                    xj = d["xb"][:, j * 1024:(j + 1) * 1024]
                    for n in range(2):
                        s.op("dve", lambda e, n=n, ob=ob, xj=xj: e.tensor_tensor(
                            out=xj[:, n * 512:(n + 1) * 512], in0=PB[ob[n]], in1=xj[:, n * 512:(n + 1) * 512], op=ALU.add),
                            r=[B_PB[ob[n]]], w=[Bk["xb"]])
                    ss, sd, rs = sm[:, 0:1], sm[:, 1:2], sm[:, 2:3]
                    s.op("act", lambda e, xj=xj: e.activation(out=d["junk"], in_=xj, func=AF.Square, accum_out=ss),
                         r=[Bk["xb"]], w=[Bk["junk"], Bk["ss"]])
                    rstd_ops(ss, 1, 1.0 / 1024, sd, rs, Bk["ss"], Bk["sd"], Bk["rs"])
                    s.op("dve", lambda e, xj=xj: e.scalar_tensor_tensor(out=d["hb"], in0=xj, scalar=rs, in1=g2,
                                                                          op0=ALU.mult, op1=ALU.mult),
                         r=[Bk["xb"], Bk["rs"], Bg], w=[Bk["hb"]])
                    yield
                    hst3 = d["hst"].rearrange("p (c s) -> p c s", c=8)
                    transposes([d["hb"][:, c * 128:(c + 1) * 128] for c in range(8)], Bk["hb"], TBK[j % 2],
                               hst3[:, :, j * 128:(j + 1) * 128], Bk["hst"], "act")
                    yield
                s.dma("sp", x1d[b * 512:(b + 1) * 512, :].rearrange("(j p) n -> p j n", p=128),
                      d["xb"].rearrange("p (j n) -> p j n", j=4), r=[Bk["xb"]], w=[B_x1d], key="d_C1o%d" % k)
                s.dma("sp", h2T[:, :, cols].rearrange("c p s -> p c s"), d["hst"].rearrange("p (c s) -> p c s", c=8),
                      r=[Bk["hst"]], w=[B_h2T], key="d_C1p%d" % k)
                yield

            interleave([block(b) for b in range(NB)], W)
            s.barrier()
            A.reset(m0)

        def phase_C2(li, dst, B_dst):
            m0 = A.mark()
            wd = A.bf16(22 * 1024)
            Bwd = Buf("C2wd")
            load_w_bf16(wd, 22, 1024, w_down[li], Bwd, "d_C2wd")
            NR = 4
            ring = [A.bf16(2 * 8 * 128) for _ in range(NR)]
            Bring = [Buf("C2r%d" % i) for i in range(NR)]
            hT_ = [A.bf16(8 * 512) for _ in range(2)]
            xb_ = [A.f32(4 * 1024) for _ in range(2)]
            BhT = [Buf("C2h0"), Buf("C2h1")]
            Bxb = [Buf("C2x0"), Buf("C2x1")]
            actT = A.bf16(22 * 512)
            BactT = Buf("C2act")
            sg = [A.f32(512) for _ in range(2)]
            Bsg = [Buf("C2sg0"), Buf("C2sg1")]
            GB = ((0, 1), (2, 3))
            YB = (4, 5, 6, 7)
            chunks = [(b, f) for b in range(NB) for f in range(22)]

            def load_chunk(ci):
                b, f = chunks[ci]
                r_ = ci % NR
                s.dma("sp", ring[r_], wgub[li, f].rearrange("p g k c -> p (g k c)"), r=[B_wgub[li]], w=[Bring[r_]],
                      key="d_C2r%d" % r_)

            def load_block(b):
                k = b % 2
                cols = slice(b * 512, (b + 1) * 512)
                s.dma("sp", hT_[k].rearrange("p (c s) -> p c s", c=8), h2T[:, :, cols].rearrange("c p s -> p c s"),
                      r=[B_h2T], w=[BhT[k]], key="d_C2h%d" % k)
                s.dma("sp", xb_[k].rearrange("p (j n) -> p j n", j=4),
                      x1d[b * 512:(b + 1) * 512, :].rearrange("(j p) n -> p j n", p=128),
                      r=[B_x1d], w=[Bxb[k]], key="d_C2x%d" % k)

            load_block(0)
            for ci in range(min(NR - 1, len(chunks))):
                load_chunk(ci)
            yi = 0
            for b in range(NB):
                k = b % 2
                if b + 1 < NB:
                    load_block(b + 1)
                for f in range(22):
                    ci = b * 22 + f
                    if ci + NR - 1 < len(chunks):
                        load_chunk(ci + NR - 1)
                    rg = ring[ci % NR]
                    gb = GB[f % 2]
                    for gu in range(2):
                        for kc in range(8):
                            s.op("pe", lambda e, gu=gu, kc=kc, rg=rg, gb=gb, k=k: e.matmul(
                                PB[gb[gu]], rg[:, gu * 1024 + kc * 128:gu * 1024 + (kc + 1) * 128],
                                hT_[k][:, kc * 512:(kc + 1) * 512], start=(kc == 0), stop=(kc == 7)),
                                r=[Bring[ci % NR], BhT[k]], w=[B_PB[gb[gu]]])
                    sgt = sg[f % 2]
                    s.op("act", lambda e, gb=gb, sgt=sgt: e.activation(out=sgt, in_=PB[gb[0]], func=AF.Silu),
                         r=[B_PB[gb[0]]], w=[Bsg[f % 2]])
                    s.op("dve", lambda e, gb=gb, sgt=sgt, f=f: e.tensor_tensor(
                        out=actT[:, f * 512:(f + 1) * 512], in0=PB[gb[1]], in1=sgt, op=ALU.mult),
                        r=[B_PB[gb[1]], Bsg[f % 2]], w=[BactT])
                for j in range(4):
                    xj = xb_[k][:, j * 1024:(j + 1) * 1024]
                    for n in range(2):
                        yb = YB[yi % 4]
                        yi += 1
                        for f in range(22):
                            s.op("pe", lambda e, f=f, j=j, n=n, yb=yb: e.matmul(
                                PB[yb], actT[:, f * 512 + j * 128:f * 512 + (j + 1) * 128],
                                wd[:, f * 1024 + n * 512:f * 1024 + (n + 1) * 512],
                                start=(f == 0), stop=(f == 21)), r=[BactT, Bwd], w=[B_PB[yb]])
                        s.op("dve", lambda e, n=n, yb=yb, xj=xj: e.tensor_tensor(
                            out=xj[:, n * 512:(n + 1) * 512], in0=PB[yb], in1=xj[:, n * 512:(n + 1) * 512], op=ALU.add),
                            r=[B_PB[yb]], w=[Bxb[k]])
                s.dma("pool", dst[b * 512:(b + 1) * 512, :].rearrange("(j p) n -> p j n", p=128),
                      xb_[k].rearrange("p (j n) -> p j n", j=4), r=[Bxb[k]], w=[B_dst], key="d_C2o%d" % k)
            s.barrier()
            A.reset(m0)

        B_x = Buf("x", True)
        B_out = Buf("out", True)
        plist = [lambda: phase_M(), lambda: phase_A_mla(x, B_x), lambda: phase_B(0, True),
                 lambda: phase_C1(0, x, B_x), lambda: phase_C2(0, x2d, B_x2d),
                 lambda: phase_A_gqa(x2d, B_x2d), lambda: phase_B(1, False),
                 lambda: phase_C1(1, x2d, B_x2d), lambda: phase_C2(1, out, B_out)]
        for pi, pf in enumerate(plist):
            if pi < stop_after:
                pf()
        s.emit_all()
    return nc


def rope_tables(S, dim):
    rows = S // 64
    row = np.repeat(np.arange(rows, dtype=np.float64), 64)
    col = np.tile(np.arange(64, dtype=np.float64), rows)
    axis_dim = dim // 2
    inv = 10000.0 ** (-np.arange(0, axis_dim, 2, dtype=np.float64) / axis_dim)
    inv = inv.astype(np.float32).astype(np.float64)
    ar = (row[:, None].astype(np.float32) * inv.astype(np.float32)[None, :]).astype(np.float64)
    ac = (col[:, None].astype(np.float32) * inv.astype(np.float32)[None, :]).astype(np.float64)
    cr, sr, cc, sc = np.cos(ar), np.sin(ar), np.cos(ac), np.sin(ac)
    cos = np.concatenate([cr, cr, cc, cc], axis=1).astype(np.float32)
    sin = np.concatenate([-sr, sr, -sc, sc], axis=1).astype(np.float32)
    return np.ascontiguousarray(cos), np.ascontiguousarray(sin)


_CACHE = {}


def run(inputs, S, n_cores):
    if S not in _CACHE:
        _CACHE[S] = build(S)
    nc = _CACHE[S]
    cosg, sing = rope_tables(S, 128)
    cosm, sinm = rope_tables(S, 64)
    shared = {k: np.ascontiguousarray(np.asarray(v, dtype=np.float32)) for k, v in inputs.items()
              if k not in ("x", "mem")}
    shared.update(cosg=cosg, sing=sing, cosm=cosm, sinm=sinm)
    in_maps = []
    for c in range(n_cores):
        m = dict(shared)
        m["x"] = np.ascontiguousarray(np.asarray(inputs["x"][c], dtype=np.float32))
        m["mem"] = np.ascontiguousarray(np.asarray(inputs["mem"][c], dtype=np.float32))
        in_maps.append(m)
    res = run_bass_kernel_spmd(nc, in_maps, core_ids=list(range(n_cores)))
    return np.stack([np.asarray(r["out"]) for r in res.results], axis=0).astype(np.float32)


def kernel(**inputs):
    S = inputs["x"].shape[1]
    return run(inputs, S, inputs["x"].shape[0])
```

```python
import numpy as np
from contextlib import ExitStack
import concourse.bass as bass
import concourse.mybir as mybir
from concourse.bass_utils import run_bass_kernel_spmd

F32 = mybir.dt.float32
BF16 = mybir.dt.bfloat16
AF = mybir.ActivationFunctionType
ALU = mybir.AluOpType
AX = mybir.AxisListType

D = 1024
DFF = 2816
EPS = 1e-6
ARENA_WORDS = 50000


class Tok:
    __slots__ = ("key", "val", "needed")

    def __init__(self, key, val=None):
        self.key = key
        self.val = val
        self.needed = False


class Buf:
    __slots__ = ("name", "writers", "readers", "ignore")

    def __init__(self, name, ignore=False):
        self.name = name
        self.writers = {}
        self.readers = {}
        self.ignore = ignore


class Sched:
    ENGS = ("pe", "act", "dve", "pool", "sp")

    def __init__(self, nc, stack):
        self.nc = nc
        self.stack = stack
        self.ops = {e: [] for e in self.ENGS}
        self.sems = {}
        self.cnt = {}
        self.last = {}
        for e in self.ENGS:
            self._mksem("c_" + e)

    def _mksem(self, key):
        if key not in self.sems:
            self.sems[key] = self.stack.enter_context(self.nc.semaphore(key))
            self.cnt[key] = 0
        return self.sems[key]

    def _deps(self, eng, r, w, deps):
        ds = {}

        def add(t):
            if t is None:
                return
            if eng == "pe" and t.key == "c_pe":
                return
            ds[id(t)] = t
        for b in r:
            if b.ignore:
                continue
            for t in b.writers.values():
                add(t)
        for b in w:
            if b.ignore:
                continue
            for t in b.writers.values():
                add(t)
            for t in b.readers.values():
                add(t)
        for t in deps:
            add(t)
        out = list(ds.values())
        for t in out:
            t.needed = True
        return out

    def op(self, eng, fn, r=(), w=(), deps=()):
        d = self._deps(eng, r, w, deps)
        key = "c_" + eng
        tok = Tok(key)
        for b in r:
            if not b.ignore:
                b.readers[key] = tok
        for b in w:
            if not b.ignore:
                b.writers[key] = tok
        self.ops[eng].append(("op", fn, d, tok))
        self.last[key] = tok
        return tok

    def dma(self, q, out, in_, r=(), w=(), key=None, deps=()):
        d = self._deps(q, r, w, deps)
        self._mksem(key)
        self.cnt[key] += 16
        tok = Tok(key, self.cnt[key])
        tok.needed = True
        for b in r:
            if not b.ignore:
                b.readers[key] = tok
        for b in w:
            if not b.ignore:
                b.writers[key] = tok
        self.ops[q].append(("dma", (out, in_), d, tok))
        self.last[key] = tok
        return tok

    def barrier(self):
        toks = list(self.last.values())
        for t in toks:
            t.needed = True
        for e in self.ENGS:
            self.ops[e].append(("wait", None, toks, None))

    def emit_all(self):
        for e in self.ENGS:
            key = "c_" + e
            c = 0
            for kind, fn, d, tok in self.ops[e]:
                if kind == "op" and tok.needed:
                    c += 1
                    tok.val = c
        sems = self.sems

        def run(e, eng):
            waited = {}
            for kind, fn, d, tok in self.ops[eng]:
                for t in d:
                    if t.key == "c_" + eng and eng == "pe":
                        continue
                    if waited.get(t.key, 0) >= t.val:
                        continue
                    waited[t.key] = t.val
                    e.wait_ge(sems[t.key], t.val)
                if kind == "op":
                    ins = fn(e)
                    if tok.needed:
                        ins.then_inc(sems[tok.key], 1)
                elif kind == "dma":
                    e.dma_start(out=fn[0], in_=fn[1]).then_inc(sems[tok.key], 16)
        with self.nc.Block() as block:
            @block.tensor
            def _(e):
                run(e, "pe")

            @block.scalar
            def _(e):
                run(e, "act")

            @block.vector
            def _(e):
                run(e, "dve")

            @block.gpsimd
            def _(e):
                run(e, "pool")

            @block.sync
            def _(e):
                run(e, "sp")


class Arena:
    def __init__(self, ap, words):
        self.ap = ap
        self.words = words
        self.off = 0

    def f32(self, n):
        a = self.ap[:, self.off:self.off + n]
        self.off += n
        assert self.off <= self.words, ("arena overflow", self.off)
        return a

    def bf16(self, n):
        assert n % 2 == 0
        return self.f32(n // 2).bitcast(BF16)

    def mark(self):
        return self.off

    def reset(self, m):
        self.off = m


def interleave(gens, width, on_done=None):
    gens = list(gens)
    active = []
    nxt = 0
    while active or nxt < len(gens):
        while len(active) < width and nxt < len(gens):
            active.append((nxt, gens[nxt]))
            nxt += 1
        still = []
        for idx, g in active:
            try:
                next(g)
                still.append((idx, g))
            except StopIteration:
                if on_done is not None:
                    on_done(idx)
        active = still


def bcast_row(ap_row, n):
    return bass.AP(ap_row.tensor, ap_row.offset, [[0, 128], [1, n]])


def build(S, debug=False, stop_after=99):
    NT = S // 128
    NB = S // 512
    nc = bass.Bass("TRN2", target_bir_lowering=False)

    def din(name, shape):
        return nc.dram_tensor(name, list(shape), F32, kind="ExternalInput").ap()

    def dscr(name, shape, dt=BF16):
        kind = "ExternalOutput" if (debug and name != "wgub") else "Internal"
        return nc.dram_tensor(name, list(shape), dt, kind=kind).ap()

    x = din("x", [S, D])
    mem = din("mem", [256, D])
    mem_norm = din("mem_norm", [D])
    norm_mix = din("norm_mix", [2, D])
    norm_ffn = din("norm_ffn", [2, D])
    w_out = din("w_out", [2, 1536, D])
    w_mem_kv = din("w_mem_kv", [2, D, 1024])
    memq_norm = din("memq_norm", [2, 128])
    memk_norm = din("memk_norm", [2, 128])
    w_gate_up = din("w_gate_up", [2, D, 2 * DFF])
    w_down = din("w_down", [2, DFF, D])
    mla_w_in = din("mla_w_in", [1, D, 1216])
    mla_q_a_norm = din("mla_q_a_norm", [1, 384])
    mla_w_q_b = din("mla_w_q_b", [1, 384, 1536])
    mla_kv_a_norm = din("mla_kv_a_norm", [1, 256])
    mla_w_kv_b = din("mla_w_kv_b", [1, 256, 2048])
    mla_q_norm = din("mla_q_norm", [1, 192])
    mla_k_norm = din("mla_k_norm", [1, 192])
    gqa_w_in = din("gqa_w_in", [1, D, 2048])
    gqa_q_norm = din("gqa_q_norm", [1, 128])
    gqa_k_norm = din("gqa_k_norm", [1, 128])
    cosg = din("cosg", [S, 128])
    sing = din("sing", [S, 128])
    cosm = din("cosm", [S, 64])
    sinm = din("sinm", [S, 64])
    out = nc.dram_tensor("out", [S, D], F32, kind="ExternalOutput").ap()

    wgub = dscr("wgub", [2, 22, 128, 2, 8, 128])
    QnT = dscr("QnT", [8, 128, S])
    QpT = dscr("QpT", [4, 128, S])
    KnT = dscr("KnT", [8, 128, S])
    KpT = dscr("KpT", [128, S])
    QmT = dscr("QmT", [4, 128, S])
    Vd = dscr("Vd", [8, 128, NT, 128])
    mixT = dscr("mixT", [12, 128, S])
    x1d = dscr("x1d", [S, D], F32)
    x2d = dscr("x2d", [S, D], F32)
    h2T = dscr("h2T", [8, 128, S])
    B_QnT, B_QpT, B_KnT, B_KpT, B_QmT, B_Vd = (Buf(n, True) for n in ("QnT", "QpT", "KnT", "KpT", "QmT", "Vd"))
    B_mixT, B_x1d, B_x2d, B_h2T = (Buf(n, True) for n in ("mixT", "x1d", "x2d", "h2T"))
    B_wgub = [Buf("wgub0", True), Buf("wgub1", True)]

    with ExitStack() as st:
        s = Sched(nc, st)
        arena_t = st.enter_context(nc.sbuf_tensor("arena", [128, ARENA_WORDS], F32))
        ps = st.enter_context(nc.psum_tensor("ps", [128, 4096], F32))
        A = Arena(arena_t, ARENA_WORDS)
        PB = [ps[:, i * 512:(i + 1) * 512] for i in range(8)]
        PBT = [p.bitcast(BF16) for p in PB]
        B_PB = [Buf("pb%d" % i) for i in range(8)]

        identf = A.f32(128)
        ident = A.bf16(128)
        ones = A.bf16(128)
        eps_t = A.f32(1)
        KmT = [A.bf16(4 * 256) for _ in range(2)]
        Vm = [A.bf16(2 * 512) for _ in range(2)]
        B_const = Buf("const")
        B_KmT = [Buf("KmT0"), Buf("KmT1")]
        B_Vm = [Buf("Vm0"), Buf("Vm1")]
        s.op("pool", lambda e: e.memset(identf, 0.0), w=[B_const])
        s.op("pool", lambda e: e.affine_select(out=identf, in_=identf, pattern=[[-1, 128]],
                                               compare_op=ALU.not_equal, fill=1.0, base=0,
                                               channel_multiplier=1), w=[B_const])
        s.op("pool", lambda e: e.tensor_copy(ident, identf), w=[B_const])
        s.op("pool", lambda e: e.memset(ones, 1.0), w=[B_const])
        s.op("pool", lambda e: e.memset(eps_t, EPS), w=[B_const])
        PERS = A.mark()

        def rstd_ops(ss, n_cols, inv_n, sd, rs, B_ss, B_sd, B_rs):
            s.op("act", lambda e: e.activation(out=sd, in_=ss, func=AF.Sqrt, scale=inv_n, bias=eps_t),
                 r=[B_ss, B_const], w=[B_sd])
            s.op("dve", lambda e: e.reciprocal(out=rs, in_=sd), r=[B_sd], w=[B_rs])

        def transposes(srcs, B_src, tb, dst, B_dst, evac):
            n = len(srcs)
            assert n <= 8
            for c, sp_ in enumerate(srcs):
                s.op("pe", lambda e, c=c, sp_=sp_: e.transpose(PBT[tb][:, c * 128:(c + 1) * 128], sp_, ident),
                     r=[B_src, B_const], w=[B_PB[tb]])
            src_ps = PBT[tb][:, 0:n * 128]
            if len(dst.shape) == 3:
                src_ps = src_ps.rearrange("p (h s) -> p h s", h=n)
            if evac == "act":
                s.op("act", lambda e: e.copy(dst, src_ps), r=[B_PB[tb]], w=[B_dst])
            else:
                s.op("dve", lambda e: e.tensor_copy(dst, src_ps), r=[B_PB[tb]], w=[B_dst])

        def load_w_bf16(dst_flat, kcn, ncols, src2d, Bw, key):
            for kc in range(kcn):
                s.dma("pool", dst_flat[:, kc * ncols:(kc + 1) * ncols], src2d[kc * 128:(kc + 1) * 128, :],
                      w=[Bw], key=key)

        def rope(eng_a, eng_b, xin, nh, hd, cos_t, sin_t, t1, t2, outb, B_in, B_tab, B_t1, B_t2, B_out):
            q = hd // 4
            xv = xin.rearrange("p (h a b c) -> p h a b c", h=nh, a=2, b=2, c=q)
            t2v = t2.rearrange("p (h a b c) -> p h a b c", h=nh, a=2, b=2, c=q)
            sv = sin_t.rearrange("p (a b c) -> p a b c", a=2, b=2, c=q)
            cb = cos_t.unsqueeze(1).to_broadcast([128, nh, hd])
            x3 = xin.rearrange("p (h d) -> p h d", h=nh)
            t13 = t1.rearrange("p (h d) -> p h d", h=nh)
            s.op(eng_a, lambda e: e.tensor_tensor(out=t13, in0=x3, in1=cb, op=ALU.mult),
                 r=[B_in, B_tab], w=[B_t1])
            for b in range(2):
                sb_ = sv[:, :, b, :].unsqueeze(1).to_broadcast([128, nh, 2, q])
                s.op(eng_b, lambda e, b=b, sb_=sb_: e.tensor_tensor(out=t2v[:, :, :, b, :], in0=xv[:, :, :, 1 - b, :],
                                                                  in1=sb_, op=ALU.mult),
                     r=[B_in, B_tab], w=[B_t2])
            s.op(eng_a, lambda e: e.tensor_tensor(out=outb, in0=t1, in1=t2, op=ALU.add),
                 r=[B_t1, B_t2], w=[B_out])

        def phase_M():
            m0 = A.mark()
            gmem = A.f32(1024)
            gk = [A.f32(128) for _ in range(2)]
            wm = A.bf16(8 * 1024)
            memt = [A.f32(1024) for _ in range(2)]
            junk = A.bf16(1024)
            hb = A.bf16(1024)
            hmT = [A.bf16(1024) for _ in range(2)]
            sq = A.f32(512)
            kf = A.f32(512)
            knb = A.bf16(512)
            small = A.f32(16)
            Bg, Bwm, Bjunk, Bhb, Bsq, Bkf, Bknb = (Buf(n) for n in ("Mg", "Mwm", "Mjunk", "Mhb", "Msq", "Mkf", "Mknb"))
            Bmem = [Buf("Mmem0"), Buf("Mmem1")]
            BhmT = [Buf("MhmT0"), Buf("MhmT1")]
            Bss, Bsd, Brs = Buf("Mss"), Buf("Msd"), Buf("Mrs")
            s.dma("sp", gmem, bcast_row(mem_norm, 1024), w=[Bg], key="d_Mg")
            for i in range(2):
                s.dma("sp", gk[i], bcast_row(memk_norm[i], 128), w=[Bg], key="d_Mg")
            for mt in range(2):
                s.dma("sp", memt[mt], mem[mt * 128:(mt + 1) * 128, :], w=[Bmem[mt]], key="d_Mmem%d" % mt)
            for mt in range(2):
                ss, sd, rs = small[:, 0:1], small[:, 1:2], small[:, 2:3]
                s.op("act", lambda e, mt=mt: e.activation(out=junk, in_=memt[mt], func=AF.Square, accum_out=ss),
                     r=[Bmem[mt]], w=[Bjunk, Bss])
                rstd_ops(ss, 1, 1.0 / 1024, sd, rs, Bss, Bsd, Brs)
                s.op("dve", lambda e, mt=mt: e.scalar_tensor_tensor(out=hb, in0=memt[mt], scalar=rs, in1=gmem,
                                                                     op0=ALU.mult, op1=ALU.mult),
                     r=[Bmem[mt], Brs, Bg], w=[Bhb])
                transposes([hb[:, c * 128:(c + 1) * 128] for c in range(8)], Bhb, 0, hmT[mt], BhmT[mt], "act")
            for i in range(2):
                load_w_bf16(wm, 8, 1024, w_mem_kv[i], Bwm, "d_Mwm")
                for mt in range(2):
                    for n in range(2):
                        for kc in range(8):
                            s.op("pe", lambda e, n=n, kc=kc, mt=mt: e.matmul(
                                PB[1 + n], hmT[mt][:, kc * 128:(kc + 1) * 128],
                                wm[:, kc * 1024 + n * 512:kc * 1024 + (n + 1) * 512],
                                start=(kc == 0), stop=(kc == 7)),
                                r=[BhmT[mt], Bwm], w=[B_PB[1 + n]])
                    ss4, sd4, rs4 = small[:, 4:8], small[:, 8:12], small[:, 12:16]
                    s.op("act", lambda e: e.activation(out=sq, in_=PB[1], func=AF.Square), r=[B_PB[1]], w=[Bsq])
                    s.op("dve", lambda e: e.tensor_reduce(out=ss4, in_=sq.rearrange("p (h d) -> p h d", h=4),
                                                          axis=AX.X, op=ALU.add), r=[Bsq], w=[Bss])
                    rstd_ops(ss4, 4, 1.0 / 128, sd4, rs4, Bss, Bsd, Brs)
                    s.op("dve", lambda e: e.tensor_tensor(out=kf.rearrange("p (h d) -> p h d", h=4),
                                                          in0=PB[1].rearrange("p (h d) -> p h d", h=4),
                                                          in1=rs4.unsqueeze(2).to_broadcast([128, 4, 128]), op=ALU.mult),
                         r=[B_PB[1], Brs], w=[Bkf])
                    s.op("pool", lambda e, i=i: e.tensor_tensor(out=knb.rearrange("p (h d) -> p h d", h=4),
                                                                 in0=kf.rearrange("p (h d) -> p h d", h=4),
                                                                 in1=gk[i].unsqueeze(1).to_broadcast([128, 4, 128]),
                                                                 op=ALU.mult),
                         r=[Bkf, Bg], w=[Bknb])
                    dstK = KmT[i].rearrange("p (h m) -> p h m", h=4)[:, :, mt * 128:(mt + 1) * 128]
                    for c in range(4):
                        s.op("pe", lambda e, c=c: e.transpose(PBT[0][:, c * 128:(c + 1) * 128],
                                                              knb[:, c * 128:(c + 1) * 128], ident),
                             r=[Bknb, B_const], w=[B_PB[0]])
                    s.op("act", lambda e, dstK=dstK: e.copy(dstK, PBT[0][:, 0:512].rearrange("p (h m) -> p h m", h=4)),
                         r=[B_PB[0]], w=[B_KmT[i]])
                    s.op("dve", lambda e, i=i, mt=mt: e.tensor_copy(Vm[i][:, mt * 512:(mt + 1) * 512], PB[2]),
                         r=[B_PB[2]], w=[B_Vm[i]])
            s.barrier()
            A.reset(m0)

        def phase_A_gqa(xsrc, B_xsrc):
            m0 = A.mark()
            W = 3
            win = A.bf16(8 * 2048)
            g1 = A.f32(1024)
            G14 = A.f32(14 * 128)
            gtmp = A.f32(3 * 128)
            Bw, Bg = Buf("A1w"), Buf("A1g")
            wsrc = gqa_w_in[0]
            for kc in range(8):
                rows = slice(kc * 128, (kc + 1) * 128)
                base = kc * 2048
                s.dma("pool", win[:, base:base + 1280], wsrc[rows, 0:1280], w=[Bw], key="d_A1w")
                s.dma("pool", win[:, base + 1280:base + 1792], wsrc[rows, 1536:2048], w=[Bw], key="d_A1w")
                s.dma("pool", win[:, base + 1792:base + 2048], wsrc[rows, 1280:1536], w=[Bw], key="d_A1w")
            s.dma("sp", g1, bcast_row(norm_mix[1], 1024), w=[Bg], key="d_A1g")
            s.dma("sp", gtmp[:, 0:128], bcast_row(gqa_q_norm[0], 128), w=[Bg], key="d_A1g")
            s.dma("sp", gtmp[:, 128:256], bcast_row(gqa_k_norm[0], 128), w=[Bg], key="d_A1g")
            s.dma("sp", gtmp[:, 256:384], bcast_row(memq_norm[1], 128), w=[Bg], key="d_A1g")
            for h in range(14):
                src = gtmp[:, 0:128] if h < 8 else (gtmp[:, 128:256] if h < 10 else gtmp[:, 256:384])
                s.op("pool", lambda e, h=h, src=src: e.tensor_copy(G14[:, h * 128:(h + 1) * 128], src),
                     r=[Bg], w=[Bg])
            sl = []
            for k in range(W):
                d = dict(
                    xt=A.f32(1024), cos=A.f32(128), sin=A.f32(128), junk=A.bf16(1024), hb=A.bf16(1024),
                    hT=A.bf16(1024), sq=A.f32(1792), qf=A.f32(1792), t1=A.f32(1280),
                    qb=A.bf16(1792), small=A.f32(48))
                d["t2"] = d["sq"][:, 0:1280]
                d["B"] = {n: Buf("A1%s%d" % (n, k)) for n in
                          ("xt", "tab", "junk", "hb", "hT", "sq", "qf", "t1", "t2", "qb", "ss", "sd", "rs",
                           "ss14", "sd14", "rs14")}
                d["B"]["t2"] = d["B"]["sq"]
                sl.append(d)
            stg = [A.bf16(14 * 512) for _ in range(2)]
            vst = [A.bf16(2 * 4 * 128) for _ in range(2)]
            Bstg = [Buf("A1stg0"), Buf("A1stg1")]
            Bvst = [Buf("A1vst0"), Buf("A1vst1")]
            TBK = (0, 1)
            PBK = (2, 3, 4, 5)

            def tile(t):
                k = t % W
                d = sl[k]
                Bk = d["B"]
                g = t // 4
                j = t % 4
                sg = stg[g % 2]
                rows = slice(t * 128, (t + 1) * 128)
                s.dma("sp", d["xt"], xsrc[rows, :], r=[B_xsrc], w=[Bk["xt"]], key="d_A1x%d" % k)
                s.dma("sp", d["cos"], cosg[rows, :], w=[Bk["tab"]], key="d_A1t%d" % k)
                s.dma("sp", d["sin"], sing[rows, :], w=[Bk["tab"]], key="d_A1t%d" % k)
                sm = d["small"]
                ss, sd, rs = sm[:, 0:1], sm[:, 1:2], sm[:, 2:3]
                ss14, sd14, rs14 = sm[:, 4:18], sm[:, 18:32], sm[:, 32:46]
                s.op("act", lambda e: e.activation(out=d["junk"], in_=d["xt"], func=AF.Square, accum_out=ss),
                     r=[Bk["xt"]], w=[Bk["junk"], Bk["ss"]])
                rstd_ops(ss, 1, 1.0 / 1024, sd, rs, Bk["ss"], Bk["sd"], Bk["rs"])
                s.op("dve", lambda e: e.scalar_tensor_tensor(out=d["hb"], in0=d["xt"], scalar=rs, in1=g1,
                                                             op0=ALU.mult, op1=ALU.mult),
                     r=[Bk["xt"], Bk["rs"], Bg], w=[Bk["hb"]])
                yield
                transposes([d["hb"][:, c * 128:(c + 1) * 128] for c in range(8)], Bk["hb"], TBK[0], d["hT"], Bk["hT"], "act")
                yield
                for n in range(4):
                    for kc in range(8):
                        s.op("pe", lambda e, n=n, kc=kc: e.matmul(
                            PB[PBK[n]], d["hT"][:, kc * 128:(kc + 1) * 128],
                            win[:, kc * 2048 + n * 512:kc * 2048 + (n + 1) * 512],
                            start=(kc == 0), stop=(kc == 7)), r=[Bk["hT"], Bw], w=[B_PB[PBK[n]]])
                for n in range(4):
                    wd = 512 if n < 3 else 256
                    s.op("act", lambda e, n=n, wd=wd: e.activation(out=d["sq"][:, n * 512:n * 512 + wd],
                                                                     in_=PB[PBK[n]][:, 0:wd], func=AF.Square),
                         r=[B_PB[PBK[n]]], w=[Bk["sq"]])
                s.op("dve", lambda e: e.tensor_reduce(out=ss14, in_=d["sq"].rearrange("p (h d) -> p h d", h=14),
                                                      axis=AX.X, op=ALU.add), r=[Bk["sq"]], w=[Bk["ss14"]])
                rstd_ops(ss14, 14, 1.0 / 128, sd14, rs14, Bk["ss14"], Bk["sd14"], Bk["rs14"])
                for n in range(4):
                    nh = 4 if n < 3 else 2
                    s.op("dve", lambda e, n=n, nh=nh: e.tensor_tensor(
                        out=d["qf"][:, n * 512:n * 512 + nh * 128].rearrange("p (h d) -> p h d", h=nh),
                        in0=PB[PBK[n]][:, 0:nh * 128].rearrange("p (h d) -> p h d", h=nh),
                        in1=rs14[:, n * 4:n * 4 + nh].unsqueeze(2).to_broadcast([128, nh, 128]), op=ALU.mult),
                        r=[B_PB[PBK[n]], Bk["rs14"]], w=[Bk["qf"]])
                vdst = vst[g % 2].rearrange("p (h j d) -> p h j d", h=2, j=4)[:, :, j, :]
                s.op("act", lambda e, vdst=vdst: e.copy(vdst, PB[PBK[3]][:, 256:512].rearrange("p (h d) -> p h d", h=2)),
                     r=[B_PB[PBK[3]]], w=[Bvst[g % 2]])
                yield
                s.op("pool", lambda e: e.tensor_tensor(out=d["qf"], in0=d["qf"], in1=G14, op=ALU.mult),
                     r=[Bg], w=[Bk["qf"]])
                rope("pool", "dve", d["qf"][:, 0:1280], 10, 128, d["cos"], d["sin"], d["t1"], d["t2"],
                     d["qb"][:, 0:1280], Bk["qf"], Bk["tab"], Bk["t1"], Bk["t2"], Bk["qb"])
                s.op("act", lambda e: e.copy(d["qb"][:, 1280:1792], d["qf"][:, 1280:1792]), r=[Bk["qf"]], w=[Bk["qb"]])
                yield
                sg3 = sg.rearrange("p (h s) -> p h s", h=14)
                transposes([d["qb"][:, h * 128:(h + 1) * 128] for h in range(8)], Bk["qb"], TBK[1],
                           sg3[:, 0:8, j * 128:(j + 1) * 128], Bstg[g % 2], "dve")
                yield
                transposes([d["qb"][:, h * 128:(h + 1) * 128] for h in range(8, 14)], Bk["qb"], TBK[0],
                           sg3[:, 8:14, j * 128:(j + 1) * 128], Bstg[g % 2], "act")
                yield

            def group_done(g):
                sg3 = stg[g % 2].rearrange("p (h s) -> p h s", h=14)
                cols = slice(g * 512, (g + 1) * 512)
                key = "d_A1st%d" % (g % 2)
                s.dma("sp", QnT[:, :, cols].rearrange("h d s -> d h s"), sg3[:, 0:8, :],
                      r=[Bstg[g % 2]], w=[B_QnT], key=key)
                s.dma("sp", KnT[0:2, :, cols].rearrange("h d s -> d h s"), sg3[:, 8:10, :],
                      r=[Bstg[g % 2]], w=[B_KnT], key=key)
                s.dma("sp", QmT[:, :, cols].rearrange("h d s -> d h s"), sg3[:, 10:14, :],
                      r=[Bstg[g % 2]], w=[B_QmT], key=key)
                s.dma("sp", Vd[0:2, :, g * 4:(g + 1) * 4, :].rearrange("h p t d -> p h t d"),
                      vst[g % 2].rearrange("p (h j d) -> p h j d", h=2, j=4),
                      r=[Bvst[g % 2]], w=[B_Vd], key="d_A1sv%d" % (g % 2))

            done = set()

            def on_done(idx):
                done.add(idx)
                g = idx // 4
                if all((g * 4 + jj) in done for jj in range(4)):
                    group_done(g)
            interleave([tile(t) for t in range(NT)], W, on_done)
            s.barrier()
            A.reset(m0)

        def phase_A_mla(xsrc, B_xsrc):
            m0 = A.mark()
            W = 2
            win = A.bf16(8 * 1216)
            wqb = A.bf16(3 * 1536)
            wkvb = A.bf16(2 * 2048)
            g0 = A.f32(1024)
            gqa_ = A.f32(384)
            gkva = A.f32(256)
            gqn = A.f32(192)
            gkn = A.f32(192)
            gmq = A.f32(128)
            Bw, Bg = Buf("A0w"), Buf("A0g")
            load_w_bf16(win, 8, 1216, mla_w_in[0], Bw, "d_A0w")
            for kc in range(3):
                rows = slice(kc * 128, (kc + 1) * 128)
                srcv = mla_w_q_b[0][rows, :].rearrange("p (h e) -> p h e", e=192)
                s.dma("pool", wqb[:, kc * 1536:kc * 1536 + 1024].rearrange("p (h d) -> p h d", d=128),
                      srcv[:, :, 0:128], w=[Bw], key="d_A0w")
                s.dma("pool", wqb[:, kc * 1536 + 1024:(kc + 1) * 1536].rearrange("p (h d) -> p h d", d=64),
                      srcv[:, :, 128:192], w=[Bw], key="d_A0w")
            for kc in range(2):
                rows = slice(kc * 128, (kc + 1) * 128)
                srcv = mla_w_kv_b[0][rows, :].rearrange("p (h e) -> p h e", e=256)
                s.dma("pool", wkvb[:, kc * 2048:kc * 2048 + 1024].rearrange("p (h d) -> p h d", d=128),
                      srcv[:, :, 0:128], w=[Bw], key="d_A0w")
                s.dma("pool", wkvb[:, kc * 2048 + 1024:(kc + 1) * 2048].rearrange("p (h d) -> p h d", d=128),
                      srcv[:, :, 128:256], w=[Bw], key="d_A0w")
            s.dma("sp", g0, bcast_row(norm_mix[0], 1024), w=[Bg], key="d_A0g")
            s.dma("sp", gqa_, bcast_row(mla_q_a_norm[0], 384), w=[Bg], key="d_A0g")
            s.dma("sp", gkva, bcast_row(mla_kv_a_norm[0], 256), w=[Bg], key="d_A0g")
            s.dma("sp", gqn, bcast_row(mla_q_norm[0], 192), w=[Bg], key="d_A0g")
            s.dma("sp", gkn, bcast_row(mla_k_norm[0], 192), w=[Bg], key="d_A0g")
            s.dma("sp", gmq, bcast_row(memq_norm[0], 128), w=[Bg], key="d_A0g")
            sl = []
            names = ("xt", "tab", "junk", "hb", "hT", "sq", "f1", "f2", "t1", "t2", "cqb", "cT", "kpf", "kpb",
                     "qmb", "qnb", "qpb", "knb", "ss", "sd", "rs", "ssA", "sdA", "rsA", "ssq", "sdq", "rsq")
            for k in range(W):
                d = dict(
                    xt=A.f32(1024), cos=A.f32(64), sin=A.f32(64), junk=A.bf16(1024), hb=A.bf16(1024),
                    hT=A.bf16(1024), sq=A.f32(1536), f1=A.f32(1024), f2=A.f32(512), t1=A.f32(512), t2=A.f32(512),
                    cqb=A.bf16(640), cT=A.bf16(640), kpf=A.f32(64), kpb=A.bf16(128), qmb=A.bf16(512),
                    qnb=A.bf16(1024), qpb=A.bf16(512), knb=A.bf16(1024), small=A.f32(80))
                d["B"] = {n: Buf("A0%s%d" % (n, k)) for n in names}
                sl.append(d)
            stg = [A.bf16(25 * 512) for _ in range(2)]
            vst = [A.bf16(8 * 4 * 128) for _ in range(2)]
            Bstg = [Buf("A0stg0"), Buf("A0stg1")]
            Bvst = [Buf("A0vst0"), Buf("A0vst1")]
            TB0, TB1 = 0, 1
            P0, P1, P2 = 2, 3, 4
            Q0, Q1, Q2 = 5, 6, 7
            KV = (2, 3, 4, 5)

            def tile(t):
                k = t % W
                d = sl[k]
                Bk = d["B"]
                g = t // 4
                j = t % 4
                sg3 = stg[g % 2].rearrange("p (h s) -> p h s", h=25)
                rows = slice(t * 128, (t + 1) * 128)
                s.dma("sp", d["xt"], xsrc[rows, :], r=[B_xsrc], w=[Bk["xt"]], key="d_A0x%d" % k)
                s.dma("sp", d["cos"], cosm[rows, :], w=[Bk["tab"]], key="d_A0t%d" % k)
                s.dma("sp", d["sin"], sinm[rows, :], w=[Bk["tab"]], key="d_A0t%d" % k)
                sm = d["small"]
                ss, sd, rs = sm[:, 0:1], sm[:, 1:2], sm[:, 2:3]
                ssA, sdA, rsA = sm[:, 4:11], sm[:, 12:19], sm[:, 20:27]
                ssq, sdq, rsq = sm[:, 28:52], sm[:, 52:76], None
                s.op("act", lambda e: e.activation(out=d["junk"], in_=d["xt"], func=AF.Square, accum_out=ss),
                     r=[Bk["xt"]], w=[Bk["junk"], Bk["ss"]])
                rstd_ops(ss, 1, 1.0 / 1024, sd, rs, Bk["ss"], Bk["sd"], Bk["rs"])
                s.op("dve", lambda e: e.scalar_tensor_tensor(out=d["hb"], in0=d["xt"], scalar=rs, in1=g0,
                                                             op0=ALU.mult, op1=ALU.mult),
                     r=[Bk["xt"], Bk["rs"], Bg], w=[Bk["hb"]])
                yield
                transposes([d["hb"][:, c * 128:(c + 1) * 128] for c in range(8)], Bk["hb"], TB0, d["hT"], Bk["hT"], "act")
                yield
                for (pb, c0, wd) in ((P0, 0, 384), (P1, 384, 320), (P2, 704, 512)):
                    for kc in range(8):
                        s.op("pe", lambda e, pb=pb, c0=c0, wd=wd, kc=kc: e.matmul(
                            PB[pb][:, 0:wd], d["hT"][:, kc * 128:(kc + 1) * 128],
                            win[:, kc * 1216 + c0:kc * 1216 + c0 + wd],
                            start=(kc == 0), stop=(kc == 7)), r=[Bk["hT"], Bw], w=[B_PB[pb]])
                sq = d["sq"]
                s.op("act", lambda e: e.activation(out=sq[:, 0:384], in_=PB[P0][:, 0:384], func=AF.Square,
                                                   accum_out=ssA[:, 0:1]), r=[B_PB[P0]], w=[Bk["sq"], Bk["ssA"]])
                s.op("act", lambda e: e.activation(out=sq[:, 384:640], in_=PB[P1][:, 0:256], func=AF.Square,
                                                   accum_out=ssA[:, 1:2]), r=[B_PB[P1]], w=[Bk["sq"], Bk["ssA"]])
                s.op("act", lambda e: e.activation(out=sq[:, 640:704], in_=PB[P1][:, 256:320], func=AF.Square,
                                                   accum_out=ssA[:, 2:3]), r=[B_PB[P1]], w=[Bk["sq"], Bk["ssA"]])
                s.op("act", lambda e: e.activation(out=sq[:, 704:1216], in_=PB[P2], func=AF.Square),
                     r=[B_PB[P2]], w=[Bk["sq"]])
                s.op("dve", lambda e: e.tensor_reduce(out=ssA[:, 3:7], in_=sq[:, 704:1216].rearrange("p (h d) -> p h d", h=4),
                                                      axis=AX.X, op=ALU.add), r=[Bk["sq"]], w=[Bk["ssA"]])
                for (c0, c1, inv) in ((0, 1, 1.0 / 384), (1, 2, 1.0 / 256), (2, 3, 1.0 / 64), (3, 7, 1.0 / 128)):
                    s.op("act", lambda e, c0=c0, c1=c1, inv=inv: e.activation(
                        out=sdA[:, c0:c1], in_=ssA[:, c0:c1], func=AF.Sqrt, scale=inv, bias=eps_t),
                        r=[Bk["ssA"], B_const], w=[Bk["sdA"]])
                s.op("dve", lambda e: e.reciprocal(out=rsA, in_=sdA), r=[Bk["sdA"]], w=[Bk["rsA"]])
                cqb = d["cqb"]
                s.op("dve", lambda e: e.scalar_tensor_tensor(out=cqb[:, 0:384], in0=PB[P0][:, 0:384], scalar=rsA[:, 0:1],
                                                             in1=gqa_, op0=ALU.mult, op1=ALU.mult),
                     r=[B_PB[P0], Bk["rsA"], Bg], w=[Bk["cqb"]])
                s.op("dve", lambda e: e.scalar_tensor_tensor(out=cqb[:, 384:640], in0=PB[P1][:, 0:256], scalar=rsA[:, 1:2],
                                                             in1=gkva, op0=ALU.mult, op1=ALU.mult),
                     r=[B_PB[P1], Bk["rsA"], Bg], w=[Bk["cqb"]])
                s.op("dve", lambda e: e.scalar_tensor_tensor(out=d["kpf"], in0=PB[P1][:, 256:320], scalar=rsA[:, 2:3],
                                                             in1=gkn[:, 128:192], op0=ALU.mult, op1=ALU.mult),
                     r=[B_PB[P1], Bk["rsA"], Bg], w=[Bk["kpf"]])
                s.op("dve", lambda e: e.tensor_tensor(out=d["f2"].rearrange("p (h d) -> p h d", h=4),
                                                      in0=PB[P2].rearrange("p (h d) -> p h d", h=4),
                                                      in1=rsA[:, 3:7].unsqueeze(2).to_broadcast([128, 4, 128]), op=ALU.mult),
                     r=[B_PB[P2], Bk["rsA"]], w=[Bk["f2"]])
                yield
                transposes([cqb[:, c * 128:(c + 1) * 128] for c in range(5)], Bk["cqb"], TB1, d["cT"], Bk["cT"], "act")
                s.op("pool", lambda e: e.tensor_tensor(out=d["qmb"].rearrange("p (h d) -> p h d", h=4),
                                                       in0=d["f2"].rearrange("p (h d) -> p h d", h=4),
                                                       in1=gmq.unsqueeze(1).to_broadcast([128, 4, 128]), op=ALU.mult),
                     r=[Bk["f2"], Bg], w=[Bk["qmb"]])
                rope("pool", "pool", d["kpf"], 1, 64, d["cos"], d["sin"], d["t1"][:, 0:64], d["t2"][:, 0:64],
                     d["kpb"][:, 0:64], Bk["kpf"], Bk["tab"], Bk["t1"], Bk["t2"], Bk["kpb"])
                s.op("pool", lambda e: e.tensor_copy(d["kpb"][:, 64:128], d["kpb"][:, 0:64]), w=[Bk["kpb"]])
                yield
                for n, qb_ in enumerate((Q0, Q1, Q2)):
                    for kc in range(3):
                        s.op("pe", lambda e, n=n, qb_=qb_, kc=kc: e.matmul(
                            PB[qb_], d["cT"][:, kc * 128:(kc + 1) * 128],
                            wqb[:, kc * 1536 + n * 512:kc * 1536 + (n + 1) * 512],
                            start=(kc == 0), stop=(kc == 2)), r=[Bk["cT"], Bw], w=[B_PB[qb_]])
                for n, qb_ in enumerate((Q0, Q1, Q2)):
                    s.op("act", lambda e, n=n, qb_=qb_: e.activation(out=sq[:, n * 512:(n + 1) * 512], in_=PB[qb_],
                                                                       func=AF.Square), r=[B_PB[qb_]], w=[Bk["sq"]])
                s.op("dve", lambda e: e.tensor_reduce(out=ssq[:, 0:8], in_=sq[:, 0:1024].rearrange("p (h d) -> p h d", h=8),
                                                      axis=AX.X, op=ALU.add), r=[Bk["sq"]], w=[Bk["ssq"]])
                s.op("dve", lambda e: e.tensor_reduce(out=ssq[:, 8:16], in_=sq[:, 1024:1536].rearrange("p (h d) -> p h d", h=8),
                                                      axis=AX.X, op=ALU.add), r=[Bk["sq"]], w=[Bk["ssq"]])
                s.op("act", lambda e: e.activation(out=sdq[:, 0:8], in_=ssq[:, 0:8], func=AF.Sqrt, scale=1.0 / 128,
                                                   bias=eps_t), r=[Bk["ssq"], B_const], w=[Bk["sdq"]])
                s.op("act", lambda e: e.activation(out=sdq[:, 8:16], in_=ssq[:, 8:16], func=AF.Sqrt, scale=1.0 / 64,
                                                   bias=eps_t), r=[Bk["ssq"], B_const], w=[Bk["sdq"]])
                rq = sm[:, 52:68]
                s.op("dve", lambda e: e.reciprocal(out=rq, in_=sdq[:, 0:16]), w=[Bk["sdq"]])
                for n, qb_ in enumerate((Q0, Q1)):
                    s.op("dve", lambda e, n=n, qb_=qb_: e.tensor_tensor(
                        out=d["f1"][:, n * 512:(n + 1) * 512].rearrange("p (h d) -> p h d", h=4),
                        in0=PB[qb_].rearrange("p (h d) -> p h d", h=4),
                        in1=rq[:, n * 4:(n + 1) * 4].unsqueeze(2).to_broadcast([128, 4, 128]), op=ALU.mult),
                        r=[B_PB[qb_], Bk["sdq"]], w=[Bk["f1"]])
                s.op("pool", lambda e: e.tensor_tensor(out=d["qnb"].rearrange("p (h d) -> p h d", h=8),
                                                       in0=d["f1"].rearrange("p (h d) -> p h d", h=8),
                                                       in1=gqn[:, 0:128].unsqueeze(1).to_broadcast([128, 8, 128]), op=ALU.mult),
                     r=[Bk["f1"], Bg], w=[Bk["qnb"]])
                s.op("dve", lambda e: e.tensor_tensor(out=d["f2"].rearrange("p (h d) -> p h d", h=8),
                                                      in0=PB[Q2].rearrange("p (h d) -> p h d", h=8),
                                                      in1=rq[:, 8:16].unsqueeze(2).to_broadcast([128, 8, 64]), op=ALU.mult),
                     r=[B_PB[Q2], Bk["sdq"]], w=[Bk["f2"]])
                s.op("pool", lambda e: e.tensor_tensor(out=d["f2"].rearrange("p (h d) -> p h d", h=8),
                                                       in0=d["f2"].rearrange("p (h d) -> p h d", h=8),
                                                       in1=gqn[:, 128:192].unsqueeze(1).to_broadcast([128, 8, 64]), op=ALU.mult),
                     r=[Bg], w=[Bk["f2"]])
                rope("pool", "dve", d["f2"], 8, 64, d["cos"], d["sin"], d["t1"], d["t2"], d["qpb"],
                     Bk["f2"], Bk["tab"], Bk["t1"], Bk["t2"], Bk["qpb"])
                yield
                transposes([d["qnb"][:, h * 128:(h + 1) * 128] for h in range(8)], Bk["qnb"], TB0,
                           sg3[:, 0:8, j * 128:(j + 1) * 128], Bstg[g % 2], "act")
                yield
                srcs = [d["qpb"][:, c * 128:(c + 1) * 128] for c in range(4)] + [d["kpb"]] + \
                       [d["qmb"][:, c * 128:(c + 1) * 128] for c in range(3)]
                for c, sp_ in enumerate(srcs):
                    s.op("pe", lambda e, c=c, sp_=sp_: e.transpose(PBT[TB1][:, c * 128:(c + 1) * 128], sp_, ident),
                         r=[Bk["qpb"], Bk["kpb"], Bk["qmb"], B_const], w=[B_PB[TB1]])
                s.op("dve", lambda e: e.tensor_copy(sg3[:, 16:24, j * 128:(j + 1) * 128],
                                                    PBT[TB1].rearrange("p (h s) -> p h s", h=8)),
                     r=[B_PB[TB1]], w=[Bstg[g % 2]])
                yield
                for n in range(4):
                    for kc in range(2):
                        s.op("pe", lambda e, n=n, kc=kc: e.matmul(
                            PB[KV[n]], d["cT"][:, (3 + kc) * 128:(4 + kc) * 128],
                            wkvb[:, kc * 2048 + n * 512:kc * 2048 + (n + 1) * 512],
                            start=(kc == 0), stop=(kc == 1)), r=[Bk["cT"], Bw], w=[B_PB[KV[n]]])
                for n in range(2):
                    s.op("act", lambda e, n=n: e.activation(out=sq[:, n * 512:(n + 1) * 512], in_=PB[KV[n]],
                                                              func=AF.Square), r=[B_PB[KV[n]]], w=[Bk["sq"]])
                s.op("dve", lambda e: e.tensor_reduce(out=ssq[:, 16:24], in_=sq[:, 0:1024].rearrange("p (h d) -> p h d", h=8),
                                                      axis=AX.X, op=ALU.add), r=[Bk["sq"]], w=[Bk["ssq"]])
                s.op("act", lambda e: e.activation(out=sdq[:, 16:24], in_=ssq[:, 16:24], func=AF.Sqrt, scale=1.0 / 128,
                                                   bias=eps_t), r=[Bk["ssq"], B_const], w=[Bk["sdq"]])
                rk = sm[:, 68:76]
                s.op("dve", lambda e: e.reciprocal(out=rk, in_=sdq[:, 16:24]), w=[Bk["sdq"]])
                for n in range(2):
                    s.op("dve", lambda e, n=n: e.tensor_tensor(
                        out=d["f1"][:, n * 512:(n + 1) * 512].rearrange("p (h d) -> p h d", h=4),
                        in0=PB[KV[n]].rearrange("p (h d) -> p h d", h=4),
                        in1=rk[:, n * 4:(n + 1) * 4].unsqueeze(2).to_broadcast([128, 4, 128]), op=ALU.mult),
                        r=[B_PB[KV[n]], Bk["sdq"]], w=[Bk["f1"]])
                s.op("pool", lambda e: e.tensor_tensor(out=d["knb"].rearrange("p (h d) -> p h d", h=8),
                                                       in0=d["f1"].rearrange("p (h d) -> p h d", h=8),
                                                       in1=gkn[:, 0:128].unsqueeze(1).to_broadcast([128, 8, 128]), op=ALU.mult),
                     r=[Bk["f1"], Bg], w=[Bk["knb"]])
                vv = vst[g % 2].rearrange("p (h j d) -> p h j d", h=8, j=4)
                for n in range(2):
                    s.op("act", lambda e, n=n: e.copy(vv[:, n * 4:(n + 1) * 4, j, :],
                                                      PB[KV[2 + n]].rearrange("p (h d) -> p h d", h=4)),
                         r=[B_PB[KV[2 + n]]], w=[Bvst[g % 2]])
                yield
                transposes([d["knb"][:, h * 128:(h + 1) * 128] for h in range(8)], Bk["knb"], TB0,
                           sg3[:, 8:16, j * 128:(j + 1) * 128], Bstg[g % 2], "act")
                s.op("pe", lambda e: e.transpose(PBT[TB1][:, 0:128], d["qmb"][:, 384:512], ident),
                     r=[Bk["qmb"], B_const], w=[B_PB[TB1]])
                s.op("dve", lambda e: e.tensor_copy(sg3[:, 24, j * 128:(j + 1) * 128], PBT[TB1][:, 0:128]),
                     r=[B_PB[TB1]], w=[Bstg[g % 2]])
                yield

            def group_done(g):
                sg3 = stg[g % 2].rearrange("p (h s) -> p h s", h=25)
                cols = slice(g * 512, (g + 1) * 512)
                key = "d_A0st%d" % (g % 2)
                Bs = Bstg[g % 2]
                s.dma("sp", QnT[:, :, cols].rearrange("h d s -> d h s"), sg3[:, 0:8, :], r=[Bs], w=[B_QnT], key=key)
                s.dma("sp", KnT[:, :, cols].rearrange("h d s -> d h s"), sg3[:, 8:16, :], r=[Bs], w=[B_KnT], key=key)
                s.dma("sp", QpT[:, :, cols].rearrange("h d s -> d h s"), sg3[:, 16:20, :], r=[Bs], w=[B_QpT], key=key)
                s.dma("sp", KpT[:, cols], sg3[:, 20, :], r=[Bs], w=[B_KpT], key=key)
                s.dma("sp", QmT[:, :, cols].rearrange("h d s -> d h s"), sg3[:, 21:25, :], r=[Bs], w=[B_QmT], key=key)
                s.dma("sp", Vd[:, :, g * 4:(g + 1) * 4, :].rearrange("h p t d -> p h t d"),
                      vst[g % 2].rearrange("p (h j d) -> p h j d", h=8, j=4), r=[Bvst[g % 2]], w=[B_Vd],
                      key="d_A0sv%d" % (g % 2))

            done = set()

            def on_done(idx):
                done.add(idx)
                g = idx // 4
                if all((g * 4 + jj) in done for jj in range(4)):
                    group_done(g)
            interleave([tile(t) for t in range(NT)], W, on_done)
            s.barrier()
            A.reset(m0)

        def phase_B(li, mla):
            m0 = A.mark()
            NS = 3
            NPT = 4
            OBK, DBK = 6, 7
            SPAIR = [ps[:, k * 1024:(k + 1) * 1024] for k in range(NS)]
            PT = [A.bf16(1024) for _ in range(NPT)]
            PSUMS = [A.bf16(512) for _ in range(4)]
            qn_t = [A.bf16(512) for _ in range(3)]
            qp_t = [A.bf16(512) for _ in range(3)] if mla else None
            rc = A.f32(512)
            ost = [A.bf16(512) for _ in range(2)]
            Bost = [Buf("Bost0"), Buf("Bost1")]
            Brc = Buf("Brc")
            for kc in range(8):
                for gu in range(2):
                    for fh in range(2):
                        f0 = fh * 11
                        src = w_gate_up[li, kc * 128:(kc + 1) * 128,
                                        gu * DFF + f0 * 128:gu * DFF + (f0 + 11) * 128]
                        src = src.rearrange("p (f c) -> p f c", c=128)
                        dst = wgub[li, f0:f0 + 11, :, gu, kc, :].rearrange("f p c -> p f c")
                        s.dma("pool", dst, src, w=[B_wgub[li]], key="d_wgu%d" % li)
            kt = [A.bf16(S) for _ in range(2)]
            vt = [A.bf16(S) for _ in range(2)]
            Bkv = [Buf("Bkv0"), Buf("Bkv1")]
            if mla:
                kpz = [A.bf16(S) for _ in range(2)]
                Bkp = Buf("Bkp")
                for z in range(2):
                    s.dma("sp", kpz[z], KpT, r=[B_KpT], w=[Bkp], key="d_Bkp")
                s.op("pool", lambda e: e.memset(kpz[0][64:128, :], 0.0), w=[Bkp])
                s.op("pool", lambda e: e.memset(kpz[1][0:64, :], 0.0), w=[Bkp])
            else:
                for h in range(2):
                    s.dma("sp", kt[h], KnT[h], r=[B_KnT], w=[Bkv[h]], key="d_Bkv%d" % h)
                    s.dma("sp", vt[h], Vd[h].rearrange("p t d -> p (t d)"), r=[B_Vd], w=[Bkv[h]], key="d_Bkv%d" % h)
            blocks = []
            for h in range(8):
                for qb in range(NB):
                    blocks.append(("main", h, qb))
            for h in range(4):
                for qb in range(NB):
                    blocks.append(("mem", h, qb))
            scale_main = (192.0 ** -0.5) if mla else (128.0 ** -0.5)
            scale_mem = 128.0 ** -0.5
            steps = []
            for n, (kind, h, qb) in enumerate(blocks):
                npair = (NT if kind == "main" else 2) // 2
                for i in range(npair):
                    steps.append((n, i, npair))
            G = len(steps)
            qtok, kvtok = {}, {}
            qk_tok = [None] * G
            exp_tok = [None] * G
            sum_tok = [None] * G
            pv_last_tok, norm_tok, last_qk_of_block = {}, {}, {}

            def issue_kv_load(h):
                b = h % 2
                deps = []
                if h >= 2:
                    deps.append(pv_last_tok[(h - 2) * NB + NB - 1])
                s.dma("sp", kt[b], KnT[h], r=[B_KnT], w=[], key="d_Bkv%d" % b, deps=deps)
                kvtok[h] = s.dma("sp", vt[b], Vd[h].rearrange("p t d -> p (t d)"), r=[B_Vd], w=[],
                                 key="d_Bkv%d" % b, deps=deps)

            def issue_q_load(n):
                kind, h, qb = blocks[n]
                b = n % 3
                deps = []
                if n >= 3:
                    deps.append(last_qk_of_block[n - 3])
                cols = slice(qb * 512, (qb + 1) * 512)
                if kind == "main":
                    t_ = s.dma("sp", qn_t[b], QnT[h][:, cols], r=[B_QnT], w=[], key="d_Bq%d" % b, deps=deps)
                    if mla:
                        t_ = s.dma("sp", qp_t[b], QpT[h // 2][:, cols], r=[B_QpT], w=[], key="d_Bq%d" % b, deps=deps)
                else:
                    t_ = s.dma("sp", qn_t[b], QmT[h][:, cols], r=[B_QmT], w=[], key="d_Bq%d" % b, deps=deps)
                qtok[n] = t_

            def do_qk(g):
                n, i, npair = steps[g]
                kind, h, qb = blocks[n]
                qt = qn_t[n % 3]
                deps = [qtok[n]]
                if g - NS >= 0:
                    deps.append(exp_tok[g - NS])
                tok = None
                for half in range(2):
                    kti = 2 * i + half
                    outp = SPAIR[g % NS][:, half * 512:(half + 1) * 512]
                    if kind == "main":
                        if mla:
                            kb = kt[h % 2]
                            deps.append(kvtok[h])
                        else:
                            kb = kt[h // 4]
                        lhs = kb[:, kti * 128:(kti + 1) * 128]
                    else:
                        lhs = KmT[li][:, h * 256 + kti * 128:h * 256 + (kti + 1) * 128]
                    if kind == "main" and mla:
                        s.op("pe", lambda e, outp=outp, lhs=lhs: e.matmul(outp, lhs, qt, start=True, stop=False), deps=deps)
                        lhs2 = kpz[h % 2][:, kti * 128:(kti + 1) * 128]
                        rhs2 = qp_t[n % 3]
                        tok = s.op("pe", lambda e, outp=outp, lhs2=lhs2, rhs2=rhs2: e.matmul(outp, lhs2, rhs2, start=False, stop=True),
                                   r=[Bkp])
                    else:
                        rr = [Bkv[0], Bkv[1]] if kind == "main" else [B_KmT[li]]
                        tok = s.op("pe", lambda e, outp=outp, lhs=lhs: e.matmul(outp, lhs, qt, start=True, stop=True),
                                   deps=deps, r=rr)
                qk_tok[g] = tok
                if i == npair - 1:
                    last_qk_of_block[n] = tok

            def do_exp(g):
                n, i, npair = steps[g]
                sc = scale_main if blocks[n][0] == "main" else scale_mem
                sp_ = SPAIR[g % NS]
                pt = PT[g % NPT]
                exp_tok[g] = s.op("act", lambda e: e.activation(out=pt, in_=sp_, func=AF.Exp, scale=sc),
                                  deps=[qk_tok[g]])
                pss = PSUMS[g % 4]
                sum_tok[g] = s.op("dve", lambda e: e.tensor_tensor(out=pss, in0=pt[:, 0:512], in1=pt[:, 512:1024], op=ALU.add),
                                  deps=[exp_tok[g]])

            def do_pv(g):
                n, i, npair = steps[g]
                kind, h, qb = blocks[n]
                pt = PT[g % NPT]
                deps = [exp_tok[g]]
                if i == 0 and n >= 1:
                    deps.append(norm_tok[n - 1])
                for half in range(2):
                    kti = 2 * i + half
                    if kind == "main":
                        vb = vt[h % 2] if mla else vt[h // 4]
                        lhs = vb[:, kti * 128:(kti + 1) * 128]
                    else:
                        lhs = Vm[li][:, kti * 512 + h * 128:kti * 512 + (h + 1) * 128]
                    s.op("pe", lambda e, lhs=lhs, half=half: e.matmul(
                        PB[OBK], lhs, pt[:, half * 512:(half + 1) * 512],
                        start=(i == 0 and half == 0), stop=(i == npair - 1 and half == 1)), deps=deps,
                        r=([B_Vm[li]] if kind == "mem" else []))
                pss = PSUMS[g % 4]
                tok = s.op("pe", lambda e: e.matmul(PB[DBK], ones, pss, start=(i == 0), stop=(i == npair - 1)),
                           r=[B_const], deps=[sum_tok[g]])
                if i == npair - 1:
                    pv_last_tok[n] = tok
                    do_norm(n)

            def do_norm(n):
                kind, h, qb = blocks[n]
                ob = n % 2
                chunk = h if kind == "main" else 8 + h
                s.op("dve", lambda e: e.reciprocal(out=rc, in_=PB[DBK]), deps=[pv_last_tok[n]], w=[Brc])
                tok = s.op("dve", lambda e: e.tensor_tensor(out=ost[ob], in0=PB[OBK], in1=rc, op=ALU.mult),
                           r=[Brc], w=[Bost[ob]])
                norm_tok[n] = tok
                s.dma("sp", mixT[chunk][:, qb * 512:(qb + 1) * 512], ost[ob], r=[Bost[ob]], w=[B_mixT],
                      key="d_Bo%d" % ob)

            if mla:
                issue_kv_load(0)
            for n in range(min(2, len(blocks))):
                issue_q_load(n)
            kv_issued = 1
            LA = 2
            for g in range(min(LA, G)):
                do_qk(g)
            for g in range(G):
                do_exp(g)
                n, i, npair = steps[g]
                kind, h, qb = blocks[n]
                if i == 0:
                    if n + 2 < len(blocks):
                        issue_q_load(n + 2)
                    if mla and kind == "main" and qb == 1 and h + 1 < 8 and kv_issued == h + 1:
                        issue_kv_load(h + 1)
                        kv_issued += 1
                if g + LA < G:
                    do_qk(g + LA)
                do_pv(g)
            s.barrier()
            A.reset(m0)

        def phase_C1(li, xsrc, B_xsrc):
            m0 = A.mark()
            W = 3
            wo = A.bf16(12 * 1024)
            g2 = A.f32(1024)
            Bw, Bg = Buf("C1w"), Buf("C1g")
            load_w_bf16(wo, 12, 1024, w_out[li], Bw, "d_C1w")
            s.dma("sp", g2, bcast_row(norm_ffn[li], 1024), w=[Bg], key="d_C1g")
            sl = []
            for k in range(W):
                d = dict(mx=A.bf16(12 * 512), xb=A.f32(4 * 1024), hst=A.bf16(8 * 512), junk=A.bf16(1024),
                         hb=A.bf16(1024), small=A.f32(8))
                d["B"] = {n: Buf("C1%s%d" % (n, k)) for n in ("mx", "xb", "hst", "junk", "hb", "ss", "sd", "rs")}
                sl.append(d)
            OBK = (1, 2, 3, 4)
            TBK = (0, 5)

            def block(b):
                k = b % W
                d = sl[k]
                Bk = d["B"]
                cols = slice(b * 512, (b + 1) * 512)
                s.dma("sp", d["mx"].rearrange("p (c s) -> p c s", c=12), mixT[:, :, cols].rearrange("c p s -> p c s"),
                      r=[B_mixT], w=[Bk["mx"]], key="d_C1m%d" % k)
                s.dma("sp", d["xb"].rearrange("p (j n) -> p j n", j=4),
                      xsrc[b * 512:(b + 1) * 512, :].rearrange("(j p) n -> p j n", p=128),
                      r=[B_xsrc], w=[Bk["xb"]], key="d_C1x%d" % k)
                yield
                sm = d["small"]
                for j in range(4):
                    ob = (OBK[0], OBK[1]) if j % 2 == 0 else (OBK[2], OBK[3])
                    for n in range(2):
                        for c in range(12):
                            s.op("pe", lambda e, n=n, c=c, j=j, ob=ob: e.matmul(
                                PB[ob[n]], d["mx"][:, c * 512 + j * 128:c * 512 + (j + 1) * 128],
                                wo[:, c * 1024 + n * 512:c * 1024 + (n + 1) * 512],
                                start=(c == 0), stop=(c == 11)), r=[Bk["mx"], Bw], w=[B_PB[ob[n]]])
                    xj = d["xb"][:, j * 1024:(j + 1) * 1024]
                    for n in range(2):
                        s.op("dve", lambda e, n=n, ob=ob, xj=xj: e.tensor_tensor(
                            out=xj[:, n * 512:(n + 1) * 512], in0=PB[ob[n]], in1=xj[:, n * 512:(n + 1) * 512], op=ALU.add),
                            r=[B_PB[ob[n]]], w=[Bk["xb"]])
                    ss, sd, rs = sm[:, 0:1], sm[:, 1:2], sm[:, 2:3]
                    s.op("act", lambda e, xj=xj: e.activation(out=d["junk"], in_=xj, func=AF.Square, accum_out=ss),
                         r=[Bk["xb"]], w=[Bk["junk"], Bk["ss"]])
                    rstd_ops(ss, 1, 1.0 / 1024, sd, rs, Bk["ss"], Bk["sd"], Bk["rs"])
                    s.op("dve", lambda e, xj=xj: e.scalar_tensor_tensor(out=d["hb"], in0=xj, scalar=rs, in1=g2,
                                                                          op0=ALU.mult, op1=ALU.mult),
                         r=[Bk["xb"], Bk["rs"], Bg], w=[Bk["hb"]])
                    yield
                    hst3 = d["hst"].rearrange("p (c s) -> p c s", c=8)
                    transposes([d["hb"][:, c * 128:(c + 1) * 128] for c in range(8)], Bk["hb"], TBK[j % 2],
                               hst3[:, :, j * 128:(j + 1) * 128], Bk["hst"], "act")
                    yield
                s.dma("sp", x1d[b * 512:(b + 1) * 512, :].rearrange("(j p) n -> p j n", p=128),
                      d["xb"].rearrange("p (j n) -> p j n", j=4), r=[Bk["xb"]], w=[B_x1d], key="d_C1o%d" % k)
                s.dma("sp", h2T[:, :, cols].rearrange("c p s -> p c s"), d["hst"].rearrange("p (c s) -> p c s", c=8),
                      r=[Bk["hst"]], w=[B_h2T], key="d_C1p%d" % k)
                yield

            interleave([block(b) for b in range(NB)], W)
            s.barrier()
            A.reset(m0)

        def phase_C2(li, dst, B_dst):
            m0 = A.mark()
            wd = A.bf16(22 * 1024)
            Bwd = Buf("C2wd")
            load_w_bf16(wd, 22, 1024, w_down[li], Bwd, "d_C2wd")
            NR = 4
            ring = [A.bf16(2 * 8 * 128) for _ in range(NR)]
            Bring = [Buf("C2r%d" % i) for i in range(NR)]
            hT_ = [A.bf16(8 * 512) for _ in range(2)]
            xb_ = [A.f32(4 * 1024) for _ in range(2)]
            BhT = [Buf("C2h0"), Buf("C2h1")]
            Bxb = [Buf("C2x0"), Buf("C2x1")]
            actT = A.bf16(22 * 512)
            BactT = Buf("C2act")
            sg = [A.f32(512) for _ in range(2)]
            Bsg = [Buf("C2sg0"), Buf("C2sg1")]
            GB = ((0, 1), (2, 3))
            YB = (4, 5, 6, 7)
            chunks = [(b, f) for b in range(NB) for f in range(22)]

            def load_chunk(ci):
                b, f = chunks[ci]
                r_ = ci % NR
                s.dma("sp", ring[r_], wgub[li, f].rearrange("p g k c -> p (g k c)"), r=[B_wgub[li]], w=[Bring[r_]],
                      key="d_C2r%d" % r_)

            def load_block(b):
                k = b % 2
                cols = slice(b * 512, (b + 1) * 512)
                s.dma("sp", hT_[k].rearrange("p (c s) -> p c s", c=8), h2T[:, :, cols].rearrange("c p s -> p c s"),
                      r=[B_h2T], w=[BhT[k]], key="d_C2h%d" % k)
                s.dma("sp", xb_[k].rearrange("p (j n) -> p j n", j=4),
                      x1d[b * 512:(b + 1) * 512, :].rearrange("(j p) n -> p j n", p=128),
                      r=[B_x1d], w=[Bxb[k]], key="d_C2x%d" % k)

            load_block(0)
            for ci in range(min(NR - 1, len(chunks))):
                load_chunk(ci)
            yi = 0
            for b in range(NB):
                k = b % 2
                if b + 1 < NB:
                    load_block(b + 1)
                for f in range(22):
                    ci = b * 22 + f
                    if ci + NR - 1 < len(chunks):
                        load_chunk(ci + NR - 1)
                    rg = ring[ci % NR]
                    gb = GB[f % 2]
                    for gu in range(2):
                        for kc in range(8):
                            s.op("pe", lambda e, gu=gu, kc=kc, rg=rg, gb=gb, k=k: e.matmul(
                                PB[gb[gu]], rg[:, gu * 1024 + kc * 128:gu * 1024 + (kc + 1) * 128],
                                hT_[k][:, kc * 512:(kc + 1) * 512], start=(kc == 0), stop=(kc == 7)),
                                r=[Bring[ci % NR], BhT[k]], w=[B_PB[gb[gu]]])
                    sgt = sg[f % 2]
                    s.op("act", lambda e, gb=gb, sgt=sgt: e.activation(out=sgt, in_=PB[gb[0]], func=AF.Silu),
                         r=[B_PB[gb[0]]], w=[Bsg[f % 2]])
                    s.op("dve", lambda e, gb=gb, sgt=sgt, f=f: e.tensor_tensor(
                        out=actT[:, f * 512:(f + 1) * 512], in0=PB[gb[1]], in1=sgt, op=ALU.mult),
                        r=[B_PB[gb[1]], Bsg[f % 2]], w=[BactT])
                for j in range(4):
                    xj = xb_[k][:, j * 1024:(j + 1) * 1024]
                    for n in range(2):
                        yb = YB[yi % 4]
                        yi += 1
                        for f in range(22):
                            s.op("pe", lambda e, f=f, j=j, n=n, yb=yb: e.matmul(
                                PB[yb], actT[:, f * 512 + j * 128:f * 512 + (j + 1) * 128],
                                wd[:, f * 1024 + n * 512:f * 1024 + (n + 1) * 512],
                                start=(f == 0), stop=(f == 21)), r=[BactT, Bwd], w=[B_PB[yb]])
                        s.op("dve", lambda e, n=n, yb=yb, xj=xj: e.tensor_tensor(
                            out=xj[:, n * 512:(n + 1) * 512], in0=PB[yb], in1=xj[:, n * 512:(n + 1) * 512], op=ALU.add),
                            r=[B_PB[yb]], w=[Bxb[k]])
                s.dma("pool", dst[b * 512:(b + 1) * 512, :].rearrange("(j p) n -> p j n", p=128),
                      xb_[k].rearrange("p (j n) -> p j n", j=4), r=[Bxb[k]], w=[B_dst], key="d_C2o%d" % k)
            s.barrier()
            A.reset(m0)

        B_x = Buf("x", True)
        B_out = Buf("out", True)
        plist = [lambda: phase_M(), lambda: phase_A_mla(x, B_x), lambda: phase_B(0, True),
                 lambda: phase_C1(0, x, B_x), lambda: phase_C2(0, x2d, B_x2d),
                 lambda: phase_A_gqa(x2d, B_x2d), lambda: phase_B(1, False),
                 lambda: phase_C1(1, x2d, B_x2d), lambda: phase_C2(1, out, B_out)]
        for pi, pf in enumerate(plist):
            if pi < stop_after:
                pf()
        s.emit_all()
    return nc


def rope_tables(S, dim):
    rows = S // 64
    row = np.repeat(np.arange(rows, dtype=np.float64), 64)
    col = np.tile(np.arange(64, dtype=np.float64), rows)
    axis_dim = dim // 2
    inv = 10000.0 ** (-np.arange(0, axis_dim, 2, dtype=np.float64) / axis_dim)
    inv = inv.astype(np.float32).astype(np.float64)
    ar = (row[:, None].astype(np.float32) * inv.astype(np.float32)[None, :]).astype(np.float64)
    ac = (col[:, None].astype(np.float32) * inv.astype(np.float32)[None, :]).astype(np.float64)
    cr, sr, cc, sc = np.cos(ar), np.sin(ar), np.cos(ac), np.sin(ac)
    cos = np.concatenate([cr, cr, cc, cc], axis=1).astype(np.float32)
    sin = np.concatenate([-sr, sr, -sc, sc], axis=1).astype(np.float32)
    return np.ascontiguousarray(cos), np.ascontiguousarray(sin)


_CACHE = {}


def run(inputs, S, n_cores):
    if S not in _CACHE:
        _CACHE[S] = build(S)
    nc = _CACHE[S]
    cosg, sing = rope_tables(S, 128)
    cosm, sinm = rope_tables(S, 64)
    shared = {k: np.ascontiguousarray(np.asarray(v, dtype=np.float32)) for k, v in inputs.items()
              if k not in ("x", "mem")}
    shared.update(cosg=cosg, sing=sing, cosm=cosm, sinm=sinm)
    in_maps = []
    for c in range(n_cores):
        m = dict(shared)
        m["x"] = np.ascontiguousarray(np.asarray(inputs["x"][c], dtype=np.float32))
        m["mem"] = np.ascontiguousarray(np.asarray(inputs["mem"][c], dtype=np.float32))
        in_maps.append(m)
    res = run_bass_kernel_spmd(nc, in_maps, core_ids=list(range(n_cores)))
    return np.stack([np.asarray(r["out"]) for r in res.results], axis=0).astype(np.float32)


def kernel(**inputs):
    S = inputs["x"].shape[1]
    return run(inputs, S, inputs["x"].shape[0])
```

```python
import numpy as np
from contextlib import ExitStack
import concourse.bass as bass
import concourse.mybir as mybir
from concourse.bass_utils import run_bass_kernel_spmd

F32 = mybir.dt.float32
BF16 = mybir.dt.bfloat16
AF = mybir.ActivationFunctionType
ALU = mybir.AluOpType
AX = mybir.AxisListType

D = 1024
DFF = 2816
EPS = 1e-6
ARENA_WORDS = 50000


class Tok:
    __slots__ = ("key", "val", "needed")

    def __init__(self, key, val=None):
        self.key = key
        self.val = val
        self.needed = False


class Buf:
    __slots__ = ("name", "writers", "readers", "ignore")

    def __init__(self, name, ignore=False):
        self.name = name
        self.writers = {}
        self.readers = {}
        self.ignore = ignore


class Sched:
    ENGS = ("pe", "act", "dve", "pool", "sp")

    def __init__(self, nc, stack):
        self.nc = nc
        self.stack = stack
        self.ops = {e: [] for e in self.ENGS}
        self.sems = {}
        self.cnt = {}
        self.last = {}
        for e in self.ENGS:
            self._mksem("c_" + e)

    def _mksem(self, key):
        if key not in self.sems:
            self.sems[key] = self.stack.enter_context(self.nc.semaphore(key))
            self.cnt[key] = 0
        return self.sems[key]

    def _deps(self, eng, r, w, deps):
        ds = {}

        def add(t):
            if t is None:
                return
            if eng == "pe" and t.key == "c_pe":
                return
            ds[id(t)] = t
        for b in r:
            if b.ignore:
                continue
            for t in b.writers.values():
                add(t)
        for b in w:
            if b.ignore:
                continue
            for t in b.writers.values():
                add(t)
            for t in b.readers.values():
                add(t)
        for t in deps:
            add(t)
        out = list(ds.values())
        for t in out:
            t.needed = True
        return out

    def op(self, eng, fn, r=(), w=(), deps=()):
        d = self._deps(eng, r, w, deps)
        key = "c_" + eng
        tok = Tok(key)
        for b in r:
            if not b.ignore:
                b.readers[key] = tok
        for b in w:
            if not b.ignore:
                b.writers[key] = tok
        self.ops[eng].append(("op", fn, d, tok))
        self.last[key] = tok
        return tok

    def dma(self, q, out, in_, r=(), w=(), key=None, deps=()):
        d = self._deps(q, r, w, deps)
        self._mksem(key)
        self.cnt[key] += 16
        tok = Tok(key, self.cnt[key])
        tok.needed = True
        for b in r:
            if not b.ignore:
                b.readers[key] = tok
        for b in w:
            if not b.ignore:
                b.writers[key] = tok
        self.ops[q].append(("dma", (out, in_), d, tok))
        self.last[key] = tok
        return tok

    def barrier(self):
        toks = list(self.last.values())
        for t in toks:
            t.needed = True
        for e in self.ENGS:
            self.ops[e].append(("wait", None, toks, None))

    def emit_all(self):
        for e in self.ENGS:
            key = "c_" + e
            c = 0
            for kind, fn, d, tok in self.ops[e]:
                if kind == "op" and tok.needed:
                    c += 1
                    tok.val = c
        sems = self.sems

        def run(e, eng):
            waited = {}
            for kind, fn, d, tok in self.ops[eng]:
                for t in d:
                    if t.key == "c_" + eng and eng == "pe":
                        continue
                    if waited.get(t.key, 0) >= t.val:
                        continue
                    waited[t.key] = t.val
                    e.wait_ge(sems[t.key], t.val)
                if kind == "op":
                    ins = fn(e)
                    if tok.needed:
                        ins.then_inc(sems[tok.key], 1)
                elif kind == "dma":
                    e.dma_start(out=fn[0], in_=fn[1]).then_inc(sems[tok.key], 16)
        with self.nc.Block() as block:
            @block.tensor
            def _(e):
                run(e, "pe")

            @block.scalar
            def _(e):
                run(e, "act")

            @block.vector
            def _(e):
                run(e, "dve")

            @block.gpsimd
            def _(e):
                run(e, "pool")

            @block.sync
            def _(e):
                run(e, "sp")


class Arena:
    def __init__(self, ap, words):
        self.ap = ap
        self.words = words
        self.off = 0

    def f32(self, n):
        a = self.ap[:, self.off:self.off + n]
        self.off += n
        assert self.off <= self.words, ("arena overflow", self.off)
        return a

    def bf16(self, n):
        assert n % 2 == 0
        return self.f32(n // 2).bitcast(BF16)

    def mark(self):
        return self.off

    def reset(self, m):
        self.off = m


def interleave(gens, width, on_done=None):
    gens = list(gens)
    active = []
    nxt = 0
    while active or nxt < len(gens):
        while len(active) < width and nxt < len(gens):
            active.append((nxt, gens[nxt]))
            nxt += 1
        still = []
        for idx, g in active:
            try:
                next(g)
                still.append((idx, g))
            except StopIteration:
                if on_done is not None:
                    on_done(idx)
        active = still


def bcast_row(ap_row, n):
    return bass.AP(ap_row.tensor, ap_row.offset, [[0, 128], [1, n]])


def build(S, debug=False, stop_after=99):
    NT = S // 128
    NB = S // 512
    nc = bass.Bass("TRN2", target_bir_lowering=False)

    def din(name, shape):
        return nc.dram_tensor(name, list(shape), F32, kind="ExternalInput").ap()

    def dscr(name, shape, dt=BF16):
        kind = "ExternalOutput" if (debug and name != "wgub") else "Internal"
        return nc.dram_tensor(name, list(shape), dt, kind=kind).ap()

    x = din("x", [S, D])
    mem = din("mem", [256, D])
    mem_norm = din("mem_norm", [D])
    norm_mix = din("norm_mix", [2, D])
    norm_ffn = din("norm_ffn", [2, D])
    w_out = din("w_out", [2, 1536, D])
    w_mem_kv = din("w_mem_kv", [2, D, 1024])
    memq_norm = din("memq_norm", [2, 128])
    memk_norm = din("memk_norm", [2, 128])
    w_gate_up = din("w_gate_up", [2, D, 2 * DFF])
    w_down = din("w_down", [2, DFF, D])
    mla_w_in = din("mla_w_in", [1, D, 1216])
    mla_q_a_norm = din("mla_q_a_norm", [1, 384])
    mla_w_q_b = din("mla_w_q_b", [1, 384, 1536])
    mla_kv_a_norm = din("mla_kv_a_norm", [1, 256])
    mla_w_kv_b = din("mla_w_kv_b", [1, 256, 2048])
    mla_q_norm = din("mla_q_norm", [1, 192])
    mla_k_norm = din("mla_k_norm", [1, 192])
    gqa_w_in = din("gqa_w_in", [1, D, 2048])
    gqa_q_norm = din("gqa_q_norm", [1, 128])
    gqa_k_norm = din("gqa_k_norm", [1, 128])
    cosg = din("cosg", [S, 128])
    sing = din("sing", [S, 128])
    cosm = din("cosm", [S, 64])
    sinm = din("sinm", [S, 64])
    out = nc.dram_tensor("out", [S, D], F32, kind="ExternalOutput").ap()

    wgub = dscr("wgub", [2, 22, 128, 2, 8, 128])
    QnT = dscr("QnT", [8, 128, S])
    QpT = dscr("QpT", [4, 128, S])
    KnT = dscr("KnT", [8, 128, S])
    KpT = dscr("KpT", [128, S])
    QmT = dscr("QmT", [4, 128, S])
    Vd = dscr("Vd", [8, 128, NT, 128])
    mixT = dscr("mixT", [12, 128, S])
    x1d = dscr("x1d", [S, D], F32)
    x2d = dscr("x2d", [S, D], F32)
    h2T = dscr("h2T", [8, 128, S])
    B_QnT, B_QpT, B_KnT, B_KpT, B_QmT, B_Vd = (Buf(n, True) for n in ("QnT", "QpT", "KnT", "KpT", "QmT", "Vd"))
    B_mixT, B_x1d, B_x2d, B_h2T = (Buf(n, True) for n in ("mixT", "x1d", "x2d", "h2T"))
    B_wgub = [Buf("wgub0", True), Buf("wgub1", True)]

    with ExitStack() as st:
        s = Sched(nc, st)
        arena_t = st.enter_context(nc.sbuf_tensor("arena", [128, ARENA_WORDS], F32))
        ps = st.enter_context(nc.psum_tensor("ps", [128, 4096], F32))
        A = Arena(arena_t, ARENA_WORDS)
        PB = [ps[:, i * 512:(i + 1) * 512] for i in range(8)]
        PBT = [p.bitcast(BF16) for p in PB]
        B_PB = [Buf("pb%d" % i) for i in range(8)]

        identf = A.f32(128)
        ident = A.bf16(128)
        ones = A.bf16(128)
        eps_t = A.f32(1)
        KmT = [A.bf16(4 * 256) for _ in range(2)]
        Vm = [A.bf16(2 * 512) for _ in range(2)]
        B_const = Buf("const")
        B_KmT = [Buf("KmT0"), Buf("KmT1")]
        B_Vm = [Buf("Vm0"), Buf("Vm1")]
        s.op("pool", lambda e: e.memset(identf, 0.0), w=[B_const])
        s.op("pool", lambda e: e.affine_select(out=identf, in_=identf, pattern=[[-1, 128]],
                                               compare_op=ALU.not_equal, fill=1.0, base=0,
                                               channel_multiplier=1), w=[B_const])
        s.op("pool", lambda e: e.tensor_copy(ident, identf), w=[B_const])
        s.op("pool", lambda e: e.memset(ones, 1.0), w=[B_const])
        s.op("pool", lambda e: e.memset(eps_t, EPS), w=[B_const])
        PERS = A.mark()

        def rstd_ops(ss, n_cols, inv_n, sd, rs, B_ss, B_sd, B_rs):
            s.op("act", lambda e: e.activation(out=sd, in_=ss, func=AF.Sqrt, scale=inv_n, bias=eps_t),
                 r=[B_ss, B_const], w=[B_sd])
            s.op("dve", lambda e: e.reciprocal(out=rs, in_=sd), r=[B_sd], w=[B_rs])

        def transposes(srcs, B_src, tb, dst, B_dst, evac):
            n = len(srcs)
            assert n <= 8
            for c, sp_ in enumerate(srcs):
                s.op("pe", lambda e, c=c, sp_=sp_: e.transpose(PBT[tb][:, c * 128:(c + 1) * 128], sp_, ident),
                     r=[B_src, B_const], w=[B_PB[tb]])
            src_ps = PBT[tb][:, 0:n * 128]
            if len(dst.shape) == 3:
                src_ps = src_ps.rearrange("p (h s) -> p h s", h=n)
            if evac == "act":
                s.op("act", lambda e: e.copy(dst, src_ps), r=[B_PB[tb]], w=[B_dst])
            else:
                s.op("dve", lambda e: e.tensor_copy(dst, src_ps), r=[B_PB[tb]], w=[B_dst])

        def load_w_bf16(dst_flat, kcn, ncols, src2d, Bw, key):
            for kc in range(kcn):
                s.dma("pool", dst_flat[:, kc * ncols:(kc + 1) * ncols], src2d[kc * 128:(kc + 1) * 128, :],
                      w=[Bw], key=key)

        def rope(eng_a, eng_b, xin, nh, hd, cos_t, sin_t, t1, t2, outb, B_in, B_tab, B_t1, B_t2, B_out):
            q = hd // 4
            xv = xin.rearrange("p (h a b c) -> p h a b c", h=nh, a=2, b=2, c=q)
            t2v = t2.rearrange("p (h a b c) -> p h a b c", h=nh, a=2, b=2, c=q)
            sv = sin_t.rearrange("p (a b c) -> p a b c", a=2, b=2, c=q)
            cb = cos_t.unsqueeze(1).to_broadcast([128, nh, hd])
            x3 = xin.rearrange("p (h d) -> p h d", h=nh)
            t13 = t1.rearrange("p (h d) -> p h d", h=nh)
            s.op(eng_a, lambda e: e.tensor_tensor(out=t13, in0=x3, in1=cb, op=ALU.mult),
                 r=[B_in, B_tab], w=[B_t1])
            for b in range(2):
                sb_ = sv[:, :, b, :].unsqueeze(1).to_broadcast([128, nh, 2, q])
                s.op(eng_b, lambda e, b=b, sb_=sb_: e.tensor_tensor(out=t2v[:, :, :, b, :], in0=xv[:, :, :, 1 - b, :],
                                                                  in1=sb_, op=ALU.mult),
                     r=[B_in, B_tab], w=[B_t2])
            s.op(eng_a, lambda e: e.tensor_tensor(out=outb, in0=t1, in1=t2, op=ALU.add),
                 r=[B_t1, B_t2], w=[B_out])

        def phase_M():
            m0 = A.mark()
            gmem = A.f32(1024)
            gk = [A.f32(128) for _ in range(2)]
            wm = A.bf16(8 * 1024)
            memt = [A.f32(1024) for _ in range(2)]
            junk = A.bf16(1024)
            hb = A.bf16(1024)
            hmT = [A.bf16(1024) for _ in range(2)]
            sq = A.f32(512)
            kf = A.f32(512)
            knb = A.bf16(512)
            small = A.f32(16)
            Bg, Bwm, Bjunk, Bhb, Bsq, Bkf, Bknb = (Buf(n) for n in ("Mg", "Mwm", "Mjunk", "Mhb", "Msq", "Mkf", "Mknb"))
            Bmem = [Buf("Mmem0"), Buf("Mmem1")]
            BhmT = [Buf("MhmT0"), Buf("MhmT1")]
            Bss, Bsd, Brs = Buf("Mss"), Buf("Msd"), Buf("Mrs")
            s.dma("sp", gmem, bcast_row(mem_norm, 1024), w=[Bg], key="d_Mg")
            for i in range(2):
                s.dma("sp", gk[i], bcast_row(memk_norm[i], 128), w=[Bg], key="d_Mg")
            for mt in range(2):
                s.dma("sp", memt[mt], mem[mt * 128:(mt + 1) * 128, :], w=[Bmem[mt]], key="d_Mmem%d" % mt)
            for mt in range(2):
                ss, sd, rs = small[:, 0:1], small[:, 1:2], small[:, 2:3]
                s.op("act", lambda e, mt=mt: e.activation(out=junk, in_=memt[mt], func=AF.Square, accum_out=ss),
                     r=[Bmem[mt]], w=[Bjunk, Bss])
                rstd_ops(ss, 1, 1.0 / 1024, sd, rs, Bss, Bsd, Brs)
                s.op("dve", lambda e, mt=mt: e.scalar_tensor_tensor(out=hb, in0=memt[mt], scalar=rs, in1=gmem,
                                                                     op0=ALU.mult, op1=ALU.mult),
                     r=[Bmem[mt], Brs, Bg], w=[Bhb])
                transposes([hb[:, c * 128:(c + 1) * 128] for c in range(8)], Bhb, 0, hmT[mt], BhmT[mt], "act")
            for i in range(2):
                load_w_bf16(wm, 8, 1024, w_mem_kv[i], Bwm, "d_Mwm")
                for mt in range(2):
                    for n in range(2):
                        for kc in range(8):
                            s.op("pe", lambda e, n=n, kc=kc, mt=mt: e.matmul(
                                PB[1 + n], hmT[mt][:, kc * 128:(kc + 1) * 128],
                                wm[:, kc * 1024 + n * 512:kc * 1024 + (n + 1) * 512],
                                start=(kc == 0), stop=(kc == 7)),
                                r=[BhmT[mt], Bwm], w=[B_PB[1 + n]])
                    ss4, sd4, rs4 = small[:, 4:8], small[:, 8:12], small[:, 12:16]
                    s.op("act", lambda e: e.activation(out=sq, in_=PB[1], func=AF.Square), r=[B_PB[1]], w=[Bsq])
                    s.op("dve", lambda e: e.tensor_reduce(out=ss4, in_=sq.rearrange("p (h d) -> p h d", h=4),
                                                          axis=AX.X, op=ALU.add), r=[Bsq], w=[Bss])
                    rstd_ops(ss4, 4, 1.0 / 128, sd4, rs4, Bss, Bsd, Brs)
                    s.op("dve", lambda e: e.tensor_tensor(out=kf.rearrange("p (h d) -> p h d", h=4),
                                                          in0=PB[1].rearrange("p (h d) -> p h d", h=4),
                                                          in1=rs4.unsqueeze(2).to_broadcast([128, 4, 128]), op=ALU.mult),
                         r=[B_PB[1], Brs], w=[Bkf])
                    s.op("pool", lambda e, i=i: e.tensor_tensor(out=knb.rearrange("p (h d) -> p h d", h=4),
                                                                 in0=kf.rearrange("p (h d) -> p h d", h=4),
                                                                 in1=gk[i].unsqueeze(1).to_broadcast([128, 4, 128]),
                                                                 op=ALU.mult),
                         r=[Bkf, Bg], w=[Bknb])
                    dstK = KmT[i].rearrange("p (h m) -> p h m", h=4)[:, :, mt * 128:(mt + 1) * 128]
                    for c in range(4):
                        s.op("pe", lambda e, c=c: e.transpose(PBT[0][:, c * 128:(c + 1) * 128],
                                                              knb[:, c * 128:(c + 1) * 128], ident),
                             r=[Bknb, B_const], w=[B_PB[0]])
                    s.op("act", lambda e, dstK=dstK: e.copy(dstK, PBT[0][:, 0:512].rearrange("p (h m) -> p h m", h=4)),
                         r=[B_PB[0]], w=[B_KmT[i]])
                    s.op("dve", lambda e, i=i, mt=mt: e.tensor_copy(Vm[i][:, mt * 512:(mt + 1) * 512], PB[2]),
                         r=[B_PB[2]], w=[B_Vm[i]])
            s.barrier()
            A.reset(m0)

        def phase_A_gqa(xsrc, B_xsrc):
            m0 = A.mark()
            W = 3
            win = A.bf16(8 * 2048)
            g1 = A.f32(1024)
            G14 = A.f32(14 * 128)
            gtmp = A.f32(3 * 128)
            Bw, Bg = Buf("A1w"), Buf("A1g")
            wsrc = gqa_w_in[0]
            for kc in range(8):
                rows = slice(kc * 128, (kc + 1) * 128)
                base = kc * 2048
                s.dma("pool", win[:, base:base + 1280], wsrc[rows, 0:1280], w=[Bw], key="d_A1w")
                s.dma("pool", win[:, base + 1280:base + 1792], wsrc[rows, 1536:2048], w=[Bw], key="d_A1w")
                s.dma("pool", win[:, base + 1792:base + 2048], wsrc[rows, 1280:1536], w=[Bw], key="d_A1w")
            s.dma("sp", g1, bcast_row(norm_mix[1], 1024), w=[Bg], key="d_A1g")
            s.dma("sp", gtmp[:, 0:128], bcast_row(gqa_q_norm[0], 128), w=[Bg], key="d_A1g")
            s.dma("sp", gtmp[:, 128:256], bcast_row(gqa_k_norm[0], 128), w=[Bg], key="d_A1g")
            s.dma("sp", gtmp[:, 256:384], bcast_row(memq_norm[1], 128), w=[Bg], key="d_A1g")
            for h in range(14):
                src = gtmp[:, 0:128] if h < 8 else (gtmp[:, 128:256] if h < 10 else gtmp[:, 256:384])
                s.op("pool", lambda e, h=h, src=src: e.tensor_copy(G14[:, h * 128:(h + 1) * 128], src),
                     r=[Bg], w=[Bg])
            sl = []
            for k in range(W):
                d = dict(
                    xt=A.f32(1024), cos=A.f32(128), sin=A.f32(128), junk=A.bf16(1024), hb=A.bf16(1024),
                    hT=A.bf16(1024), sq=A.f32(1792), qf=A.f32(1792), t1=A.f32(1280),
                    qb=A.bf16(1792), small=A.f32(48))
                d["t2"] = d["sq"][:, 0:1280]
                d["B"] = {n: Buf("A1%s%d" % (n, k)) for n in
                          ("xt", "tab", "junk", "hb", "hT", "sq", "qf", "t1", "t2", "qb", "ss", "sd", "rs",
                           "ss14", "sd14", "rs14")}
                d["B"]["t2"] = d["B"]["sq"]
                sl.append(d)
            stg = [A.bf16(14 * 512) for _ in range(2)]
            vst = [A.bf16(2 * 4 * 128) for _ in range(2)]
            Bstg = [Buf("A1stg0"), Buf("A1stg1")]
            Bvst = [Buf("A1vst0"), Buf("A1vst1")]
            TBK = (0, 1)
            PBK = (2, 3, 4, 5)

            def tile(t):
                k = t % W
                d = sl[k]
                Bk = d["B"]
                g = t // 4
                j = t % 4
                sg = stg[g % 2]
                rows = slice(t * 128, (t + 1) * 128)
                s.dma("sp", d["xt"], xsrc[rows, :], r=[B_xsrc], w=[Bk["xt"]], key="d_A1x%d" % k)
                s.dma("sp", d["cos"], cosg[rows, :], w=[Bk["tab"]], key="d_A1t%d" % k)
                s.dma("sp", d["sin"], sing[rows, :], w=[Bk["tab"]], key="d_A1t%d" % k)
                sm = d["small"]
                ss, sd, rs = sm[:, 0:1], sm[:, 1:2], sm[:, 2:3]
                ss14, sd14, rs14 = sm[:, 4:18], sm[:, 18:32], sm[:, 32:46]
                s.op("act", lambda e: e.activation(out=d["junk"], in_=d["xt"], func=AF.Square, accum_out=ss),
                     r=[Bk["xt"]], w=[Bk["junk"], Bk["ss"]])
                rstd_ops(ss, 1, 1.0 / 1024, sd, rs, Bk["ss"], Bk["sd"], Bk["rs"])
                s.op("dve", lambda e: e.scalar_tensor_tensor(out=d["hb"], in0=d["xt"], scalar=rs, in1=g1,
                                                             op0=ALU.mult, op1=ALU.mult),
                     r=[Bk["xt"], Bk["rs"], Bg], w=[Bk["hb"]])
                yield
                transposes([d["hb"][:, c * 128:(c + 1) * 128] for c in range(8)], Bk["hb"], TBK[0], d["hT"], Bk["hT"], "act")
                yield
                for n in range(4):
                    for kc in range(8):
                        s.op("pe", lambda e, n=n, kc=kc: e.matmul(
                            PB[PBK[n]], d["hT"][:, kc * 128:(kc + 1) * 128],
                            win[:, kc * 2048 + n * 512:kc * 2048 + (n + 1) * 512],
                            start=(kc == 0), stop=(kc == 7)), r=[Bk["hT"], Bw], w=[B_PB[PBK[n]]])
                for n in range(4):
                    wd = 512 if n < 3 else 256
                    s.op("act", lambda e, n=n, wd=wd: e.activation(out=d["sq"][:, n * 512:n * 512 + wd],
                                                                     in_=PB[PBK[n]][:, 0:wd], func=AF.Square),
                         r=[B_PB[PBK[n]]], w=[Bk["sq"]])
                s.op("dve", lambda e: e.tensor_reduce(out=ss14, in_=d["sq"].rearrange("p (h d) -> p h d", h=14),
                                                      axis=AX.X, op=ALU.add), r=[Bk["sq"]], w=[Bk["ss14"]])
                rstd_ops(ss14, 14, 1.0 / 128, sd14, rs14, Bk["ss14"], Bk["sd14"], Bk["rs14"])
                for n in range(4):
                    nh = 4 if n < 3 else 2
                    s.op("dve", lambda e, n=n, nh=nh: e.tensor_tensor(
                        out=d["qf"][:, n * 512:n * 512 + nh * 128].rearrange("p (h d) -> p h d", h=nh),
                        in0=PB[PBK[n]][:, 0:nh * 128].rearrange("p (h d) -> p h d", h=nh),
                        in1=rs14[:, n * 4:n * 4 + nh].unsqueeze(2).to_broadcast([128, nh, 128]), op=ALU.mult),
                        r=[B_PB[PBK[n]], Bk["rs14"]], w=[Bk["qf"]])
                vdst = vst[g % 2].rearrange("p (h j d) -> p h j d", h=2, j=4)[:, :, j, :]
                s.op("act", lambda e, vdst=vdst: e.copy(vdst, PB[PBK[3]][:, 256:512].rearrange("p (h d) -> p h d", h=2)),
                     r=[B_PB[PBK[3]]], w=[Bvst[g % 2]])
                yield
                s.op("pool", lambda e: e.tensor_tensor(out=d["qf"], in0=d["qf"], in1=G14, op=ALU.mult),
                     r=[Bg], w=[Bk["qf"]])
                rope("pool", "dve", d["qf"][:, 0:1280], 10, 128, d["cos"], d["sin"], d["t1"], d["t2"],
                     d["qb"][:, 0:1280], Bk["qf"], Bk["tab"], Bk["t1"], Bk["t2"], Bk["qb"])
                s.op("act", lambda e: e.copy(d["qb"][:, 1280:1792], d["qf"][:, 1280:1792]), r=[Bk["qf"]], w=[Bk["qb"]])
                yield
                sg3 = sg.rearrange("p (h s) -> p h s", h=14)
                transposes([d["qb"][:, h * 128:(h + 1) * 128] for h in range(8)], Bk["qb"], TBK[1],
                           sg3[:, 0:8, j * 128:(j + 1) * 128], Bstg[g % 2], "dve")
                yield
                transposes([d["qb"][:, h * 128:(h + 1) * 128] for h in range(8, 14)], Bk["qb"], TBK[0],
                           sg3[:, 8:14, j * 128:(j + 1) * 128], Bstg[g % 2], "act")
                yield

            def group_done(g):
                sg3 = stg[g % 2].rearrange("p (h s) -> p h s", h=14)
                cols = slice(g * 512, (g + 1) * 512)
                key = "d_A1st%d" % (g % 2)
                s.dma("sp", QnT[:, :, cols].rearrange("h d s -> d h s"), sg3[:, 0:8, :],
                      r=[Bstg[g % 2]], w=[B_QnT], key=key)
                s.dma("sp", KnT[0:2, :, cols].rearrange("h d s -> d h s"), sg3[:, 8:10, :],
                      r=[Bstg[g % 2]], w=[B_KnT], key=key)
                s.dma("sp", QmT[:, :, cols].rearrange("h d s -> d h s"), sg3[:, 10:14, :],
                      r=[Bstg[g % 2]], w=[B_QmT], key=key)
                s.dma("sp", Vd[0:2, :, g * 4:(g + 1) * 4, :].rearrange("h p t d -> p h t d"),
                      vst[g % 2].rearrange("p (h j d) -> p h j d", h=2, j=4),
                      r=[Bvst[g % 2]], w=[B_Vd], key="d_A1sv%d" % (g % 2))

            done = set()

            def on_done(idx):
                done.add(idx)
                g = idx // 4
                if all((g * 4 + jj) in done for jj in range(4)):
                    group_done(g)
            interleave([tile(t) for t in range(NT)], W, on_done)
            s.barrier()
            A.reset(m0)

        def phase_A_mla(xsrc, B_xsrc):
            m0 = A.mark()
            W = 2
            win = A.bf16(8 * 1216)
            wqb = A.bf16(3 * 1536)
            wkvb = A.bf16(2 * 2048)
            g0 = A.f32(1024)
            gqa_ = A.f32(384)
            gkva = A.f32(256)
            gqn = A.f32(192)
            gkn = A.f32(192)
            gmq = A.f32(128)
            Bw, Bg = Buf("A0w"), Buf("A0g")
            load_w_bf16(win, 8, 1216, mla_w_in[0], Bw, "d_A0w")
            for kc in range(3):
                rows = slice(kc * 128, (kc + 1) * 128)
                srcv = mla_w_q_b[0][rows, :].rearrange("p (h e) -> p h e", e=192)
                s.dma("pool", wqb[:, kc * 1536:kc * 1536 + 1024].rearrange("p (h d) -> p h d", d=128),
                      srcv[:, :, 0:128], w=[Bw], key="d_A0w")
                s.dma("pool", wqb[:, kc * 1536 + 1024:(kc + 1) * 1536].rearrange("p (h d) -> p h d", d=64),
                      srcv[:, :, 128:192], w=[Bw], key="d_A0w")
            for kc in range(2):
                rows = slice(kc * 128, (kc + 1) * 128)
                srcv = mla_w_kv_b[0][rows, :].rearrange("p (h e) -> p h e", e=256)
                s.dma("pool", wkvb[:, kc * 2048:kc * 2048 + 1024].rearrange("p (h d) -> p h d", d=128),
                      srcv[:, :, 0:128], w=[Bw], key="d_A0w")
                s.dma("pool", wkvb[:, kc * 2048 + 1024:(kc + 1) * 2048].rearrange("p (h d) -> p h d", d=128),
                      srcv[:, :, 128:256], w=[Bw], key="d_A0w")
            s.dma("sp", g0, bcast_row(norm_mix[0], 1024), w=[Bg], key="d_A0g")
            s.dma("sp", gqa_, bcast_row(mla_q_a_norm[0], 384), w=[Bg], key="d_A0g")
            s.dma("sp", gkva, bcast_row(mla_kv_a_norm[0], 256), w=[Bg], key="d_A0g")
            s.dma("sp", gqn, bcast_row(mla_q_norm[0], 192), w=[Bg], key="d_A0g")
            s.dma("sp", gkn, bcast_row(mla_k_norm[0], 192), w=[Bg], key="d_A0g")
            s.dma("sp", gmq, bcast_row(memq_norm[0], 128), w=[Bg], key="d_A0g")
            sl = []
            names = ("xt", "tab", "junk", "hb", "hT", "sq", "f1", "f2", "t1", "t2", "cqb", "cT", "kpf", "kpb",
                     "qmb", "qnb", "qpb", "knb", "ss", "sd", "rs", "ssA", "sdA", "rsA", "ssq", "sdq", "rsq")
            for k in range(W):
                d = dict(
                    xt=A.f32(1024), cos=A.f32(64), sin=A.f32(64), junk=A.bf16(1024), hb=A.bf16(1024),
                    hT=A.bf16(1024), sq=A.f32(1536), f1=A.f32(1024), f2=A.f32(512), t1=A.f32(512), t2=A.f32(512),
                    cqb=A.bf16(640), cT=A.bf16(640), kpf=A.f32(64), kpb=A.bf16(128), qmb=A.bf16(512),
                    qnb=A.bf16(1024), qpb=A.bf16(512), knb=A.bf16(1024), small=A.f32(80))
                d["B"] = {n: Buf("A0%s%d" % (n, k)) for n in names}
                sl.append(d)
            stg = [A.bf16(25 * 512) for _ in range(2)]
            vst = [A.bf16(8 * 4 * 128) for _ in range(2)]
            Bstg = [Buf("A0stg0"), Buf("A0stg1")]
            Bvst = [Buf("A0vst0"), Buf("A0vst1")]
            TB0, TB1 = 0, 1
            P0, P1, P2 = 2, 3, 4
            Q0, Q1, Q2 = 5, 6, 7
            KV = (2, 3, 4, 5)

            def tile(t):
                k = t % W
                d = sl[k]
                Bk = d["B"]
                g = t // 4
                j = t % 4
                sg3 = stg[g % 2].rearrange("p (h s) -> p h s", h=25)
                rows = slice(t * 128, (t + 1) * 128)
                s.dma("sp", d["xt"], xsrc[rows, :], r=[B_xsrc], w=[Bk["xt"]], key="d_A0x%d" % k)
                s.dma("sp", d["cos"], cosm[rows, :], w=[Bk["tab"]], key="d_A0t%d" % k)
                s.dma("sp", d["sin"], sinm[rows, :], w=[Bk["tab"]], key="d_A0t%d" % k)
                sm = d["small"]
                ss, sd, rs = sm[:, 0:1], sm[:, 1:2], sm[:, 2:3]
                ssA, sdA, rsA = sm[:, 4:11], sm[:, 12:19], sm[:, 20:27]
                ssq, sdq, rsq = sm[:, 28:52], sm[:, 52:76], None
                s.op("act", lambda e: e.activation(out=d["junk"], in_=d["xt"], func=AF.Square, accum_out=ss),
                     r=[Bk["xt"]], w=[Bk["junk"], Bk["ss"]])
                rstd_ops(ss, 1, 1.0 / 1024, sd, rs, Bk["ss"], Bk["sd"], Bk["rs"])
                s.op("dve", lambda e: e.scalar_tensor_tensor(out=d["hb"], in0=d["xt"], scalar=rs, in1=g0,
                                                             op0=ALU.mult, op1=ALU.mult),
                     r=[Bk["xt"], Bk["rs"], Bg], w=[Bk["hb"]])
                yield
                transposes([d["hb"][:, c * 128:(c + 1) * 128] for c in range(8)], Bk["hb"], TB0, d["hT"], Bk["hT"], "act")
                yield
                for (pb, c0, wd) in ((P0, 0, 384), (P1, 384, 320), (P2, 704, 512)):
                    for kc in range(8):
                        s.op("pe", lambda e, pb=pb, c0=c0, wd=wd, kc=kc: e.matmul(
                            PB[pb][:, 0:wd], d["hT"][:, kc * 128:(kc + 1) * 128],
                            win[:, kc * 1216 + c0:kc * 1216 + c0 + wd],
                            start=(kc == 0), stop=(kc == 7)), r=[Bk["hT"], Bw], w=[B_PB[pb]])
                sq = d["sq"]
                s.op("act", lambda e: e.activation(out=sq[:, 0:384], in_=PB[P0][:, 0:384], func=AF.Square,
                                                   accum_out=ssA[:, 0:1]), r=[B_PB[P0]], w=[Bk["sq"], Bk["ssA"]])
                s.op("act", lambda e: e.activation(out=sq[:, 384:640], in_=PB[P1][:, 0:256], func=AF.Square,
                                                   accum_out=ssA[:, 1:2]), r=[B_PB[P1]], w=[Bk["sq"], Bk["ssA"]])
                s.op("act", lambda e: e.activation(out=sq[:, 640:704], in_=PB[P1][:, 256:320], func=AF.Square,
                                                   accum_out=ssA[:, 2:3]), r=[B_PB[P1]], w=[Bk["sq"], Bk["ssA"]])
                s.op("act", lambda e: e.activation(out=sq[:, 704:1216], in_=PB[P2], func=AF.Square),
                     r=[B_PB[P2]], w=[Bk["sq"]])
                s.op("dve", lambda e: e.tensor_reduce(out=ssA[:, 3:7], in_=sq[:, 704:1216].rearrange("p (h d) -> p h d", h=4),
                                                      axis=AX.X, op=ALU.add), r=[Bk["sq"]], w=[Bk["ssA"]])
                for (c0, c1, inv) in ((0, 1, 1.0 / 384), (1, 2, 1.0 / 256), (2, 3, 1.0 / 64), (3, 7, 1.0 / 128)):
                    s.op("act", lambda e, c0=c0, c1=c1, inv=inv: e.activation(
                        out=sdA[:, c0:c1], in_=ssA[:, c0:c1], func=AF.Sqrt, scale=inv, bias=eps_t),
                        r=[Bk["ssA"], B_const], w=[Bk["sdA"]])
                s.op("dve", lambda e: e.reciprocal(out=rsA, in_=sdA), r=[Bk["sdA"]], w=[Bk["rsA"]])
                cqb = d["cqb"]
                s.op("dve", lambda e: e.scalar_tensor_tensor(out=cqb[:, 0:384], in0=PB[P0][:, 0:384], scalar=rsA[:, 0:1],
                                                             in1=gqa_, op0=ALU.mult, op1=ALU.mult),
                     r=[B_PB[P0], Bk["rsA"], Bg], w=[Bk["cqb"]])
                s.op("dve", lambda e: e.scalar_tensor_tensor(out=cqb[:, 384:640], in0=PB[P1][:, 0:256], scalar=rsA[:, 1:2],
                                                             in1=gkva, op0=ALU.mult, op1=ALU.mult),
                     r=[B_PB[P1], Bk["rsA"], Bg], w=[Bk["cqb"]])
                s.op("dve", lambda e: e.scalar_tensor_tensor(out=d["kpf"], in0=PB[P1][:, 256:320], scalar=rsA[:, 2:3],
                                                             in1=gkn[:, 128:192], op0=ALU.mult, op1=ALU.mult),
                     r=[B_PB[P1], Bk["rsA"], Bg], w=[Bk["kpf"]])
                s.op("dve", lambda e: e.tensor_tensor(out=d["f2"].rearrange("p (h d) -> p h d", h=4),
                                                      in0=PB[P2].rearrange("p (h d) -> p h d", h=4),
                                                      in1=rsA[:, 3:7].unsqueeze(2).to_broadcast([128, 4, 128]), op=ALU.mult),
                     r=[B_PB[P2], Bk["rsA"]], w=[Bk["f2"]])
                yield
                transposes([cqb[:, c * 128:(c + 1) * 128] for c in range(5)], Bk["cqb"], TB1, d["cT"], Bk["cT"], "act")
                s.op("pool", lambda e: e.tensor_tensor(out=d["qmb"].rearrange("p (h d) -> p h d", h=4),
                                                       in0=d["f2"].rearrange("p (h d) -> p h d", h=4),
                                                       in1=gmq.unsqueeze(1).to_broadcast([128, 4, 128]), op=ALU.mult),
                     r=[Bk["f2"], Bg], w=[Bk["qmb"]])
                rope("pool", "pool", d["kpf"], 1, 64, d["cos"], d["sin"], d["t1"][:, 0:64], d["t2"][:, 0:64],
                     d["kpb"][:, 0:64], Bk["kpf"], Bk["tab"], Bk["t1"], Bk["t2"], Bk["kpb"])
                s.op("pool", lambda e: e.tensor_copy(d["kpb"][:, 64:128], d["kpb"][:, 0:64]), w=[Bk["kpb"]])
                yield
                for n, qb_ in enumerate((Q0, Q1, Q2)):
                    for kc in range(3):
                        s.op("pe", lambda e, n=n, qb_=qb_, kc=kc: e.matmul(
                            PB[qb_], d["cT"][:, kc * 128:(kc + 1) * 128],
                            wqb[:, kc * 1536 + n * 512:kc * 1536 + (n + 1) * 512],
                            start=(kc == 0), stop=(kc == 2)), r=[Bk["cT"], Bw], w=[B_PB[qb_]])
                for n, qb_ in enumerate((Q0, Q1, Q2)):
                    s.op("act", lambda e, n=n, qb_=qb_: e.activation(out=sq[:, n * 512:(n + 1) * 512], in_=PB[qb_],
                                                                       func=AF.Square), r=[B_PB[qb_]], w=[Bk["sq"]])
                s.op("dve", lambda e: e.tensor_reduce(out=ssq[:, 0:8], in_=sq[:, 0:1024].rearrange("p (h d) -> p h d", h=8),
                                                      axis=AX.X, op=ALU.add), r=[Bk["sq"]], w=[Bk["ssq"]])
                s.op("dve", lambda e: e.tensor_reduce(out=ssq[:, 8:16], in_=sq[:, 1024:1536].rearrange("p (h d) -> p h d", h=8),
                                                      axis=AX.X, op=ALU.add), r=[Bk["sq"]], w=[Bk["ssq"]])
                s.op("act", lambda e: e.activation(out=sdq[:, 0:8], in_=ssq[:, 0:8], func=AF.Sqrt, scale=1.0 / 128,
                                                   bias=eps_t), r=[Bk["ssq"], B_const], w=[Bk["sdq"]])
                s.op("act", lambda e: e.activation(out=sdq[:, 8:16], in_=ssq[:, 8:16], func=AF.Sqrt, scale=1.0 / 64,
                                                   bias=eps_t), r=[Bk["ssq"], B_const], w=[Bk["sdq"]])
                rq = sm[:, 52:68]
                s.op("dve", lambda e: e.reciprocal(out=rq, in_=sdq[:, 0:16]), w=[Bk["sdq"]])
                for n, qb_ in enumerate((Q0, Q1)):
                    s.op("dve", lambda e, n=n, qb_=qb_: e.tensor_tensor(
                        out=d["f1"][:, n * 512:(n + 1) * 512].rearrange("p (h d) -> p h d", h=4),
                        in0=PB[qb_].rearrange("p (h d) -> p h d", h=4),
                        in1=rq[:, n * 4:(n + 1) * 4].unsqueeze(2).to_broadcast([128, 4, 128]), op=ALU.mult),
                        r=[B_PB[qb_], Bk["sdq"]], w=[Bk["f1"]])
                s.op("pool", lambda e: e.tensor_tensor(out=d["qnb"].rearrange("p (h d) -> p h d", h=8),
                                                       in0=d["f1"].rearrange("p (h d) -> p h d", h=8),
                                                       in1=gqn[:, 0:128].unsqueeze(1).to_broadcast([128, 8, 128]), op=ALU.mult),
                     r=[Bk["f1"], Bg], w=[Bk["qnb"]])
                s.op("dve", lambda e: e.tensor_tensor(out=d["f2"].rearrange("p (h d) -> p h d", h=8),
                                                      in0=PB[Q2].rearrange("p (h d) -> p h d", h=8),
                                                      in1=rq[:, 8:16].unsqueeze(2).to_broadcast([128, 8, 64]), op=ALU.mult),
                     r=[B_PB[Q2], Bk["sdq"]], w=[Bk["f2"]])
                s.op("pool", lambda e: e.tensor_tensor(out=d["f2"].rearrange("p (h d) -> p h d", h=8),
                                                       in0=d["f2"].rearrange("p (h d) -> p h d", h=8),
                                                       in1=gqn[:, 128:192].unsqueeze(1).to_broadcast([128, 8, 64]), op=ALU.mult),
                     r=[Bg], w=[Bk["f2"]])
                rope("pool", "dve", d["f2"], 8, 64, d["cos"], d["sin"], d["t1"], d["t2"], d["qpb"],
                     Bk["f2"], Bk["tab"], Bk["t1"], Bk["t2"], Bk["qpb"])
                yield
                transposes([d["qnb"][:, h * 128:(h + 1) * 128] for h in range(8)], Bk["qnb"], TB0,
                           sg3[:, 0:8, j * 128:(j + 1) * 128], Bstg[g % 2], "act")
                yield
                srcs = [d["qpb"][:, c * 128:(c + 1) * 128] for c in range(4)] + [d["kpb"]] + \
                       [d["qmb"][:, c * 128:(c + 1) * 128] for c in range(3)]
                for c, sp_ in enumerate(srcs):
                    s.op("pe", lambda e, c=c, sp_=sp_: e.transpose(PBT[TB1][:, c * 128:(c + 1) * 128], sp_, ident),
                         r=[Bk["qpb"], Bk["kpb"], Bk["qmb"], B_const], w=[B_PB[TB1]])
                s.op("dve", lambda e: e.tensor_copy(sg3[:, 16:24, j * 128:(j + 1) * 128],
                                                    PBT[TB1].rearrange("p (h s) -> p h s", h=8)),
                     r=[B_PB[TB1]], w=[Bstg[g % 2]])
                yield
                for n in range(4):
                    for kc in range(2):
                        s.op("pe", lambda e, n=n, kc=kc: e.matmul(
                            PB[KV[n]], d["cT"][:, (3 + kc) * 128:(4 + kc) * 128],
                            wkvb[:, kc * 2048 + n * 512:kc * 2048 + (n + 1) * 512],
                            start=(kc == 0), stop=(kc == 1)), r=[Bk["cT"], Bw], w=[B_PB[KV[n]]])
                for n in range(2):
                    s.op("act", lambda e, n=n: e.activation(out=sq[:, n * 512:(n + 1) * 512], in_=PB[KV[n]],
                                                              func=AF.Square), r=[B_PB[KV[n]]], w=[Bk["sq"]])
                s.op("dve", lambda e: e.tensor_reduce(out=ssq[:, 16:24], in_=sq[:, 0:1024].rearrange("p (h d) -> p h d", h=8),
                                                      axis=AX.X, op=ALU.add), r=[Bk["sq"]], w=[Bk["ssq"]])
                s.op("act", lambda e: e.activation(out=sdq[:, 16:24], in_=ssq[:, 16:24], func=AF.Sqrt, scale=1.0 / 128,
                                                   bias=eps_t), r=[Bk["ssq"], B_const], w=[Bk["sdq"]])
                rk = sm[:, 68:76]
                s.op("dve", lambda e: e.reciprocal(out=rk, in_=sdq[:, 16:24]), w=[Bk["sdq"]])
                for n in range(2):
                    s.op("dve", lambda e, n=n: e.tensor_tensor(
                        out=d["f1"][:, n * 512:(n + 1) * 512].rearrange("p (h d) -> p h d", h=4),
                        in0=PB[KV[n]].rearrange("p (h d) -> p h d", h=4),
                        in1=rk[:, n * 4:(n + 1) * 4].unsqueeze(2).to_broadcast([128, 4, 128]), op=ALU.mult),
                        r=[B_PB[KV[n]], Bk["sdq"]], w=[Bk["f1"]])
                s.op("pool", lambda e: e.tensor_tensor(out=d["knb"].rearrange("p (h d) -> p h d", h=8),
                                                       in0=d["f1"].rearrange("p (h d) -> p h d", h=8),
                                                       in1=gkn[:, 0:128].unsqueeze(1).to_broadcast([128, 8, 128]), op=ALU.mult),
                     r=[Bk["f1"], Bg], w=[Bk["knb"]])
                vv = vst[g % 2].rearrange("p (h j d) -> p h j d", h=8, j=4)
                for n in range(2):
                    s.op("act", lambda e, n=n: e.copy(vv[:, n * 4:(n + 1) * 4, j, :],
                                                      PB[KV[2 + n]].rearrange("p (h d) -> p h d", h=4)),
                         r=[B_PB[KV[2 + n]]], w=[Bvst[g % 2]])
                yield
                transposes([d["knb"][:, h * 128:(h + 1) * 128] for h in range(8)], Bk["knb"], TB0,
                           sg3[:, 8:16, j * 128:(j + 1) * 128], Bstg[g % 2], "act")
                s.op("pe", lambda e: e.transpose(PBT[TB1][:, 0:128], d["qmb"][:, 384:512], ident),
                     r=[Bk["qmb"], B_const], w=[B_PB[TB1]])
                s.op("dve", lambda e: e.tensor_copy(sg3[:, 24, j * 128:(j + 1) * 128], PBT[TB1][:, 0:128]),
                     r=[B_PB[TB1]], w=[Bstg[g % 2]])
                yield

            def group_done(g):
                sg3 = stg[g % 2].rearrange("p (h s) -> p h s", h=25)
                cols = slice(g * 512, (g + 1) * 512)
                key = "d_A0st%d" % (g % 2)
                Bs = Bstg[g % 2]
                s.dma("sp", QnT[:, :, cols].rearrange("h d s -> d h s"), sg3[:, 0:8, :], r=[Bs], w=[B_QnT], key=key)
                s.dma("sp", KnT[:, :, cols].rearrange("h d s -> d h s"), sg3[:, 8:16, :], r=[Bs], w=[B_KnT], key=key)
                s.dma("sp", QpT[:, :, cols].rearrange("h d s -> d h s"), sg3[:, 16:20, :], r=[Bs], w=[B_QpT], key=key)
                s.dma("sp", KpT[:, cols], sg3[:, 20, :], r=[Bs], w=[B_KpT], key=key)
                s.dma("sp", QmT[:, :, cols].rearrange("h d s -> d h s"), sg3[:, 21:25, :], r=[Bs], w=[B_QmT], key=key)
                s.dma("sp", Vd[:, :, g * 4:(g + 1) * 4, :].rearrange("h p t d -> p h t d"),
                      vst[g % 2].rearrange("p (h j d) -> p h j d", h=8, j=4), r=[Bvst[g % 2]], w=[B_Vd],
                      key="d_A0sv%d" % (g % 2))

            done = set()

            def on_done(idx):
                done.add(idx)
                g = idx // 4
                if all((g * 4 + jj) in done for jj in range(4)):
                    group_done(g)
            interleave([tile(t) for t in range(NT)], W, on_done)
            s.barrier()
            A.reset(m0)

        def phase_B(li, mla):
            m0 = A.mark()
            NS = 3
            NPT = 4
            OBK, DBK = 6, 7
            SPAIR = [ps[:, k * 1024:(k + 1) * 1024] for k in range(NS)]
            PT = [A.bf16(1024) for _ in range(NPT)]
            PSUMS = [A.bf16(512) for _ in range(4)]
            qn_t = [A.bf16(512) for _ in range(3)]
            qp_t = [A.bf16(512) for _ in range(3)] if mla else None
            rc = A.f32(512)
            ost = [A.bf16(512) for _ in range(2)]
            Bost = [Buf("Bost0"), Buf("Bost1")]
            Brc = Buf("Brc")
            den_off = mla
            acc = A.f32(512)
            accb = A.bf16(512)
            Bacc, Baccb = Buf("Bacc"), Buf("Baccb")
            accP = A.f32(512)
            accPb = A.bf16(512)
            BaccP, BaccPb = Buf("BaccP"), Buf("BaccPb")
            Bpss = [Buf("Bpss%d" % k) for k in range(4)]
            for kc in range(8):
                for gu in range(2):
                    for fh in range(2):
                        f0 = fh * 11
                        src = w_gate_up[li, kc * 128:(kc + 1) * 128,
                                        gu * DFF + f0 * 128:gu * DFF + (f0 + 11) * 128]
                        src = src.rearrange("p (f c) -> p f c", c=128)
                        dst = wgub[li, f0:f0 + 11, :, gu, kc, :].rearrange("f p c -> p f c")
                        s.dma("pool", dst, src, w=[B_wgub[li]], key="d_wgu%d" % li)
            kt = [A.bf16(S) for _ in range(2)]
            vt = [A.bf16(S) for _ in range(2)]
            Bkv = [Buf("Bkv0"), Buf("Bkv1")]
            if mla:
                kpz = [A.bf16(S) for _ in range(2)]
                Bkp = Buf("Bkp")
                for z in range(2):
                    s.dma("sp", kpz[z], KpT, r=[B_KpT], w=[Bkp], key="d_Bkp")
                s.op("pool", lambda e: e.memset(kpz[0][64:128, :], 0.0), w=[Bkp])
                s.op("pool", lambda e: e.memset(kpz[1][0:64, :], 0.0), w=[Bkp])
            else:
                for h in range(2):
                    s.dma("sp", kt[h], KnT[h], r=[B_KnT], w=[Bkv[h]], key="d_Bkv%d" % h)
                    s.dma("sp", vt[h], Vd[h].rearrange("p t d -> p (t d)"), r=[B_Vd], w=[Bkv[h]], key="d_Bkv%d" % h)
            blocks = []
            for h in range(8):
                for qb in range(NB):
                    blocks.append(("main", h, qb))
            for h in range(4):
                for qb in range(NB):
                    blocks.append(("mem", h, qb))
            scale_main = (192.0 ** -0.5) if mla else (128.0 ** -0.5)
            scale_mem = 128.0 ** -0.5
            steps = []
            for n, (kind, h, qb) in enumerate(blocks):
                npair = (NT if kind == "main" else 2) // 2
                for i in range(npair):
                    steps.append((n, i, npair))
            G = len(steps)
            qtok, kvtok = {}, {}
            qk_tok = [None] * G
            exp_tok = [None] * G
            sum_tok = [None] * G
            pv_last_tok, norm_tok, last_qk_of_block = {}, {}, {}

            def issue_kv_load(h):
                b = h % 2
                deps = []
                if h >= 2:
                    deps.append(pv_last_tok[(h - 2) * NB + NB - 1])
                s.dma("sp", kt[b], KnT[h], r=[B_KnT], w=[], key="d_Bkv%d" % b, deps=deps)
                kvtok[h] = s.dma("sp", vt[b], Vd[h].rearrange("p t d -> p (t d)"), r=[B_Vd], w=[],
                                 key="d_Bkv%d" % b, deps=deps)

            def issue_q_load(n):
                kind, h, qb = blocks[n]
                b = n % 3
                deps = []
                if n >= 3:
                    deps.append(last_qk_of_block[n - 3])
                cols = slice(qb * 512, (qb + 1) * 512)
                if kind == "main":
                    t_ = s.dma("sp", qn_t[b], QnT[h][:, cols], r=[B_QnT], w=[], key="d_Bq%d" % b, deps=deps)
                    if mla:
                        t_ = s.dma("sp", qp_t[b], QpT[h // 2][:, cols], r=[B_QpT], w=[], key="d_Bq%d" % b, deps=deps)
                else:
                    t_ = s.dma("sp", qn_t[b], QmT[h][:, cols], r=[B_QmT], w=[], key="d_Bq%d" % b, deps=deps)
                qtok[n] = t_

            def do_qk(g):
                n, i, npair = steps[g]
                kind, h, qb = blocks[n]
                qt = qn_t[n % 3]
                deps = [qtok[n]]
                if g - NS >= 0:
                    deps.append(exp_tok[g - NS])
                tok = None
                for half in range(2):
                    kti = 2 * i + half
                    outp = SPAIR[g % NS][:, half * 512:(half + 1) * 512]
                    if kind == "main":
                        if mla:
                            kb = kt[h % 2]
                            deps.append(kvtok[h])
                        else:
                            kb = kt[h // 4]
                        lhs = kb[:, kti * 128:(kti + 1) * 128]
                    else:
                        lhs = KmT[li][:, h * 256 + kti * 128:h * 256 + (kti + 1) * 128]
                    if kind == "main" and mla:
                        s.op("pe", lambda e, outp=outp, lhs=lhs: e.matmul(outp, lhs, qt, start=True, stop=False), deps=deps)
                        lhs2 = kpz[h % 2][:, kti * 128:(kti + 1) * 128]
                        rhs2 = qp_t[n % 3]
                        tok = s.op("pe", lambda e, outp=outp, lhs2=lhs2, rhs2=rhs2: e.matmul(outp, lhs2, rhs2, start=False, stop=True),
                                   r=[Bkp])
                    else:
                        rr = [Bkv[0], Bkv[1]] if kind == "main" else [B_KmT[li]]
                        tok = s.op("pe", lambda e, outp=outp, lhs=lhs: e.matmul(outp, lhs, qt, start=True, stop=True),
                                   deps=deps, r=rr)
                qk_tok[g] = tok
                if i == npair - 1:
                    last_qk_of_block[n] = tok

            def do_exp(g):
                n, i, npair = steps[g]
                sc = scale_main if blocks[n][0] == "main" else scale_mem
                sp_ = SPAIR[g % NS]
                pt = PT[g % NPT]
                edeps = [qk_tok[g]]
                if g - NPT >= 0:
                    edeps.append(sum_tok[g - NPT])
                exp_tok[g] = s.op("act", lambda e: e.activation(out=pt, in_=sp_, func=AF.Exp, scale=sc),
                                  deps=edeps)
                pss = PSUMS[g % 4]
                sum_tok[g] = s.op("dve", lambda e: e.tensor_tensor(out=pss, in0=pt[:, 0:512], in1=pt[:, 512:1024], op=ALU.add),
                                  deps=[exp_tok[g]], w=[Bpss[g % 4]])
                if den_off:
                    if i % 2 == 0:
                        if i == 0:
                            s.op("dve", lambda e: e.tensor_copy(acc, pss), r=[Bpss[g % 4]], w=[Bacc])
                        else:
                            s.op("dve", lambda e: e.tensor_tensor(out=acc, in0=acc, in1=pss, op=ALU.add),
                                 r=[Bpss[g % 4]], w=[Bacc])
                    else:
                        if i == 1:
                            s.op("pool", lambda e: e.tensor_copy(accP, pss), r=[Bpss[g % 4]], w=[BaccP])
                        else:
                            s.op("pool", lambda e: e.tensor_tensor(out=accP, in0=accP, in1=pss, op=ALU.add),
                                 r=[Bpss[g % 4]], w=[BaccP])

            def do_pv(g):
                n, i, npair = steps[g]
                kind, h, qb = blocks[n]
                pt = PT[g % NPT]
                deps = [exp_tok[g]]
                if i == 0 and n >= 1:
                    deps.append(norm_tok[n - 1])
                for half in range(2):
                    kti = 2 * i + half
                    if kind == "main":
                        vb = vt[h % 2] if mla else vt[h // 4]
                        lhs = vb[:, kti * 128:(kti + 1) * 128]
                    else:
                        lhs = Vm[li][:, kti * 512 + h * 128:kti * 512 + (h + 1) * 128]
                    s.op("pe", lambda e, lhs=lhs, half=half: e.matmul(
                        PB[OBK], lhs, pt[:, half * 512:(half + 1) * 512],
                        start=(i == 0 and half == 0), stop=(i == npair - 1 and half == 1)), deps=deps,
                        r=([B_Vm[li]] if kind == "mem" else []))
                pss = PSUMS[g % 4]
                tok = None
                if not den_off:
                    tok = s.op("pe", lambda e: e.matmul(PB[DBK], ones, pss, start=(i == 0), stop=(i == npair - 1)),
                               r=[B_const], deps=[sum_tok[g]])
                elif i == npair - 1:
                    s.op("dve", lambda e: e.tensor_copy(accb, acc), r=[Bacc], w=[Baccb])
                    if npair > 1:
                        s.op("pool", lambda e: e.tensor_copy(accPb, accP), r=[BaccP], w=[BaccPb])
                        s.op("pe", lambda e: e.matmul(PB[DBK], ones, accb, start=True, stop=False),
                             r=[B_const, Baccb])
                        tok = s.op("pe", lambda e: e.matmul(PB[DBK], ones, accPb, start=False, stop=True),
                                   r=[B_const, BaccPb])
                    else:
                        tok = s.op("pe", lambda e: e.matmul(PB[DBK], ones, accb, start=True, stop=True),
                                   r=[B_const, Baccb])
                if i == npair - 1:
                    pv_last_tok[n] = tok
                    do_norm(n)

            def do_norm(n):
                kind, h, qb = blocks[n]
                ob = n % 2
                chunk = h if kind == "main" else 8 + h
                s.op("dve", lambda e: e.reciprocal(out=rc, in_=PB[DBK]), deps=[pv_last_tok[n]], w=[Brc])
                tok = s.op("dve", lambda e: e.tensor_tensor(out=ost[ob], in0=PB[OBK], in1=rc, op=ALU.mult),
                           r=[Brc], w=[Bost[ob]])
                norm_tok[n] = tok
                s.dma("sp", mixT[chunk][:, qb * 512:(qb + 1) * 512], ost[ob], r=[Bost[ob]], w=[B_mixT],
                      key="d_Bo%d" % ob)

            if mla:
                issue_kv_load(0)
            for n in range(min(2, len(blocks))):
                issue_q_load(n)
            kv_issued = 1
            LA = 2
            for g in range(min(LA, G)):
                do_qk(g)
            for g in range(G):
                do_exp(g)
                n, i, npair = steps[g]
                kind, h, qb = blocks[n]
                if i == 0:
                    if n + 2 < len(blocks):
                        issue_q_load(n + 2)
                    if mla and kind == "main" and qb == 1 and h + 1 < 8 and kv_issued == h + 1:
                        issue_kv_load(h + 1)
                        kv_issued += 1
                if g + LA < G:
                    do_qk(g + LA)
                do_pv(g)
            s.barrier()
            A.reset(m0)

        def phase_C1(li, xsrc, B_xsrc):
            m0 = A.mark()
            W = 3
            wo = A.bf16(12 * 1024)
            g2 = A.f32(1024)
            Bw, Bg = Buf("C1w"), Buf("C1g")
            load_w_bf16(wo, 12, 1024, w_out[li], Bw, "d_C1w")
            s.dma("sp", g2, bcast_row(norm_ffn[li], 1024), w=[Bg], key="d_C1g")
            sl = []
            for k in range(W):
                d = dict(mx=A.bf16(12 * 512), xb=A.f32(4 * 1024), hst=A.bf16(8 * 512), junk=A.bf16(1024),
                         hb=A.bf16(1024), small=A.f32(8))
                d["B"] = {n: Buf("C1%s%d" % (n, k)) for n in ("mx", "xb", "hst", "junk", "hb", "ss", "sd", "rs")}
                sl.append(d)
            OBK = (1, 2, 3, 4)
            TBK = (0, 5)

            def block(b):
                k = b % W
                d = sl[k]
                Bk = d["B"]
                cols = slice(b * 512, (b + 1) * 512)
                s.dma("sp", d["mx"].rearrange("p (c s) -> p c s", c=12), mixT[:, :, cols].rearrange("c p s -> p c s"),
                      r=[B_mixT], w=[Bk["mx"]], key="d_C1m%d" % k)
                s.dma("sp", d["xb"].rearrange("p (j n) -> p j n", j=4),
                      xsrc[b * 512:(b + 1) * 512, :].rearrange("(j p) n -> p j n", p=128),
                      r=[B_xsrc], w=[Bk["xb"]], key="d_C1x%d" % k)
                yield
                sm = d["small"]
                for j in range(4):
                    ob = (OBK[0], OBK[1]) if j % 2 == 0 else (OBK[2], OBK[3])
                    for n in range(2):
                        for c in range(12):
                            s.op("pe", lambda e, n=n, c=c, j=j, ob=ob: e.matmul(
                                PB[ob[n]], d["mx"][:, c * 512 + j * 128:c * 512 + (j + 1) * 128],
                                wo[:, c * 1024 + n * 512:c * 1024 + (n + 1) * 512],
                                start=(c == 0), stop=(c == 11)), r=[Bk["mx"], Bw], w=[B_PB[ob[n]]])
                    xj = d["xb"][:, j * 1024:(j + 1) * 1024]
                    for n in range(2):
                        s.op("dve", lambda e, n=n, ob=ob, xj=xj: e.tensor_tensor(
                            out=xj[:, n * 512:(n + 1) * 512], in0=PB[ob[n]], in1=xj[:, n * 512:(n + 1) * 512], op=ALU.add),
                            r=[B_PB[ob[n]]], w=[Bk["xb"]])
                    ss, sd, rs = sm[:, 0:1], sm[:, 1:2], sm[:, 2:3]
                    s.op("act", lambda e, xj=xj: e.activation(out=d["junk"], in_=xj, func=AF.Square, accum_out=ss),
                         r=[Bk["xb"]], w=[Bk["junk"], Bk["ss"]])
                    rstd_ops(ss, 1, 1.0 / 1024, sd, rs, Bk["ss"], Bk["sd"], Bk["rs"])
                    s.op("dve", lambda e, xj=xj: e.scalar_tensor_tensor(out=d["hb"], in0=xj, scalar=rs, in1=g2,
                                                                          op0=ALU.mult, op1=ALU.mult),
                         r=[Bk["xb"], Bk["rs"], Bg], w=[Bk["hb"]])
                    yield
                    hst3 = d["hst"].rearrange("p (c s) -> p c s", c=8)
                    transposes([d["hb"][:, c * 128:(c + 1) * 128] for c in range(8)], Bk["hb"], TBK[j % 2],
                               hst3[:, :, j * 128:(j + 1) * 128], Bk["hst"], "act")
                    yield
                s.dma("sp", x1d[b * 512:(b + 1) * 512, :].rearrange("(j p) n -> p j n", p=128),
                      d["xb"].rearrange("p (j n) -> p j n", j=4), r=[Bk["xb"]], w=[B_x1d], key="d_C1o%d" % k)
                s.dma("sp", h2T[:, :, cols].rearrange("c p s -> p c s"), d["hst"].rearrange("p (c s) -> p c s", c=8),
                      r=[Bk["hst"]], w=[B_h2T], key="d_C1p%d" % k)
                yield

            interleave([block(b) for b in range(NB)], W)
            s.barrier()
            A.reset(m0)

        def phase_C2(li, dst, B_dst):
            m0 = A.mark()
            wd = A.bf16(22 * 1024)
            Bwd = Buf("C2wd")
            load_w_bf16(wd, 22, 1024, w_down[li], Bwd, "d_C2wd")
            NR = 4
            ring = [A.bf16(2 * 8 * 128) for _ in range(NR)]
            Bring = [Buf("C2r%d" % i) for i in range(NR)]
            hT_ = [A.bf16(8 * 512) for _ in range(2)]
            xb_ = [A.f32(4 * 1024) for _ in range(2)]
            BhT = [Buf("C2h0"), Buf("C2h1")]
            Bxb = [Buf("C2x0"), Buf("C2x1")]
            actT = A.bf16(22 * 512)
            BactT = Buf("C2act")
            sg = [A.f32(512) for _ in range(2)]
            Bsg = [Buf("C2sg0"), Buf("C2sg1")]
            GB = ((0, 1), (2, 3))
            YB = (4, 5, 6, 7)
            chunks = [(b, f) for b in range(NB) for f in range(22)]

            def load_chunk(ci):
                b, f = chunks[ci]
                r_ = ci % NR
                s.dma("sp", ring[r_], wgub[li, f].rearrange("p g k c -> p (g k c)"), r=[B_wgub[li]], w=[Bring[r_]],
                      key="d_C2r%d" % r_)

            def load_block(b):
                k = b % 2
                cols = slice(b * 512, (b + 1) * 512)
                s.dma("sp", hT_[k].rearrange("p (c s) -> p c s", c=8), h2T[:, :, cols].rearrange("c p s -> p c s"),
                      r=[B_h2T], w=[BhT[k]], key="d_C2h%d" % k)
                s.dma("sp", xb_[k].rearrange("p (j n) -> p j n", j=4),
                      x1d[b * 512:(b + 1) * 512, :].rearrange("(j p) n -> p j n", p=128),
                      r=[B_x1d], w=[Bxb[k]], key="d_C2x%d" % k)

            load_block(0)
            for ci in range(min(NR - 1, len(chunks))):
                load_chunk(ci)
            yi = 0
            for b in range(NB):
                k = b % 2
                if b + 1 < NB:
                    load_block(b + 1)
                for f in range(22):
                    ci = b * 22 + f
                    if ci + NR - 1 < len(chunks):
                        load_chunk(ci + NR - 1)
                    rg = ring[ci % NR]
                    gb = GB[f % 2]
                    for gu in range(2):
                        for kc in range(8):
                            s.op("pe", lambda e, gu=gu, kc=kc, rg=rg, gb=gb, k=k: e.matmul(
                                PB[gb[gu]], rg[:, gu * 1024 + kc * 128:gu * 1024 + (kc + 1) * 128],
                                hT_[k][:, kc * 512:(kc + 1) * 512], start=(kc == 0), stop=(kc == 7)),
                                r=[Bring[ci % NR], BhT[k]], w=[B_PB[gb[gu]]])
                    sgt = sg[f % 2]
                    s.op("act", lambda e, gb=gb, sgt=sgt: e.activation(out=sgt, in_=PB[gb[0]], func=AF.Silu),
                         r=[B_PB[gb[0]]], w=[Bsg[f % 2]])
                    s.op("dve", lambda e, gb=gb, sgt=sgt, f=f: e.tensor_tensor(
                        out=actT[:, f * 512:(f + 1) * 512], in0=PB[gb[1]], in1=sgt, op=ALU.mult),
                        r=[B_PB[gb[1]], Bsg[f % 2]], w=[BactT])
                for j in range(4):
                    xj = xb_[k][:, j * 1024:(j + 1) * 1024]
                    for n in range(2):
                        yb = YB[yi % 4]
                        yi += 1
                        for f in range(22):
                            s.op("pe", lambda e, f=f, j=j, n=n, yb=yb: e.matmul(
                                PB[yb], actT[:, f * 512 + j * 128:f * 512 + (j + 1) * 128],
                                wd[:, f * 1024 + n * 512:f * 1024 + (n + 1) * 512],
                                start=(f == 0), stop=(f == 21)), r=[BactT, Bwd], w=[B_PB[yb]])
                        s.op("dve", lambda e, n=n, yb=yb, xj=xj: e.tensor_tensor(
                            out=xj[:, n * 512:(n + 1) * 512], in0=PB[yb], in1=xj[:, n * 512:(n + 1) * 512], op=ALU.add),
                            r=[B_PB[yb]], w=[Bxb[k]])
                s.dma("pool", dst[b * 512:(b + 1) * 512, :].rearrange("(j p) n -> p j n", p=128),
                      xb_[k].rearrange("p (j n) -> p j n", j=4), r=[Bxb[k]], w=[B_dst], key="d_C2o%d" % k)
            s.barrier()
            A.reset(m0)

        B_x = Buf("x", True)
        B_out = Buf("out", True)
        plist = [lambda: phase_M(), lambda: phase_A_mla(x, B_x), lambda: phase_B(0, True),
                 lambda: phase_C1(0, x, B_x), lambda: phase_C2(0, x2d, B_x2d),
                 lambda: phase_A_gqa(x2d, B_x2d), lambda: phase_B(1, False),
                 lambda: phase_C1(1, x2d, B_x2d), lambda: phase_C2(1, out, B_out)]
        for pi, pf in enumerate(plist):
            if pi < stop_after:
                pf()
        s.emit_all()
    return nc


def rope_tables(S, dim):
    rows = S // 64
    row = np.repeat(np.arange(rows, dtype=np.float64), 64)
    col = np.tile(np.arange(64, dtype=np.float64), rows)
    axis_dim = dim // 2
    inv = 10000.0 ** (-np.arange(0, axis_dim, 2, dtype=np.float64) / axis_dim)
    inv = inv.astype(np.float32).astype(np.float64)
    ar = (row[:, None].astype(np.float32) * inv.astype(np.float32)[None, :]).astype(np.float64)
    ac = (col[:, None].astype(np.float32) * inv.astype(np.float32)[None, :]).astype(np.float64)
    cr, sr, cc, sc = np.cos(ar), np.sin(ar), np.cos(ac), np.sin(ac)
    cos = np.concatenate([cr, cr, cc, cc], axis=1).astype(np.float32)
    sin = np.concatenate([-sr, sr, -sc, sc], axis=1).astype(np.float32)
    return np.ascontiguousarray(cos), np.ascontiguousarray(sin)


_CACHE = {}


def run(inputs, S, n_cores):
    if S not in _CACHE:
        _CACHE[S] = build(S)
    nc = _CACHE[S]
    cosg, sing = rope_tables(S, 128)
    cosm, sinm = rope_tables(S, 64)
    shared = {k: np.ascontiguousarray(np.asarray(v, dtype=np.float32)) for k, v in inputs.items()
              if k not in ("x", "mem")}
    shared.update(cosg=cosg, sing=sing, cosm=cosm, sinm=sinm)
    in_maps = []
    for c in range(n_cores):
        m = dict(shared)
        m["x"] = np.ascontiguousarray(np.asarray(inputs["x"][c], dtype=np.float32))
        m["mem"] = np.ascontiguousarray(np.asarray(inputs["mem"][c], dtype=np.float32))
        in_maps.append(m)
    res = run_bass_kernel_spmd(nc, in_maps, core_ids=list(range(n_cores)))
    return np.stack([np.asarray(r["out"]) for r in res.results], axis=0).astype(np.float32)


def kernel(**inputs):
    S = inputs["x"].shape[1]
    return run(inputs, S, inputs["x"].shape[0])
```
